# Optimizing a Trainium2 kernel written in Bass

```python
import math
import jax, jax.numpy as jnp
from jax import lax
import numpy as np

D_MODEL = 2048
BATCH = 2
SEQ = 16384
DEPTH = 2

HEAD_DIM = 128
DIL_GROUPS = ((128, 1), (512, 4), (2048, 16))
HEADS_PER_DIL_GROUP = 4
N_HEADS_A = HEADS_PER_DIL_GROUP * len(DIL_GROUPS)
N_HEADS_B = 4
WIDTH_A = N_HEADS_A * HEAD_DIM
WIDTH_B = N_HEADS_B * HEAD_DIM
A_OUT = HEADS_PER_DIL_GROUP * HEAD_DIM
D_IN = 3 * WIDTH_A + 3 * WIDTH_B + 2 * D_MODEL
D_FF = 5632
CONV_WIDTH = 3
PLE_DIM = 256
BLOCK = 128
RMS_EPS = 1e-6
ALIBI_MAX = 8.0

kernel_name = 'hybrid_dilated_stickbreak_convffn'


def rmsnorm(x, g):
    xf = x.astype(jnp.float32)
    y = xf * lax.rsqrt(jnp.mean(xf * xf, axis=-1, keepdims=True) + RMS_EPS)
    return (y * g.astype(jnp.float32)).astype(x.dtype)


def alibi_slopes(n):
    return jnp.exp2(-ALIBI_MAX * jnp.arange(1, n + 1, dtype=jnp.float32) / n)


def dilated_window_group(q, k, v, window, dilation, slopes):
    b, s, hg, dh = q.shape
    n_back = window // dilation
    span = dilation * BLOCK
    s_pad = -(-s // span) * span
    L = s_pad // dilation
    nb = L // BLOCK

    def to_blocks(t):
        t = jnp.pad(t, ((0, 0), (0, s_pad - s), (0, 0), (0, 0)))
        t = t.reshape(b, L, dilation, hg, dh).transpose(0, 2, 3, 1, 4)
        return t.reshape(b, dilation, hg, nb, BLOCK, dh)

    def with_prev(t):
        prev = jnp.pad(t, ((0, 0), (0, 0), (0, 0), (1, 0), (0, 0), (0, 0)))[:, :, :, :-1]
        return jnp.concatenate([prev, t], axis=4)

    qb = to_blocks(q)
    kk = with_prev(to_blocks(k))
    vv = with_prev(to_blocks(v))
    scores = jnp.einsum('brhnqe,brhnke->brhnqk', qb, kk).astype(jnp.float32) * (1.0 / math.sqrt(dh))
    qi = jnp.arange(BLOCK)[:, None]
    ki = jnp.arange(2 * BLOCK)[None, :]
    steps = BLOCK + qi - ki
    valid = (steps >= 0) & (steps <= n_back)
    first = (jnp.arange(nb) == 0)[:, None, None]
    valid = valid[None] & ~(first & (ki < BLOCK)[None])
    bias = -slopes[:, None, None] * (steps * dilation).astype(jnp.float32)[None]
    scores = scores + bias[None, None, :, None]
    scores = jnp.where(valid[None, None, None], scores, -jnp.inf)
    lse = jax.nn.logsumexp(scores, axis=-1)
    probs = jnp.exp(scores - lse[..., None])
    out = jnp.einsum('brhnqk,brhnke->brhnqe', probs.astype(v.dtype), vv)

    def from_blocks(t):
        t = t.reshape(b, dilation, hg, L, *t.shape[5:])
        t = jnp.moveaxis(t, 3, 1)
        return t.reshape(b, s_pad, hg, *t.shape[4:])[:, :s]

    return from_blocks(out), from_blocks(lse)


def dilated_mixture_attention(q, k, v):
    b, s = q.shape[:2]
    slopes = alibi_slopes(N_HEADS_A)
    outs, lses = [], []
    for g, (window, dilation) in enumerate(DIL_GROUPS):
        sl = slice(g * HEADS_PER_DIL_GROUP, (g + 1) * HEADS_PER_DIL_GROUP)
        o, l = dilated_window_group(q[:, :, sl], k[:, :, sl], v[:, :, sl], window, dilation, slopes[sl])
        outs.append(o)
        lses.append(l)
    alpha = jax.nn.softmax(jnp.stack(lses), axis=0)
    y = jnp.sum(alpha[..., None] * jnp.stack(outs).astype(jnp.float32), axis=0)
    return y.reshape(b, s, A_OUT).astype(q.dtype)


def stick_breaking_attention(q, k, v):
    b, s, h, dh = q.shape
    nb = s // BLOCK
    qt = q.transpose(0, 2, 1, 3)
    kt = k.transpose(0, 2, 1, 3)
    vt = v.transpose(0, 2, 1, 3)
    scale = 1.0 / math.sqrt(dh)
    idx = jnp.arange(BLOCK)
    tri_within = (idx[:, None] > idx[None, :]).astype(jnp.float32)
    diag_before = idx[None, :] < idx[:, None]
    outs = []
    for n in range(nb):
        c = n + 1
        kl = c * BLOCK
        qblk = qt[:, :, n * BLOCK:(n + 1) * BLOCK]
        z = jnp.einsum('bhqe,bhke->bhqk', qblk, kt[:, :, :kl]).astype(jnp.float32) * scale
        before = jnp.concatenate([jnp.ones((BLOCK, n * BLOCK), dtype=bool), diag_before], axis=1)
        ln = jnp.where(before, jax.nn.log_sigmoid(-z), 0.0)
        lnc = ln.reshape(b, h, BLOCK, c, BLOCK)
        within = jnp.einsum('bhqcj,ji->bhqci', lnc, tri_within)
        tot = jnp.sum(lnc, axis=-1)
        cidx = jnp.arange(c)
        after = jnp.einsum('bhqc,cd->bhqd', tot, (cidx[:, None] > cidx[None, :]).astype(jnp.float32))
        logw = z.reshape(b, h, BLOCK, c, BLOCK) + lnc + within + after[..., None]
        w = jnp.where(before, jnp.exp(logw.reshape(b, h, BLOCK, kl)), 0.0)
        outs.append(jnp.einsum('bhqk,bhke->bhqe', w.astype(v.dtype), vt[:, :, :kl]))
    out = jnp.concatenate(outs, axis=2)
    return out.transpose(0, 2, 1, 3).reshape(b, s, h * dh)


def causal_depthwise_conv(x, w, bias):
    s = x.shape[1]
    xp = jnp.pad(x, ((0, 0), (CONV_WIDTH - 1, 0), (0, 0)))
    y = bias + w[0] * xp[:, 0:s]
    for j in range(1, CONV_WIDTH):
        y = y + w[j] * xp[:, j:j + s]
    return y


def setup_inputs(seed: int = 0) -> dict:
    key = jax.random.key(seed)
    ks = jax.random.split(key, 20)
    f32 = jnp.float32

    def nrm(k, shape, scale):
        return jax.random.normal(k, shape, f32) * scale

    def gain(k):
        return 1.0 + 0.05 * jax.random.normal(k, (DEPTH, D_MODEL), f32)

    return {
        'x': nrm(ks[0], (BATCH, SEQ, D_MODEL), 1.0),
        'p': nrm(ks[1], (DEPTH, BATCH, SEQ, PLE_DIM), 1.0),
        'g_mix_pre': gain(ks[2]),
        'w_in': nrm(ks[3], (DEPTH, D_MODEL, D_IN), D_MODEL ** -0.5),
        'w_branch_a': nrm(ks[4], (DEPTH, A_OUT, D_MODEL), A_OUT ** -0.5),
        'w_branch_b': nrm(ks[5], (DEPTH, WIDTH_B, D_MODEL), WIDTH_B ** -0.5),
        'w_out': nrm(ks[6], (DEPTH, D_MODEL, D_MODEL), D_MODEL ** -0.5),
        'g_mix_post': gain(ks[7]),
        'g_ffn_pre': gain(ks[8]),
        'w_up': nrm(ks[9], (DEPTH, D_MODEL, 2 * D_FF), D_MODEL ** -0.5),
        'conv_w': nrm(ks[10], (DEPTH, CONV_WIDTH, 2 * D_FF), CONV_WIDTH ** -0.5),
        'conv_b': nrm(ks[11], (DEPTH, 2 * D_FF), 0.01),
        'w_down': nrm(ks[12], (DEPTH, D_FF, D_MODEL), D_FF ** -0.5),
        'g_ffn_post': gain(ks[13]),
        'w_ple_in': nrm(ks[14], (DEPTH, PLE_DIM, D_MODEL), PLE_DIM ** -0.5),
        'w_ple_gate': nrm(ks[15], (DEPTH, D_MODEL, D_MODEL), D_MODEL ** -0.5),
    }


def reference(x, p, g_mix_pre, w_in, w_branch_a, w_branch_b, w_out, g_mix_post, g_ffn_pre, w_up, conv_w, conv_b, w_down, g_ffn_post, w_ple_in, w_ple_gate):
    b, s, _ = x.shape
    splits = [WIDTH_A, 2 * WIDTH_A, 3 * WIDTH_A,
              3 * WIDTH_A + WIDTH_B, 3 * WIDTH_A + 2 * WIDTH_B, 3 * WIDTH_A + 3 * WIDTH_B,
              3 * WIDTH_A + 3 * WIDTH_B + D_MODEL]
    h = x
    for i in range(DEPTH):
        u = rmsnorm(h, g_mix_pre[i])
        proj = u @ w_in[i]
        qa, ka, va, qb, kb, vb, gla, glb = jnp.split(proj, splits, axis=-1)
        heads_a = lambda t: t.reshape(b, s, N_HEADS_A, HEAD_DIM)
        heads_b = lambda t: t.reshape(b, s, N_HEADS_B, HEAD_DIM)
        ya = dilated_mixture_attention(heads_a(qa), heads_a(ka), heads_a(va)) @ w_branch_a[i]
        yb = stick_breaking_attention(heads_b(qb), heads_b(kb), heads_b(vb)) @ w_branch_b[i]
        gate_a = jax.nn.sigmoid(gla.astype(jnp.float32)).astype(h.dtype)
        gate_b = jax.nn.sigmoid(glb.astype(jnp.float32)).astype(h.dtype)
        mix = (gate_a * ya + gate_b * yb) @ w_out[i]
        h = h + rmsnorm(mix, g_mix_post[i])
        u = rmsnorm(h, g_ffn_pre[i])
        up = causal_depthwise_conv(u @ w_up[i], conv_w[i], conv_b[i])
        gate_ff, value = jnp.split(up, [D_FF], axis=-1)
        f = (jax.nn.gelu(gate_ff, approximate=True) * value) @ w_down[i]
        h = h + rmsnorm(f, g_ffn_post[i])
        e = p[i] @ w_ple_in[i]
        h = h + jax.nn.sigmoid((h @ w_ple_gate[i]).astype(jnp.float32)).astype(h.dtype) * e
    return h
```

```python
from contextlib import ExitStack
import math
import numpy as np
import ml_dtypes
import concourse.bass as bass
import concourse.mybir as mybir
from concourse.bass_utils import run_bass_kernel_spmd

F32 = mybir.dt.float32
BF16 = mybir.dt.bfloat16
AF = mybir.ActivationFunctionType
ALU = mybir.AluOpType

PE, ACT, DVE, POOL, SP = "tensor", "scalar", "vector", "gpsimd", "sync"
ENGS = (PE, ACT, DVE, POOL, SP)

D = 2048
KD = 16
DFF = 5632
NFF = 44
PLE = 256
L_ = 2
W = 512
GROUPS = [[0, 1, 2, 3], [4, 5, 6, 7]]
RMS_EPS = 1e-6
QSCALE = 1.0 / math.sqrt(128.0)
NEGC = 11.3125
QSCALE2 = 1.0 / NEGC
MASKV = -30000.0


class Buf:
    __slots__ = ("name", "w", "r", "rd")

    def __init__(self, name=""):
        self.name = name
        self.w = None
        self.r = {}
        self.rd = []


class DSem:
    __slots__ = ("name", "cnt", "h", "step", "async_")

    def __init__(self, name, step=16, async_=False):
        self.name = name
        self.cnt = 0
        self.h = None
        self.step = step
        self.async_ = async_


class Op:
    __slots__ = ("eng", "fn", "deps", "sig", "sidx", "dsem", "dval", "gid")

    def __init__(self, eng, fn, gid):
        self.eng = eng
        self.fn = fn
        self.deps = []
        self.sig = False
        self.sidx = 0
        self.dsem = None
        self.dval = 0
        self.gid = gid


PID = {}


class Sched:
    def __init__(self):
        self.ops = {e: [] for e in ENGS}
        self.gid = 0
        self.dsems = []
        self.bar = []
        self.last = {}
        self.lastdma = {}
        self.dsem_by_name = {}

    def dsem(self, name, step=16, async_=False):
        if name in self.dsem_by_name:
            return self.dsem_by_name[name]
        d = DSem(name, step, async_)
        self.dsems.append(d)
        self.dsem_by_name[name] = d
        return d

    def add(self, eng, fn, reads=(), writes=(), dsem=None, nobar=False):
        self.gid += 1
        op = Op(eng, fn, self.gid)
        is_dma = dsem is not None
        deps = {}

        def adddep(d):
            if d is None:
                return
            if d.dsem is not None:
                deps[("d", d.gid)] = d
            else:
                if d.eng == PE and eng == PE and not is_dma:
                    return
                k = d.eng
                if k not in deps or deps[k].gid < d.gid:
                    deps[k] = d

        for b in reads:
            adddep(b.w)
        for b in writes:
            if not (is_dma and b.w is not None and b.w.dsem is dsem):
                adddep(b.w)
            for d in b.r.values():
                adddep(d)
            for d in b.rd:
                adddep(d)
        if not nobar:
            for d in self.bar:
                adddep(d)
        op.deps = list(deps.values())
        for d in op.deps:
            if d.dsem is None:
                d.sig = True
        for b in reads:
            if is_dma:
                b.rd.append(op)
            else:
                b.r[eng] = op
        for b in writes:
            b.w = op
            b.r = {}
            b.rd = []
        if is_dma:
            dsem.cnt += dsem.step
            op.dsem = dsem
            op.dval = dsem.cnt
            self.lastdma[id(dsem)] = op
        else:
            self.last[eng] = op
        self.ops[eng].append(op)
        return op

    def barrier(self, full=False):
        deps = list(self.last.values()) + [o for o in self.lastdma.values() if full or not o.dsem.async_]
        for d in deps:
            if d.dsem is None:
                d.sig = True
        self.bar = deps

    def final_wait(self, eng=SP):
        self.barrier(full=True)
        self.add(eng, None)

    def emit(self, nc):
        for e in ENGS:
            c = 0
            for op in self.ops[e]:
                if op.dsem is None and op.sig:
                    c += 1
                    op.sidx = c
        stats = {}
        with ExitStack() as es:
            esem = {e: es.enter_context(nc.semaphore("s_" + e)) for e in ENGS}
            for d in self.dsems:
                d.h = es.enter_context(nc.semaphore("d_" + d.name))
            block = es.enter_context(nc.Block())

            def run(e_name):
                def body(e):
                    waited = {}
                    nw = 0
                    if e_name == SP:
                        PID["sp"] = e.partition_id()
                        PID["i"] = PID["sp"] % 4
                        PID["im1"] = (PID["sp"] + 3) % 4
                    for op in self.ops[e_name]:
                        for d in op.deps:
                            if d.dsem is not None:
                                key, h, val = id(d.dsem), d.dsem.h, d.dval
                            else:
                                key, h, val = d.eng, esem[d.eng], d.sidx
                            if waited.get(key, 0) < val:
                                e.wait_ge(h, val)
                                waited[key] = val
                                nw += 1
                        if op.fn is None:
                            continue
                        ins = op.fn(e)
                        if op.dsem is not None:
                            ins.then_inc(op.dsem.h, op.dsem.step)
                        elif op.sig:
                            ins.then_inc(esem[e_name], 1)
                    stats[e_name] = (len(self.ops[e_name]), nw)
                return body

            block.tensor(run(PE))
            block.scalar(run(ACT))
            block.vector(run(DVE))
            block.gpsimd(run(POOL))
            block.sync(run(SP))
        return stats


WSPECS = [
    ("win", D, 10240), ("wa", 512, D), ("wb", 512, D), ("wo", D, D),
    ("wup", D, 2 * DFF), ("wd", DFF, D), ("wpi", PLE, D), ("wpg", D, D),
]
CE = 2 * 1024 * 1024


def wchunks(n_total):
    out = []
    st = 0
    while st < n_total:
        sz = min(CE, n_total - st)
        assert sz % 8192 == 0
        out.append((st, sz))
        st += sz
    return out


def build(S, nlayers=L_, stop_after=None, dbg=()):
    TC = S // 4
    NT = TC // W
    NSB = S // 2048
    NCH = S // 128
    nc = bass.Bass("TRN2", target_bir_lowering=False)
    sch = Sched()
    ein = lambda n, sh, dt=F32: nc.dram_tensor(n, list(sh), dt, kind="ExternalInput").ap()

    xT = ein("xT", [128, KD, TC])
    xh = ein("xh", [128, KD, 2])
    pT = ein("pT", [nlayers, 128, 2, TC])
    wsh = {n: ein(n + "_sh", [nlayers * r * c // 4 // 2048, 2048]) for n, r, c in WSPECS}
    gains = ein("gains", [128, nlayers, 4, KD])
    convw = ein("convw", [128, nlayers, 4, 2 * NFF])
    hflag = ein("hflag", [128, 1])
    cbf = ein("cbf", [128, 3, 128])
    abias = ein("abias", [128, 3, 256])
    bmask = ein("bmask", [128, 4, 512])
    outT = nc.dram_tensor("outT", [128, KD, TC], F32, kind="ExternalOutput").ap()
    dbg_out = {}
    dt_ = lambda n, sh, dt: nc.dram_tensor(n, list(sh), dt).ap()
    wshbf = {n: dt_(n + "_shbf", [nlayers * r * c // 4 // 2048, 2048], BF16) for n, r, c in WSPECS}
    wflat = {n: dt_(n + "_bf", [nlayers * r * c // 2048, 2048], BF16) for n, r, c in WSPECS}
    wbf = {n: wflat[n].rearrange("a b -> (a b)").rearrange("(l r c) -> l r c", l=nlayers, r=r, c=c)
           for n, r, c in WSPECS}
    NQ = TC // 256
    ubuf_own = dt_("ubuf_own", [NQ, D, 256], BF16)
    uall = dt_("uall", [NQ, 4 * D, 256], BF16)
    qk = dt_("qk", [8, 128, S], BF16)
    vbuf = dt_("vbuf", [S, 512], BF16)
    ACW = min(2048, TC)
    NAC = S // ACW
    att_own = dt_("att_own", [NAC, 256, ACW], BF16)
    att_all = dt_("att_all", [NAC, 4 * 256, ACW], BF16)
    wq_mine = dt_("wq_mine", [D, 12, 128], BF16)
    att_mine = dt_("att_mine", [TC // ACW, 1024, ACW], BF16)
    att_halo = dt_("att_halo", [1024, 2], BF16)
    hbuf = dt_("hbuf", [128, KD, TC], F32)
    hh_own = dt_("hh_own", [128, KD * 2], F32)
    hh_all = dt_("hh_all", [4 * 128, KD * 2], F32)

    wshB = {n: [Buf(), Buf(), Buf()] for n, _, _ in WSPECS}
    wflB = {n: Buf() for n, _, _ in WSPECS}
    ubufB = [Buf() for _ in range(NQ)]
    uallB = [Buf() for _ in range(NQ)]
    attoB = [Buf() for _ in range(NAC)]
    attaB = [Buf() for _ in range(NAC)]

    def ag_u(q):
        sch.add(POOL, (lambda e, q=q: e.collective_compute(
            "AllGather", ALU.bypass, replica_groups=GROUPS, ins=[ubuf_own[q]], outs=[uall[q]])),
            reads=[ubufB[q]], writes=[uallB[q]], dsem=sch.dsem("cc_u%d" % (q % 8), step=1, async_=True), nobar=True)

    def ag_att(q):
        sch.add(POOL, (lambda e, q=q: e.collective_compute(
            "AllGather", ALU.bypass, replica_groups=GROUPS, ins=[att_own[q]], outs=[att_all[q]])),
            reads=[attoB[q]], writes=[attaB[q]], dsem=sch.dsem("cc_a%d" % (q % 8), step=1, async_=True), nobar=True)

    def dbg_tensor(name, shape, dt=F32):
        t = nc.dram_tensor("dbg_" + name, list(shape), dt, kind="ExternalOutput").ap()
        dbg_out[name] = t
        return t

    with ExitStack() as top:
        _uid = [0]

        def sbt(es, n, sh, dt):
            _uid[0] += 1
            return es.enter_context(nc.sbuf_tensor("%s_%d" % (n, _uid[0]), list(sh), dt))
        gains_sb = sbt(top, "gains_sb", [128, nlayers, 4, KD], F32)
        convw_sb = sbt(top, "convw_sb", [128, nlayers, 4, 2 * NFF], F32)
        hflag_sb = sbt(top, "hflag_sb", [128, 1], F32)
        cbf_sb = sbt(top, "cbf_sb", [128, 3, 128], BF16)
        psum = [top.enter_context(nc.psum_tensor("ps%d" % i, [128, 512], F32)) for i in range(8)]
        PB = [Buf("ps%d" % i) for i in range(8)]
        Bconst = Buf("const")
        dconst = [sch.dsem("const%d" % i) for i in range(4)]
        sch.add(SP, lambda e: e.dma_start(out=gains_sb[:], in_=gains), writes=[Bconst], dsem=dconst[0])
        sch.add(SP, lambda e: e.dma_start(out=convw_sb[:], in_=convw), writes=[Bconst], dsem=dconst[1])
        sch.add(SP, lambda e: e.dma_start(out=hflag_sb[:], in_=hflag), writes=[Bconst], dsem=dconst[2])
        sch.add(POOL, lambda e: e.dma_start(out=cbf_sb[:], in_=cbf), writes=[Bconst], dsem=dconst[3])
        NTm = cbf_sb[:, 0, :]
        NONESm = cbf_sb[:, 1, :]
        ONESm = cbf_sb[:, 2, :]

        import os as _os
        _skip_cast = _os.environ.get("K_SKIP_CAST") == "1"
        with ExitStack() as es:
            NCB = 3
            cst = [sbt(es, "cast%d" % i, [128, 4, 2048], BF16) for i in range(NCB)]
            cstB = [Buf() for _ in range(NCB)]
            dci = [sch.dsem("casti%d" % i) for i in range(NCB)]
            dco = [sch.dsem("casto%d" % i) for i in range(NCB)]
            ci = 0
            for n, r, c in ([] if _skip_cast else WSPECS):
                rows = nlayers * r * c // 4 // 2048
                src = wsh[n]
                dst = wshbf[n]
                for r0 in range(0, rows, 512):
                    nr = min(512, rows - r0)
                    pc = 128 if nr % 128 == 0 else nr
                    assert pc <= 128
                    q = nr // pc
                    s_ = ci % NCB
                    ci += 1
                    sap = src[r0:r0 + nr, :].rearrange("(q p) b -> p q b", p=pc)
                    dap = dst[r0:r0 + nr, :].rearrange("(q p) b -> p q b", p=pc)
                    sch.add(POOL, (lambda e, s_=s_, q=q, sap=sap, pc=pc: e.dma_start(out=cst[s_][0:pc, 0:q, :], in_=sap)),
                            writes=[cstB[s_]], dsem=dci[s_])
                    sch.add(SP, (lambda e, s_=s_, q=q, dap=dap, pc=pc: e.dma_start(out=dap, in_=cst[s_][0:pc, 0:q, :])),
                            reads=[cstB[s_]], writes=[wshB[n][s_]], dsem=dco[s_])
        sch.barrier()

        def gather_weight(n, r, c):
            dccw = sch.dsem("cc_w_" + n, step=1, async_=True)
            for (st, sz) in wchunks(nlayers * r * c):
                a0, na = st // 4 // 2048, sz // 4 // 2048
                b0, nb_ = st // 2048, sz // 2048
                sch.add(POOL, (lambda e, n=n, a0=a0, na=na, b0=b0, nb_=nb_: e.collective_compute(
                    "AllGather", ALU.bypass, replica_groups=GROUPS, ins=[wshbf[n][a0:a0 + na, :]],
                    outs=[wflat[n][b0:b0 + nb_, :]])), reads=wshB[n], writes=[wflB[n]], dsem=dccw, nobar=True)

        if not _skip_cast:
            gather_weight(*WSPECS[0])

        ring = {"ps": 0}

        def next_ps(n=7):
            i = ring["ps"] % n
            ring["ps"] += 1
            return i

        def mm(out, lhsT, rhs, start, stop, reads, pb):
            sch.add(PE, (lambda e: e.matmul(out, lhsT, rhs, start=start, stop=stop)), reads=reads, writes=[pb])

        def act(out, in_, func, reads, writes, **kw):
            sch.add(ACT, (lambda e: e.activation(out=out, in_=in_, func=func, **kw)), reads=reads, writes=writes)

        def norm_rstd(es_bufs, src_aps, src_bufs, Wt):
            sq, sqB, rstd, rstdB = es_bufs
            n = len(src_aps)
            for k in range(n):
                s = k % len(sq)
                act(sq[s][:, 0:Wt], src_aps[k], AF.Square, reads=[src_bufs[k]], writes=[sqB[s]])
                mm(psum[7][:, 0:Wt], ONESm, sq[s][:, 0:Wt], k == 0, k == n - 1, [sqB[s], Bconst], PB[7])
            act(rstd[:, 0:Wt], psum[7][:, 0:Wt], AF.Sqrt, reads=[PB[7]], writes=[rstdB], scale=1.0 / D, bias=RMS_EPS)
            sch.add(DVE, (lambda e: e.reciprocal(out=rstd[:, 0:Wt], in_=rstd[:, 0:Wt])), reads=[rstdB], writes=[rstdB])
            return rstd[:, 0:Wt], rstdB

        with ExitStack() as es:
            hT = sbt(es, "p0_h", [128, KD, W], F32)
            uT = sbt(es, "p0_u", [128, KD, W], BF16)
            sq = [sbt(es, "p0_sq%d" % i, [128, W], BF16) for i in range(2)]
            rstd = sbt(es, "p0_rstd", [128, W], F32)
            hB = [Buf() for _ in range(KD)]
            uB = [Buf() for _ in range(KD)]
            sqB = [Buf(), Buf()]
            rstdB = Buf()
            dh, du = sch.dsem("p0h"), sch.dsem("p0u")
            for tt in range(0 if stop_after == "cast" else NT):
                t0 = tt * W
                sch.add(SP, (lambda e, t0=t0: e.dma_start(out=hT[:], in_=xT[:, :, t0:t0 + W])), writes=hB, dsem=dh)
                r_ap, rB = norm_rstd((sq, sqB, rstd, rstdB), [hT[:, k, :] for k in range(KD)], hB, W)
                for k in range(KD):
                    sch.add(DVE, (lambda e, k=k: e.scalar_tensor_tensor(
                        out=uT[:, k, :], in0=hT[:, k, :], scalar=gains_sb[:, 0, 0, k:k + 1], in1=r_ap,
                        op0=ALU.mult, op1=ALU.mult)), reads=[hB[k], rB, Bconst], writes=[uB[k]])
                for q in range(2):
                    sch.add(SP, (lambda e, tt=tt, q=q: e.dma_start(
                        out=ubuf_own[2 * tt + q].rearrange("(k p) t -> p k t", p=128),
                        in_=uT[:, :, q * 256:(q + 1) * 256])),
                        reads=uB, writes=[ubufB[2 * tt + q]], dsem=du)
                for q in range(2):
                    ag_u(2 * tt + q)
            sch.barrier()
        for spec in WSPECS[1:]:
            if not _skip_cast:
                gather_weight(*spec)

        for l in range(nlayers):
            if stop_after in ("cast", "p0"):
                break
            if stop_after == "ag_u":
                break

            with ExitStack() as es:
                wq = sbt(es, "p2a_w", [128, KD, 1536], BF16)
                uT2 = [sbt(es, "p2a_u%d" % i, [128, KD, W], BF16) for i in range(2)]
                stq = [sbt(es, "p2a_sq%d" % i, [128, W], BF16) for i in range(4)]
                stv = [sbt(es, "p2a_sv%d" % i, [128, 512], BF16) for i in range(4)]
                wqB = Buf()
                uB2 = [Buf(), Buf()]
                stqB = [Buf() for _ in range(4)]
                stvB = [Buf() for _ in range(4)]
                dw = sch.dsem("p2aw")
                du2 = [sch.dsem("p2au%d" % i) for i in range(2)]
                dsq = [sch.dsem("p2asq%d" % i) for i in range(4)]
                dsv = [sch.dsem("p2asv%d" % i) for i in range(4)]
                wqmB = Buf()
                dwm = sch.dsem("p2awm")
                sch.add(SP, (lambda e, l=l: e.dma_start(
                    out=wq_mine, in_=wbf["win"][l].rearrange("r (G j c) -> r G j c", j=4, c=128)[
                        :, 0:12, bass.ds(PID["i"], 1), :].rearrange("r G o c -> r G (o c)"))),
                    reads=[wflB["win"]], writes=[wqmB], dsem=dwm)
                for (ga, gb_, c0_) in ((0, 6, 0), (9, 11, 768), (6, 9, 1024), (11, 12, 1408)):
                    sch.add(SP, (lambda e, ga=ga, gb_=gb_, c0_=c0_: e.dma_start(
                        out=wq[:, :, c0_:c0_ + (gb_ - ga) * 128],
                        in_=wq_mine[:, ga:gb_, :].rearrange("(k p) G c -> p k (G c)", p=128))),
                        reads=[wqmB], writes=[wqB], dsem=dw)
                it = 0
                ev = 0
                for r in range(4):
                    for tt in range(NT):
                        s = it % 2
                        it += 1
                        tok0 = r * TC + tt * W
                        for q in range(2):
                            sch.add(SP, (lambda e, r=r, tt=tt, s=s, q=q: e.dma_start(
                                out=uT2[s][:, :, q * 256:(q + 1) * 256],
                                in_=uall[2 * tt + q, r * D:(r + 1) * D, :].rearrange(
                                    "(k p) t -> p k t", p=128))), reads=[uallB[2 * tt + q]], writes=[uB2[s]], dsem=du2[s])
                        for hd in range(8):
                            pi = next_ps()
                            for k in range(KD):
                                mm(psum[pi][:, :], wq[:, k, hd * 128:(hd + 1) * 128], uT2[s][:, k, :],
                                   k == 0, k == KD - 1, [wqB, uB2[s]], PB[pi])
                            q_ = ev % 4
                            ev += 1
                            if ev % 2 == 0:
                                act(stq[q_][:], psum[pi][:, :], AF.Copy, reads=[PB[pi]], writes=[stqB[q_]])
                            else:
                                sch.add(DVE, (lambda e, q_=q_, pi=pi: e.tensor_copy(out=stq[q_][:], in_=psum[pi][:, :])),
                                        reads=[PB[pi]], writes=[stqB[q_]])
                            sch.add(SP, (lambda e, hd=hd, q_=q_, tok0=tok0: e.dma_start(
                                out=qk[hd, :, tok0:tok0 + W], in_=stq[q_][:])), reads=[stqB[q_]], dsem=dsq[q_])
                        for sub in range(4):
                            pi = next_ps()
                            for k in range(KD):
                                mm(psum[pi][:, :], uT2[s][:, k, sub * 128:(sub + 1) * 128], wq[:, k, 1024:1536],
                                   k == 0, k == KD - 1, [wqB, uB2[s]], PB[pi])
                            q_ = ev % 4
                            ev += 1
                            if ev % 2 == 0:
                                act(stv[q_][:], psum[pi][:, :], AF.Copy, reads=[PB[pi]], writes=[stvB[q_]])
                            else:
                                sch.add(DVE, (lambda e, q_=q_, pi=pi: e.tensor_copy(out=stv[q_][:], in_=psum[pi][:, :])),
                                        reads=[PB[pi]], writes=[stvB[q_]])
                            sch.add(SP, (lambda e, sub=sub, q_=q_, tok0=tok0: e.dma_start(
                                out=vbuf[tok0 + sub * 128:tok0 + (sub + 1) * 128, :], in_=stv[q_][:])),
                                reads=[stvB[q_]], dsem=dsv[q_])
                sch.barrier()
            if stop_after == "p2a":
                break

            with ExitStack() as es:
                ab_sb = sbt(es, "pa_bias", [128, 3, 256], F32)
                qTg = [sbt(es, "pa_q%d" % g, [128, 2048], BF16) for g in range(3)]
                kTg = [sbt(es, "pa_k%d" % g, [128, 2, 2048], BF16) for g in range(3)]
                Vg = [sbt(es, "pa_v%d" % g, [128, 2, 16, 128], BF16) for g in range(3)]
                accUS = sbt(es, "pa_accUS", [128, 2, 2048], F32)
                accU = accUS[:, 0, :]
                accS = accUS[:, 1, :]
                sc = [sbt(es, "pa_sc%d" % i, [128, 256], F32) for i in range(4)]
                Pt = [sbt(es, "pa_P%d" % i, [128, 256], BF16) for i in range(4)]
                yst = sbt(es, "pa_y", [128, 2048], BF16)
                abB = Buf()
                qB = [Buf() for _ in range(3)]
                kB = [[Buf(), Buf()] for _ in range(3)]
                vB = [[Buf(), Buf()] for _ in range(3)]
                accUB, accSB, ystB = Buf(), Buf(), Buf()
                scB = [Buf() for _ in range(4)]
                PtB = [Buf() for _ in range(4)]
                dab = sch.dsem("pa_ab")
                dq = [sch.dsem("pa_q%d" % g) for g in range(3)]
                dk = [[sch.dsem("pa_k%d_%d" % (g, s)) for s in range(2)] for g in range(3)]
                dv = [[sch.dsem("pa_v%d_%d" % (g, s)) for s in range(2)] for g in range(3)]
                dy = sch.dsem("pa_y")
                sch.add(SP, lambda e: e.dma_start(out=ab_sb[:], in_=abias), writes=[abB], dsem=dab)
                DIL = (1, 4, 16)
                bi = 0
                for sb in range(NSB):
                    slot = sb % 2
                    c0 = sb * 2048
                    for g in range(3):
                        d = DIL[g]
                        sch.add(SP, (lambda e, g=g, c0=c0: e.dma_start(out=qTg[g][:], in_=qk[g, :, c0:c0 + 2048])),
                                writes=[qB[g]], dsem=dq[g])
                        sch.add(SP, (lambda e, g=g, c0=c0, slot=slot: e.dma_start(
                            out=kTg[g][:, slot, :], in_=qk[3 + g, :, c0:c0 + 2048])),
                            writes=[kB[g][slot]], dsem=dk[g][slot])
                        srcv = vbuf[c0:c0 + 2048, g * 128:(g + 1) * 128].rearrange(
                            "(nn i r) c -> i r nn c", i=128, r=d)
                        nn_ = 16 // d
                        if d == 4:
                            for r in range(d):
                                sch.add(SP, (lambda e, g=g, slot=slot, srcv=srcv, r=r, nn_=nn_: e.dma_start(
                                    out=Vg[g][:, slot, r * nn_:(r + 1) * nn_, :], in_=srcv[:, r, :, :])),
                                    writes=[vB[g][slot]], dsem=dv[g][slot])
                        else:
                            sv = srcv[:, 0, :, :] if d == 1 else srcv[:, :, 0, :]
                            sch.add(SP, (lambda e, g=g, slot=slot, sv=sv: e.dma_start(
                                out=Vg[g][:, slot, :, :], in_=sv)),
                                writes=[vB[g][slot]], dsem=dv[g][slot])
                    for g in range(3):
                        d = DIL[g]
                        nn_ = 16 // d

                        def colsel(t, nn, r, d=d):
                            return t[:, nn * 128 * d:(nn + 1) * 128 * d].rearrange("p (i r) -> p r i", r=d)[:, r, :]

                        for r in range(d):
                            for nn in range(nn_):
                                first = (sb == 0 and nn == 0)
                                q_ap = colsel(qTg[g][:], nn, r)
                                kc_ap = colsel(kTg[g][:, slot, :], nn, r)
                                vc_ap = Vg[g][:, slot, r * nn_ + nn, :]
                                rd = [qB[g], kB[g][slot]]
                                vrd = [vB[g][slot]]
                                if not first:
                                    if nn >= 1:
                                        kp_ap = colsel(kTg[g][:, slot, :], nn - 1, r)
                                        vp_ap = Vg[g][:, slot, r * nn_ + nn - 1, :]
                                    else:
                                        kp_ap = colsel(kTg[g][:, 1 - slot, :], nn_ - 1, r)
                                        vp_ap = Vg[g][:, 1 - slot, r * nn_ + nn_ - 1, :]
                                        rd = rd + [kB[g][1 - slot]]
                                        vrd = vrd + [vB[g][1 - slot]]
                                ncol = 128 if first else 256
                                pi = next_ps()
                                b_ = bi % 4
                                bi += 1
                                mm(psum[pi][:, 0:128], kc_ap, q_ap, True, True, rd, PB[pi])
                                if not first:
                                    mm(psum[pi][:, 128:256], kp_ap, q_ap, True, True, rd, PB[pi])
                                sch.add(DVE, (lambda e, pi=pi, b_=b_, g=g, ncol=ncol: e.scalar_tensor_tensor(
                                    out=sc[b_][:, 0:ncol], in0=psum[pi][:, 0:ncol], scalar=QSCALE,
                                    in1=ab_sb[:, g, 0:ncol], op0=ALU.mult, op1=ALU.add)),
                                    reads=[PB[pi], abB], writes=[scB[b_]])
                                act(Pt[b_][:, 0:ncol], sc[b_][:, 0:ncol], AF.Exp, reads=[scB[b_]], writes=[PtB[b_]])
                                po = next_ps()
                                mm(psum[po][:, 0:128], vc_ap, Pt[b_][:, 0:128], True, first, vrd + [PtB[b_]], PB[po])
                                if not first:
                                    mm(psum[po][:, 0:128], vp_ap, Pt[b_][:, 128:256], False, True, vrd + [PtB[b_]], PB[po])
                                mm(psum[po][:, 128:256], ONESm, Pt[b_][:, 0:128], True, first, [PtB[b_], Bconst], PB[po])
                                if not first:
                                    mm(psum[po][:, 128:256], ONESm, Pt[b_][:, 128:256], False, True, [PtB[b_], Bconst], PB[po])
                                aUS = accUS[:, :, nn * 128 * d:(nn + 1) * 128 * d].rearrange(
                                    "p t (i r) -> p t r i", r=d)[:, :, r, :]
                                pUS = psum[po][:, 0:256].rearrange("p (t i) -> p t i", t=2)
                                if g == 0:
                                    sch.add(DVE, (lambda e, aUS=aUS, pUS=pUS: e.tensor_copy(out=aUS, in_=pUS)),
                                            reads=[PB[po]], writes=[accUB])
                                else:
                                    sch.add(DVE, (lambda e, aUS=aUS, pUS=pUS: e.tensor_tensor(
                                        out=aUS, in0=aUS, in1=pUS, op=ALU.add)),
                                        reads=[PB[po], accUB], writes=[accUB])
                    sch.add(DVE, (lambda e: e.reciprocal(out=accS, in_=accS)), reads=[accUB], writes=[accUB])
                    sch.add(DVE, (lambda e: e.tensor_tensor(out=yst[:], in0=accU, in1=accS, op=ALU.mult)),
                            reads=[accUB], writes=[ystB])
                    for qq in range(2048 // ACW):
                        sch.add(SP, (lambda e, c0=c0, qq=qq: e.dma_start(
                            out=att_own[(c0 + qq * ACW) // ACW, 0:128, :], in_=yst[:, qq * ACW:(qq + 1) * ACW])),
                            reads=[ystB], writes=[attoB[(c0 + qq * ACW) // ACW]], dsem=dy)
                sch.barrier()
            if stop_after == "p2bA":
                break

            with ExitStack() as es:
                kTB = sbt(es, "pb_k", [128, S], BF16)
                qTB = sbt(es, "pb_q", [128, S], BF16)
                VB = sbt(es, "pb_v", [128, NCH, 128], BF16)
                bm_sb = sbt(es, "pb_mask", [128, 4, 512], BF16)
                ez = [sbt(es, "pb_ez%d" % i, [128, 512], F32) for i in range(2)]
                sp = [sbt(es, "pb_sp%d" % i, [128, 512], BF16) for i in range(3)]
                wt = [sbt(es, "pb_w%d" % i, [128, 512], BF16) for i in range(2)]
                Lr = sbt(es, "pb_Lr", [128, 512], F32)
                Lrb = [sbt(es, "pb_Lrb%d" % i, [128, 512], BF16) for i in range(2)]
                ost = [sbt(es, "pb_o%d" % i, [128, 512], BF16) for i in range(2)]
                kvB, bmB = Buf(), Buf()
                ezB = [Buf(), Buf()]
                spB = [Buf(), Buf(), Buf()]
                wtB = [Buf(), Buf()]
                LrB = Buf()
                LrbB = [Buf(), Buf()]
                ostB = [Buf(), Buf()]
                dkv = [sch.dsem("pb_kv%d" % i) for i in range(4)]
                do = [sch.dsem("pb_o%d" % i) for i in range(2)]
                sch.add(SP, lambda e: e.dma_start(out=kTB[:], in_=qk[7, :, :]), writes=[kvB], dsem=dkv[0])
                sch.add(SP, lambda e: e.dma_start(out=qTB[:], in_=qk[6, :, :]), writes=[kvB], dsem=dkv[1])
                sch.add(SP, lambda e: e.dma_start(
                    out=VB[:], in_=vbuf[:, 384:512].rearrange("(c p) d -> p c d", p=128)), writes=[kvB], dsem=dkv[2])
                sch.add(POOL, lambda e: e.dma_start(out=bm_sb[:], in_=bmask), writes=[bmB], dsem=dkv[3])
                zi = 0
                si = 0
                for gq in range(S // 512):
                    n0 = 4 * gq
                    q_ap = qTB[:, n0 * 128:n0 * 128 + 512]
                    steps = list(range(n0 + 3, -1, -1))
                    po = 4 + (gq % 2)
                    ob = gq % 2
                    stA = {}

                    def stageA1(c):
                        nonlocal zi
                        pz = zi % 4
                        e_ = zi % 2
                        zi += 1
                        mm(psum[pz][:, :], kTB[:, c * 128:(c + 1) * 128], q_ap, True, False, [kvB], PB[pz])
                        act(ez[e_][:], psum[pz][:, :], AF.Exp, reads=[PB[pz]], writes=[ezB[e_]], scale=QSCALE)
                        stA[c] = (pz, e_)

                    def stageA2(c):
                        nonlocal si
                        pz, e_ = stA[c]
                        s_ = si % 3
                        si += 1
                        act(sp[s_][:], ez[e_][:], AF.Ln, reads=[ezB[e_]], writes=[spB[s_]], bias=1.0)
                        if c >= n0:
                            j = c - n0
                            sch.add(DVE, (lambda e, s_=s_, j=j: e.tensor_tensor(
                                out=sp[s_][:], in0=sp[s_][:], in1=bm_sb[:, j, :], op=ALU.mult)),
                                reads=[spB[s_], bmB], writes=[spB[s_]])
                        stA[c] = (pz, s_)

                    def stageB(c, idx):
                        pz, s_ = stA.pop(c)
                        firststep = idx == 0
                        laststep = idx == len(steps) - 1
                        lb = idx % 2
                        mm(psum[pz][:, :], NTm, sp[s_][:], False, firststep, [spB[s_], Bconst, PB[pz]], PB[pz])
                        if not firststep:
                            mm(psum[pz][:, :], NONESm, Lrb[lb][:], False, True, [LrbB[lb], Bconst, PB[pz]], PB[pz])
                        w_ = idx % 2
                        act(wt[w_][:], psum[pz][:, :], AF.Exp, reads=[PB[pz]], writes=[wtB[w_]], scale=QSCALE2)
                        if c >= n0:
                            j = c - n0
                            sch.add(DVE, (lambda e, w_=w_, j=j: e.tensor_tensor(
                                out=wt[w_][:], in0=wt[w_][:], in1=bm_sb[:, j, :], op=ALU.mult)),
                                reads=[wtB[w_], bmB], writes=[wtB[w_]])
                        mm(psum[po][:, :], VB[:, c, :], wt[w_][:], firststep, laststep, [kvB, wtB[w_]], PB[po])
                        if not laststep:
                            if firststep:
                                sch.add(DVE, (lambda e, s_=s_: e.tensor_copy(out=Lr[:], in_=sp[s_][:])),
                                        reads=[spB[s_]], writes=[LrB])
                            else:
                                sch.add(DVE, (lambda e, s_=s_: e.tensor_tensor(
                                    out=Lr[:], in0=Lr[:], in1=sp[s_][:], op=ALU.add)),
                                    reads=[spB[s_], LrB], writes=[LrB])
                            nb_ = (idx + 1) % 2
                            sch.add(DVE, (lambda e, nb_=nb_: e.tensor_copy(out=Lrb[nb_][:], in_=Lr[:])),
                                    reads=[LrB], writes=[LrbB[nb_]])

                    nst = len(steps)
                    stageA1(steps[0])
                    if nst > 1:
                        stageA1(steps[1])
                    stageA2(steps[0])
                    for idx, c in enumerate(steps):
                        if idx + 2 < nst:
                            stageA1(steps[idx + 2])
                        if idx + 1 < nst:
                            stageA2(steps[idx + 1])
                        stageB(c, idx)
                    if gq % 2 == 0:
                        act(ost[ob][:], psum[po][:, :], AF.Copy, reads=[PB[po]], writes=[ostB[ob]])
                    else:
                        sch.add(DVE, (lambda e, ob=ob, po=po: e.tensor_copy(out=ost[ob][:], in_=psum[po][:, :])),
                                reads=[PB[po]], writes=[ostB[ob]])
                    sch.add(SP, (lambda e, ob=ob, n0=n0: e.dma_start(
                        out=att_own[(n0 * 128) // ACW, 128:256, (n0 * 128) % ACW:(n0 * 128) % ACW + 512], in_=ost[ob][:])),
                        reads=[ostB[ob]], writes=[attoB[(n0 * 128) // ACW]], dsem=do[ob])
                    if ((n0 * 128) + 512) % ACW == 0:
                        ag_att((n0 * 128) // ACW)
                sch.barrier()
            if stop_after == "p2bB":
                break

            if stop_after == "ag_a":
                break

            last = (l == nlayers - 1)
            with ExitStack() as es:
                hT = sbt(es, "p3_h", [128, KD, W], F32)
                uT = sbt(es, "p3_u", [128, KD, W], BF16)
                cp = sbt(es, "p3_cp", [128, NFF, W], BF16)
                nb = sbt(es, "p3_nb", [128, KD, W], F32)
                pTt = sbt(es, "p3_p", [128, 2, W], BF16)
                wsl = [sbt(es, "p3_w%d" % i, [128, KD, 256], BF16) for i in range(4)]
                sq = [sbt(es, "p3_sq%d" % i, [128, W], BF16) for i in range(2)]
                rstd = sbt(es, "p3_rstd", [128, W], F32)
                tmp = [sbt(es, "p3_t%d" % i, [128, W + 2], F32) for i in range(6)]
                uph = sbt(es, "p3_uph", [128, 2 * NFF, 2], F32)
                hB = [Buf() for _ in range(KD)]
                uB = [Buf() for _ in range(KD)]
                cB = [Buf() for _ in range(NFF)]
                nB = [Buf() for _ in range(KD)]
                pB = Buf()
                wB = [Buf() for _ in range(4)]
                sqB = [Buf(), Buf()]
                rstdB = Buf()
                tB = [Buf() for _ in range(6)]
                uphB = [Buf() for _ in range(2 * NFF)]
                dW = [sch.dsem("p3w%d" % i) for i in range(4)]
                dH, dU, dA, dP, dO, dUo, dHH = [sch.dsem("p3%s" % n) for n in "h u a p o uo hh".split()]
                wst = {"i": 0, "t": 0}

                def slab(name, r0, nr, c0, ncol=256):
                    s = wst["i"] % 4
                    wst["i"] += 1
                    nk = nr // 128
                    if name == "wg":
                        name, c0 = "win", c0 + 6144
                    src = wbf[name][l, r0:r0 + nr, c0:c0 + ncol].rearrange("(k p) c -> p k c", p=128)
                    if not (_os.environ.get("K_NOSLAB") == "1" and wst["i"] > 4):
                        sch.add(SP, (lambda e, s=s, nk=nk, src=src, ncol=ncol: e.dma_start(
                            out=wsl[s][:, 0:nk, 0:ncol], in_=src)), reads=[wflB[name]], writes=[wB[s]], dsem=dW[s])
                    return s

                def next_t():
                    i = wst["t"] % 6
                    wst["t"] += 1
                    return i

                ne = (sq, sqB, rstd, rstdB)

                def p3_tile(t0, Wt, halo, l=l, last=last):
                    def ld_h(e):
                        if halo:
                            if l == 0:
                                return e.dma_start(out=hT[:, :, 0:2], in_=xh)
                            return e.dma_start(out=hT[:, :, 0:2], in_=hh_all.rearrange("(r p) c -> r p c", r=4)[
                                bass.ds(PID["im1"], 1), :, :].rearrange("o p (k t) -> p k (o t)", t=2))
                        src = xT if l == 0 else hbuf
                        return e.dma_start(out=hT[:], in_=src[:, :, t0:t0 + W])
                    sch.add(SP, ld_h, writes=hB, dsem=dH)

                    def ld_u(e):
                        if halo:
                            return e.dma_start(out=uT[:, :, 0:2], in_=uall[NQ - 1].rearrange("(r f) t -> r f t", r=4)[
                                bass.ds(PID["im1"], 1), :, 254:256].rearrange("o (k p) t -> p k (o t)", p=128))
                        return None
                    if halo:
                        sch.add(SP, ld_u, reads=[uallB[NQ - 1]], writes=uB, dsem=dU)
                    else:
                        for q in range(2):
                            sch.add(SP, (lambda e, q=q: e.dma_start(
                                out=uT[:, :, q * 256:(q + 1) * 256],
                                in_=ubuf_own[2 * (t0 // W) + q].rearrange("(k p) t -> p k t", p=128))),
                                reads=[ubufB[2 * (t0 // W) + q]], writes=uB, dsem=dU)

                    for a_ in range(2):
                        def ld_a(e, a_=a_):
                            if halo:
                                src = att_halo.rearrange("(s a p) t -> p a s t", s=4, a=2)[:, a_, :, :]
                            else:
                                src = att_mine[t0 // ACW, :, t0 % ACW:t0 % ACW + Wt].rearrange(
                                    "(s a p) t -> p a s t", s=4, a=2)[:, a_, :, :]
                            return e.dma_start(out=cp[:, a_ * 4:(a_ + 1) * 4, 0:Wt], in_=src)
                        sch.add(SP, ld_a, reads=[attmB], writes=cB[0:8], dsem=dA)
                    if not halo:
                        sch.add(POOL, (lambda e: e.dma_start(out=pTt[:], in_=pT[l, :, :, t0:t0 + W])),
                                writes=[pB], dsem=dP)

                    for co2 in range(8):
                        sGA = slab("wg", 0, D, co2 * 256)
                        sGB = slab("wg", 0, D, D + co2 * 256)
                        sAB = slab("wa", 0, 512, co2 * 256)
                        sBB = slab("wb", 0, 512, co2 * 256)
                        for cc in range(2):
                            co = co2 * 2 + cc
                            cs = slice(cc * 128, (cc + 1) * 128)
                            pa, pb_, pya, pyb = next_ps(), next_ps(), next_ps(), next_ps()
                            for k in range(KD):
                                mm(psum[pa][:, 0:Wt], wsl[sGA][:, k, cs], uT[:, k, 0:Wt], k == 0, k == KD - 1,
                                   [wB[sGA], uB[k]], PB[pa])
                            for k in range(KD):
                                mm(psum[pb_][:, 0:Wt], wsl[sGB][:, k, cs], uT[:, k, 0:Wt], k == 0, k == KD - 1,
                                   [wB[sGB], uB[k]], PB[pb_])
                            for s4 in range(4):
                                mm(psum[pya][:, 0:Wt], wsl[sAB][:, s4, cs], cp[:, s4, 0:Wt], s4 == 0, s4 == 3,
                                   [wB[sAB], cB[s4]], PB[pya])
                            for s4 in range(4):
                                mm(psum[pyb][:, 0:Wt], wsl[sBB][:, s4, cs], cp[:, 4 + s4, 0:Wt], s4 == 0, s4 == 3,
                                   [wB[sBB], cB[4 + s4]], PB[pyb])
                            ta, tb = next_t(), next_t()
                            act(tmp[ta][:, 0:Wt], psum[pa][:, 0:Wt], AF.Sigmoid, reads=[PB[pa]], writes=[tB[ta]])
                            act(tmp[tb][:, 0:Wt], psum[pb_][:, 0:Wt], AF.Sigmoid, reads=[PB[pb_]], writes=[tB[tb]])
                            sch.add(DVE, (lambda e, ta=ta, pya=pya: e.tensor_tensor(
                                out=tmp[ta][:, 0:Wt], in0=tmp[ta][:, 0:Wt], in1=psum[pya][:, 0:Wt], op=ALU.mult)),
                                reads=[tB[ta], PB[pya]], writes=[tB[ta]])
                            sch.add(DVE, (lambda e, tb=tb, pyb=pyb: e.tensor_tensor(
                                out=tmp[tb][:, 0:Wt], in0=tmp[tb][:, 0:Wt], in1=psum[pyb][:, 0:Wt], op=ALU.mult)),
                                reads=[tB[tb], PB[pyb]], writes=[tB[tb]])
                            sch.add(POOL, (lambda e, ta=ta, tb=tb, co=co: e.tensor_tensor(
                                out=cp[:, 8 + co, 0:Wt], in0=tmp[ta][:, 0:Wt], in1=tmp[tb][:, 0:Wt], op=ALU.add)),
                                reads=[tB[ta], tB[tb]], writes=[cB[8 + co]])
                    for co2 in range(8):
                        sO = slab("wo", 0, D, co2 * 256)
                        for cc in range(2):
                            co = co2 * 2 + cc
                            cs = slice(cc * 128, (cc + 1) * 128)
                            pi = next_ps()
                            for k in range(KD):
                                mm(psum[pi][:, 0:Wt], wsl[sO][:, k, cs], cp[:, 8 + k, 0:Wt], k == 0, k == KD - 1,
                                   [wB[sO], cB[8 + k]], PB[pi])
                            sch.add(DVE, (lambda e, co=co, pi=pi: e.tensor_copy(out=nb[:, co, 0:Wt], in_=psum[pi][:, 0:Wt])),
                                    reads=[PB[pi]], writes=[nB[co]])
                    r_ap, rB = norm_rstd(ne, [nb[:, k, 0:Wt] for k in range(KD)], nB, Wt)
                    for k in range(KD):
                        ti = next_t()
                        sch.add(DVE, (lambda e, k=k, ti=ti: e.scalar_tensor_tensor(
                            out=tmp[ti][:, 0:Wt], in0=nb[:, k, 0:Wt], scalar=gains_sb[:, l, 1, k:k + 1], in1=r_ap,
                            op0=ALU.mult, op1=ALU.mult)), reads=[nB[k], rB, Bconst], writes=[tB[ti]])
                        sch.add(POOL, (lambda e, k=k, ti=ti: e.tensor_tensor(
                            out=hT[:, k, 0:Wt], in0=hT[:, k, 0:Wt], in1=tmp[ti][:, 0:Wt], op=ALU.add)),
                            reads=[hB[k], tB[ti]], writes=[hB[k]])
                    if "hmid" in dbg and not halo and t0 == 0 and l == 0:
                        tdb = dbg_tensor("hmid", [128, KD, W])
                        sch.add(SP, (lambda e: e.dma_start(out=tdb, in_=hT[:])), reads=hB, dsem=sch.dsem("dbg2"))
                    if "mT" in dbg and not halo and t0 == 0 and l == 0:
                        tdb2 = dbg_tensor("mT", [128, KD, W], BF16)
                        sch.add(SP, (lambda e: e.dma_start(out=tdb2, in_=cp[:, 8:24, :])), reads=cB[8:24], dsem=sch.dsem("dbg3"))
                    r_ap, rB = norm_rstd(ne, [hT[:, k, 0:Wt] for k in range(KD)], hB, Wt)
                    for k in range(KD):
                        sch.add(DVE, (lambda e, k=k: e.scalar_tensor_tensor(
                            out=uT[:, k, 0:Wt], in0=hT[:, k, 0:Wt], scalar=gains_sb[:, l, 2, k:k + 1], in1=r_ap,
                            op0=ALU.mult, op1=ALU.mult)), reads=[hB[k], rB, Bconst], writes=[uB[k]])
                    for c2 in range(NFF // 2):
                        sUG = slab("wup", 0, D, c2 * 256)
                        sUV = slab("wup", 0, D, DFF + c2 * 256)
                        for cc in range(2):
                            c = c2 * 2 + cc
                            cs = slice(cc * 128, (cc + 1) * 128)
                            ys = []
                            for (sw, ch) in ((sUG, c), (sUV, NFF + c)):
                                pi = next_ps()
                                for k in range(KD):
                                    mm(psum[pi][:, 0:Wt], wsl[sw][:, k, cs], uT[:, k, 0:Wt], k == 0, k == KD - 1,
                                       [wB[sw], uB[k]], PB[pi])
                                if halo:
                                    sch.add(DVE, (lambda e, ch=ch, pi=pi: e.tensor_scalar(
                                        out=uph[:, ch, :], in0=psum[pi][:, 0:2], scalar1=hflag_sb[:, 0:1], scalar2=None,
                                        op0=ALU.mult)), reads=[PB[pi], Bconst], writes=[uphB[ch]])
                                    continue
                                xi, yi = next_t(), next_t()
                                sch.add(POOL, (lambda e, xi=xi, ch=ch: e.tensor_copy(out=tmp[xi][:, 0:2], in_=uph[:, ch, :])),
                                        reads=[uphB[ch]], writes=[tB[xi]])
                                act(tmp[xi][:, 2:2 + W], psum[pi][:, :], AF.Copy, reads=[PB[pi]], writes=[tB[xi]])
                                sch.add(POOL, (lambda e, xi=xi, ch=ch: e.tensor_copy(out=uph[:, ch, :], in_=tmp[xi][:, W:W + 2])),
                                        reads=[tB[xi]], writes=[uphB[ch]])
                                sch.add(DVE, (lambda e, xi=xi, yi=yi, ch=ch: e.tensor_scalar(
                                    out=tmp[yi][:, 0:W], in0=tmp[xi][:, 2:2 + W], scalar1=convw_sb[:, l, 2, ch:ch + 1],
                                    scalar2=convw_sb[:, l, 3, ch:ch + 1], op0=ALU.mult, op1=ALU.add)),
                                    reads=[tB[xi], Bconst], writes=[tB[yi]])
                                sch.add(DVE, (lambda e, xi=xi, yi=yi, ch=ch: e.scalar_tensor_tensor(
                                    out=tmp[yi][:, 0:W], in0=tmp[xi][:, 1:1 + W], scalar=convw_sb[:, l, 1, ch:ch + 1],
                                    in1=tmp[yi][:, 0:W], op0=ALU.mult, op1=ALU.add)),
                                    reads=[tB[xi], tB[yi], Bconst], writes=[tB[yi]])
                                sch.add(DVE, (lambda e, xi=xi, yi=yi, ch=ch: e.scalar_tensor_tensor(
                                    out=tmp[yi][:, 0:W], in0=tmp[xi][:, 0:W], scalar=convw_sb[:, l, 0, ch:ch + 1],
                                    in1=tmp[yi][:, 0:W], op0=ALU.mult, op1=ALU.add)),
                                    reads=[tB[xi], tB[yi], Bconst], writes=[tB[yi]])
                                ys.append(yi)
                            if halo:
                                continue
                            yg, yv = ys
                            act(tmp[yg][:, 0:W], tmp[yg][:, 0:W], AF.Gelu_apprx_tanh, reads=[tB[yg]], writes=[tB[yg]])
                            sch.add(POOL, (lambda e, yg=yg, yv=yv, c=c: e.tensor_tensor(
                                out=cp[:, c, :], in0=tmp[yg][:, 0:W], in1=tmp[yv][:, 0:W], op=ALU.mult)),
                                reads=[tB[yg], tB[yv]], writes=[cB[c]])
                    if halo:
                        return
                    for co2 in range(8):
                        pis = [next_ps(), next_ps()]
                        kparts = [(0, 16), (16, 16), (32, 12)]
                        for (k0, nk) in kparts:
                            sD = slab("wd", k0 * 128, nk * 128, co2 * 256)
                            for cc in range(2):
                                cs = slice(cc * 128, (cc + 1) * 128)
                                for kk in range(nk):
                                    k = k0 + kk
                                    mm(psum[pis[cc]][:, :], wsl[sD][:, kk, cs], cp[:, k, :], k == 0, k == NFF - 1,
                                       [wB[sD], cB[k]], PB[pis[cc]])
                        for cc in range(2):
                            co = co2 * 2 + cc
                            sch.add(DVE, (lambda e, co=co, pi=pis[cc]: e.tensor_copy(out=nb[:, co, :], in_=psum[pi][:, :])),
                                    reads=[PB[pis[cc]]], writes=[nB[co]])
                    r_ap, rB = norm_rstd(ne, [nb[:, k, :] for k in range(KD)], nB, W)
                    for k in range(KD):
                        ti = next_t()
                        sch.add(DVE, (lambda e, k=k, ti=ti: e.scalar_tensor_tensor(
                            out=tmp[ti][:, 0:W], in0=nb[:, k, :], scalar=gains_sb[:, l, 3, k:k + 1], in1=r_ap,
                            op0=ALU.mult, op1=ALU.mult)), reads=[nB[k], rB, Bconst], writes=[tB[ti]])
                        sch.add(POOL, (lambda e, k=k, ti=ti: e.tensor_tensor(
                            out=hT[:, k, :], in0=hT[:, k, :], in1=tmp[ti][:, 0:W], op=ALU.add)),
                            reads=[hB[k], tB[ti]], writes=[hB[k]])
                        sch.add(DVE, (lambda e, k=k: e.tensor_copy(out=uT[:, k, :], in_=hT[:, k, :])),
                                reads=[hB[k]], writes=[uB[k]])
                    if "hffn" in dbg and t0 == 0 and l == 0:
                        tdb3 = dbg_tensor("hffn", [128, KD, W])
                        sch.add(SP, (lambda e: e.dma_start(out=tdb3, in_=hT[:])), reads=hB, dsem=sch.dsem("dbg4"))
                    for co2 in range(8):
                        sG = slab("wpg", 0, D, co2 * 256)
                        sI = slab("wpi", 0, PLE, co2 * 256)
                        for cc in range(2):
                            co = co2 * 2 + cc
                            cs = slice(cc * 128, (cc + 1) * 128)
                            pg, pe = next_ps(), next_ps()
                            for k in range(KD):
                                mm(psum[pg][:, :], wsl[sG][:, k, cs], uT[:, k, :], k == 0, k == KD - 1,
                                   [wB[sG], uB[k]], PB[pg])
                            for k in range(2):
                                mm(psum[pe][:, :], wsl[sI][:, k, cs], pTt[:, k, :], k == 0, k == 1,
                                   [wB[sI], pB], PB[pe])
                            ti = next_t()
                            act(tmp[ti][:, 0:W], psum[pg][:, :], AF.Sigmoid, reads=[PB[pg]], writes=[tB[ti]])
                            sch.add(DVE, (lambda e, ti=ti, pe=pe: e.tensor_tensor(
                                out=tmp[ti][:, 0:W], in0=tmp[ti][:, 0:W], in1=psum[pe][:, :], op=ALU.mult)),
                                reads=[tB[ti], PB[pe]], writes=[tB[ti]])
                            sch.add(POOL, (lambda e, co=co, ti=ti: e.tensor_tensor(
                                out=hT[:, co, :], in0=hT[:, co, :], in1=tmp[ti][:, 0:W], op=ALU.add)),
                                reads=[hB[co], tB[ti]], writes=[hB[co]])
                    dst = outT if last else hbuf
                    sch.add(SP, (lambda e: e.dma_start(out=dst[:, :, t0:t0 + W], in_=hT[:])), reads=hB, dsem=dO)
                    if not last:
                        if t0 + W == TC:
                            sch.add(SP, (lambda e: e.dma_start(
                                out=hh_own.rearrange("p (k t) -> p k t", t=2), in_=hT[:, :, W - 2:W])),
                                reads=hB, dsem=dHH)
                        r_ap2, rB2 = norm_rstd(ne, [hT[:, k, :] for k in range(KD)], hB, W)
                        for k in range(KD):
                            sch.add(DVE, (lambda e, k=k: e.scalar_tensor_tensor(
                                out=uT[:, k, :], in0=hT[:, k, :], scalar=gains_sb[:, l + 1, 0, k:k + 1], in1=r_ap2,
                                op0=ALU.mult, op1=ALU.mult)), reads=[hB[k], rB2, Bconst], writes=[uB[k]])
                        for q in range(2):
                            sch.add(SP, (lambda e, q=q: e.dma_start(
                                out=ubuf_own[2 * (t0 // W) + q].rearrange("(k p) t -> p k t", p=128),
                                in_=uT[:, :, q * 256:(q + 1) * 256])),
                                reads=uB, writes=[ubufB[2 * (t0 // W) + q]], dsem=dUo)
                        for q in range(2):
                            ag_u(2 * (t0 // W) + q)

                attmB = Buf()
                dam = sch.dsem("p3am")
                sch.add(SP, (lambda e: e.dma_start(
                    out=att_mine.rearrange("c r t -> (c r) t"),
                    in_=att_all.rearrange("(i c) r t -> i (c r) t", i=4)[bass.ds(PID["i"], 1), :, :].rearrange(
                        "o x t -> (o x) t"))), reads=attaB, writes=[attmB], dsem=dam)
                sch.add(SP, (lambda e: e.dma_start(
                    out=att_halo,
                    in_=att_all.rearrange("(i c) r t -> i c r t", i=4)[
                        bass.ds(PID["im1"], 1), TC // ACW - 1, :, ACW - 2:ACW].rearrange("o r t -> (o r) t"))),
                    reads=attaB, writes=[attmB], dsem=dam)
                p3_tile(0, 2, True)
                for tt in range(NT):
                    p3_tile(tt * W, W, False)
                sch.barrier()
            if not last:
                dcc3 = sch.dsem("cc_h", step=1)
                sch.add(POOL, (lambda e: e.collective_compute(
                    "AllGather", ALU.bypass, replica_groups=GROUPS, ins=[hh_own], outs=[hh_all])), dsem=dcc3)
                sch.barrier()

        ddbg = sch.dsem("dbg")
        srcs = {"uall": (uall, [NQ, 4 * D, 256], BF16), "qk": (qk, [8, 128, S], BF16), "vbuf": (vbuf, [S, 512], BF16),
                "att_own": (att_own, [NAC, 256, ACW], BF16), "att_all": (att_all, [NAC, 1024, ACW], BF16),
                "hbuf": (hbuf, [128, KD, TC], F32), "ubuf_own": (ubuf_own, [NQ, D, 256], BF16)}
        for name in dbg:
            if name not in srcs:
                continue
            src, shape, dt = srcs[name]
            t = dbg_tensor(name, shape, dt)
            sch.add(SP, (lambda e, t=t, src=src: e.dma_start(out=t, in_=src)), dsem=ddbg)
        sch.final_wait(SP)
        stats = sch.emit(nc)
    return nc, stats, list(dbg_out.keys())


def make_inputs(S, nl, x, p, g_mix_pre, w_in, w_branch_a, w_branch_b, w_out, g_mix_post, g_ffn_pre, w_up,
                conv_w, conv_b, w_down, g_ffn_post, w_ple_in, w_ple_gate):
    TC = S // 4
    f32 = np.float32
    WA, WB_ = 1536, 512
    gains = np.stack([g_mix_pre[:nl], g_mix_post[:nl], g_ffn_pre[:nl], g_ffn_post[:nl]], axis=1)
    gains = np.ascontiguousarray(gains.reshape(nl, 4, KD, 128).transpose(3, 0, 1, 2)).astype(f32)
    cw = np.concatenate([conv_w[:nl], conv_b[:nl, None, :]], axis=1)
    cw = np.ascontiguousarray(cw.reshape(nl, 4, 2 * NFF, 128).transpose(3, 0, 1, 2)).astype(f32)
    wfull = {"win": w_in[:nl], "wa": w_branch_a[:nl], "wb": w_branch_b[:nl], "wo": w_out[:nl], "wup": w_up[:nl],
             "wd": w_down[:nl], "wpi": w_ple_in[:nl], "wpg": w_ple_gate[:nl]}
    wshards = [dict() for _ in range(4)]
    for n, r, c in WSPECS:
        flat = np.ascontiguousarray(wfull[n]).reshape(-1).astype(f32, copy=False)
        pieces = [[] for _ in range(4)]
        for (st, sz) in wchunks(flat.size):
            q = sz // 4
            for i in range(4):
                pieces[i].append(flat[st + i * q: st + (i + 1) * q])
        for i in range(4):
            wshards[i][n + "_sh"] = np.concatenate(pieces[i]).reshape(-1, 2048)
    common = {"gains": gains, "convw": cw}
    idx = np.arange(128)
    kk, kq = idx[:, None], idx[None, :]
    cbf = np.zeros((128, 3, 128), f32)
    cbf[:, 0, :] = np.where(kk >= kq, -NEGC, 0.0)
    cbf[:, 1, :] = -NEGC
    cbf[:, 2, :] = 1.0
    common["cbf"] = cbf
    bm = np.zeros((128, 4, 512), f32)
    for j in range(4):
        for b in range(4):
            if b > j:
                bm[:, j, b * 128:(b + 1) * 128] = 1.0
            elif b == j:
                bm[:, j, b * 128:(b + 1) * 128] = (kk < kq).astype(f32)
    common["bmask"] = bm
    DIL = (1, 4, 16)
    in_maps = []
    for c in range(8):
        b, i = divmod(c, 4)
        j = i
        m = dict(common)
        xs = x[b, i * TC:(i + 1) * TC, :]
        m["xT"] = np.ascontiguousarray(xs.reshape(TC, KD, 128).transpose(2, 1, 0)).astype(f32)
        if i == 0:
            m["xh"] = np.zeros((128, KD, 2), f32)
        else:
            m["xh"] = np.ascontiguousarray(x[b, i * TC - 2:i * TC, :].reshape(2, KD, 128).transpose(2, 1, 0)).astype(f32)
        ps_ = p[:nl, b, i * TC:(i + 1) * TC, :]
        m["pT"] = np.ascontiguousarray(ps_.reshape(nl, TC, 2, 128).transpose(0, 3, 2, 1)).astype(f32)
        m.update(wshards[i])
        m["hflag"] = np.full((128, 1), 0.0 if i == 0 else 1.0, f32)
        ab = np.zeros((128, 3, 256), f32)
        for g in range(3):
            h = g * 4 + j
            slope = 2.0 ** (-8.0 * (h + 1) / 12.0)
            d = DIL[g]
            st_cur = (kq - kk).astype(np.float64)
            cur = np.where(st_cur >= 0, -slope * st_cur * d, MASKV)
            st_prev = (128 + kq - kk).astype(np.float64)
            prev = np.where(st_prev <= 128, -slope * st_prev * d, MASKV)
            ab[:, g, 0:128] = cur
            ab[:, g, 128:256] = prev
        m["abias"] = ab
        in_maps.append(m)
    return in_maps


_CACHE = {}


def kernel(**inputs):
    inputs = {k: np.asarray(v) for k, v in inputs.items()}
    x = inputs["x"]
    B, S, _ = x.shape
    nl = inputs["w_in"].shape[0]
    key = (S, nl)
    if key not in _CACHE:
        import os as _os2
        _CACHE[key] = build(S, nl, stop_after=_os2.environ.get("K_STOP"))[0]
    nc = _CACHE[key]
    in_maps = make_inputs(S, nl, **inputs)
    res = run_bass_kernel_spmd(nc, in_maps, core_ids=list(range(8)))
    TC = S // 4
    out = np.empty((B, S, D), np.float32)
    for c in range(8):
        b, i = divmod(c, 4)
        o = np.asarray(res.results[c]["outT"])
        out[b, i * TC:(i + 1) * TC, :] = o.transpose(2, 1, 0).reshape(TC, D)
    return out
```

```python
from contextlib import ExitStack
import math
import numpy as np
import ml_dtypes
import concourse.bass as bass
import concourse.mybir as mybir
from concourse.bass_utils import run_bass_kernel_spmd

F32 = mybir.dt.float32
BF16 = mybir.dt.bfloat16
AF = mybir.ActivationFunctionType
ALU = mybir.AluOpType

PE, ACT, DVE, POOL, SP = "tensor", "scalar", "vector", "gpsimd", "sync"
ENGS = (PE, ACT, DVE, POOL, SP)

D = 2048
KD = 16
DFF = 5632
NFF = 44
PLE = 256
L_ = 2
W = 512
GROUPS = [[0, 1, 2, 3], [4, 5, 6, 7]]
RMS_EPS = 1e-6
QSCALE = 1.0 / math.sqrt(128.0)
NEGC = 11.3125
QSCALE2 = 1.0 / NEGC
MASKV = -30000.0


class Buf:
    __slots__ = ("name", "w", "r", "rd")

    def __init__(self, name=""):
        self.name = name
        self.w = None
        self.r = {}
        self.rd = []


class DSem:
    __slots__ = ("name", "cnt", "h", "step", "async_")

    def __init__(self, name, step=16, async_=False):
        self.name = name
        self.cnt = 0
        self.h = None
        self.step = step
        self.async_ = async_


class Op:
    __slots__ = ("eng", "fn", "deps", "sig", "sidx", "dsem", "dval", "gid")

    def __init__(self, eng, fn, gid):
        self.eng = eng
        self.fn = fn
        self.deps = []
        self.sig = False
        self.sidx = 0
        self.dsem = None
        self.dval = 0
        self.gid = gid


PID = {}


class Sched:
    def __init__(self):
        self.ops = {e: [] for e in ENGS}
        self.gid = 0
        self.dsems = []
        self.bar = []
        self.last = {}
        self.lastdma = {}
        self.dsem_by_name = {}

    def dsem(self, name, step=16, async_=False):
        if name in self.dsem_by_name:
            return self.dsem_by_name[name]
        d = DSem(name, step, async_)
        self.dsems.append(d)
        self.dsem_by_name[name] = d
        return d

    def add(self, eng, fn, reads=(), writes=(), dsem=None, nobar=False):
        self.gid += 1
        op = Op(eng, fn, self.gid)
        is_dma = dsem is not None
        deps = {}

        def adddep(d):
            if d is None:
                return
            if d.dsem is not None:
                deps[("d", d.gid)] = d
            else:
                if d.eng == PE and eng == PE and not is_dma:
                    return
                k = d.eng
                if k not in deps or deps[k].gid < d.gid:
                    deps[k] = d

        for b in reads:
            adddep(b.w)
        for b in writes:
            if not (is_dma and b.w is not None and b.w.dsem is dsem):
                adddep(b.w)
            for d in b.r.values():
                adddep(d)
            for d in b.rd:
                adddep(d)
        if not nobar:
            for d in self.bar:
                adddep(d)
        op.deps = list(deps.values())
        for d in op.deps:
            if d.dsem is None:
                d.sig = True
        for b in reads:
            if is_dma:
                b.rd.append(op)
            else:
                b.r[eng] = op
        for b in writes:
            b.w = op
            b.r = {}
            b.rd = []
        if is_dma:
            dsem.cnt += dsem.step
            op.dsem = dsem
            op.dval = dsem.cnt
            self.lastdma[id(dsem)] = op
        else:
            self.last[eng] = op
        self.ops[eng].append(op)
        return op

    def barrier(self, full=False):
        deps = list(self.last.values()) + [o for o in self.lastdma.values() if full or not o.dsem.async_]
        for d in deps:
            if d.dsem is None:
                d.sig = True
        self.bar = deps

    def final_wait(self, eng=SP):
        self.barrier(full=True)
        self.add(eng, None)

    def emit(self, nc):
        for e in ENGS:
            c = 0
            for op in self.ops[e]:
                if op.dsem is None and op.sig:
                    c += 1
                    op.sidx = c
        stats = {}
        with ExitStack() as es:
            esem = {e: es.enter_context(nc.semaphore("s_" + e)) for e in ENGS}
            for d in self.dsems:
                d.h = es.enter_context(nc.semaphore("d_" + d.name))
            block = es.enter_context(nc.Block())

            def run(e_name):
                def body(e):
                    waited = {}
                    nw = 0
                    if e_name == SP:
                        PID["sp"] = e.partition_id()
                        PID["i"] = PID["sp"] % 4
                        PID["im1"] = (PID["sp"] + 3) % 4
                    for op in self.ops[e_name]:
                        for d in op.deps:
                            if d.dsem is not None:
                                key, h, val = id(d.dsem), d.dsem.h, d.dval
                            else:
                                key, h, val = d.eng, esem[d.eng], d.sidx
                            if waited.get(key, 0) < val:
                                e.wait_ge(h, val)
                                waited[key] = val
                                nw += 1
                        if op.fn is None:
                            continue
                        ins = op.fn(e)
                        if op.dsem is not None:
                            ins.then_inc(op.dsem.h, op.dsem.step)
                        elif op.sig:
                            ins.then_inc(esem[e_name], 1)
                    stats[e_name] = (len(self.ops[e_name]), nw)
                return body

            block.tensor(run(PE))
            block.scalar(run(ACT))
            block.vector(run(DVE))
            block.gpsimd(run(POOL))
            block.sync(run(SP))
        return stats


WSPECS = [
    ("win", D, 10240), ("wa", 512, D), ("wb", 512, D), ("wo", D, D),
    ("wup", D, 2 * DFF), ("wd", DFF, D), ("wpi", PLE, D), ("wpg", D, D),
]
CE = 2 * 1024 * 1024


def wchunks(n_total):
    out = []
    st = 0
    while st < n_total:
        sz = min(CE, n_total - st)
        assert sz % 8192 == 0
        out.append((st, sz))
        st += sz
    return out


def build(S, nlayers=L_, stop_after=None, dbg=()):
    TC = S // 4
    NT = TC // W
    NSB = S // 2048
    NCH = S // 128
    nc = bass.Bass("TRN2", target_bir_lowering=False)
    sch = Sched()
    ein = lambda n, sh, dt=F32: nc.dram_tensor(n, list(sh), dt, kind="ExternalInput").ap()

    xT = ein("xT", [128, KD, TC])
    xh = ein("xh", [128, KD, 2])
    pT = ein("pT", [nlayers, 128, 2, TC])
    wsh = {n: ein(n + "_sh", [nlayers * r * c // 4 // 2048, 2048]) for n, r, c in WSPECS}
    gains = ein("gains", [128, nlayers, 4, KD])
    convw = ein("convw", [128, nlayers, 4, 2 * NFF])
    hflag = ein("hflag", [128, 1])
    cbf = ein("cbf", [128, 3, 128])
    abias = ein("abias", [128, 3, 256])
    bmask = ein("bmask", [128, 4, 512])
    outT = nc.dram_tensor("outT", [128, KD, TC], F32, kind="ExternalOutput").ap()
    dbg_out = {}
    dt_ = lambda n, sh, dt: nc.dram_tensor(n, list(sh), dt).ap()
    wshbf = {n: dt_(n + "_shbf", [nlayers * r * c // 4 // 2048, 2048], BF16) for n, r, c in WSPECS}
    wflat = {n: dt_(n + "_bf", [nlayers * r * c // 2048, 2048], BF16) for n, r, c in WSPECS}
    wbf = {n: wflat[n].rearrange("a b -> (a b)").rearrange("(l r c) -> l r c", l=nlayers, r=r, c=c)
           for n, r, c in WSPECS}
    NQ = TC // 256
    ubuf_own = dt_("ubuf_own", [NQ, D, 256], BF16)
    uall = dt_("uall", [NQ, 4 * D, 256], BF16)
    qk = dt_("qk", [8, 128, S], BF16)
    vbuf = dt_("vbuf", [S, 512], BF16)
    ACW = min(2048, TC)
    NAC = S // ACW
    att_own = dt_("att_own", [NAC, 256, ACW], BF16)
    att_all = dt_("att_all", [NAC, 4 * 256, ACW], BF16)
    wq_mine = dt_("wq_mine", [D, 12, 128], BF16)
    att_mine = dt_("att_mine", [TC // ACW, 1024, ACW], BF16)
    att_halo = dt_("att_halo", [1024, 2], BF16)
    hbuf = dt_("hbuf", [128, KD, TC], F32)
    hh_own = dt_("hh_own", [128, KD * 2], F32)
    hh_all = dt_("hh_all", [4 * 128, KD * 2], F32)

    wshB = {n: [Buf(), Buf(), Buf()] for n, _, _ in WSPECS}
    wflB = {n: Buf() for n, _, _ in WSPECS}
    ubufB = [Buf() for _ in range(NQ)]
    uallB = [Buf() for _ in range(NQ)]
    attoB = [Buf() for _ in range(NAC)]
    attaB = [Buf() for _ in range(NAC)]

    def ag_u(q):
        sch.add(POOL, (lambda e, q=q: e.collective_compute(
            "AllGather", ALU.bypass, replica_groups=GROUPS, ins=[ubuf_own[q]], outs=[uall[q]])),
            reads=[ubufB[q]], writes=[uallB[q]], dsem=sch.dsem("cc_u%d" % (q % 8), step=1, async_=True), nobar=True)

    def ag_att(q):
        sch.add(POOL, (lambda e, q=q: e.collective_compute(
            "AllGather", ALU.bypass, replica_groups=GROUPS, ins=[att_own[q]], outs=[att_all[q]])),
            reads=[attoB[q]], writes=[attaB[q]], dsem=sch.dsem("cc_a%d" % (q % 8), step=1, async_=True), nobar=True)

    def dbg_tensor(name, shape, dt=F32):
        t = nc.dram_tensor("dbg_" + name, list(shape), dt, kind="ExternalOutput").ap()
        dbg_out[name] = t
        return t

    with ExitStack() as top:
        _uid = [0]

        def sbt(es, n, sh, dt):
            _uid[0] += 1
            return es.enter_context(nc.sbuf_tensor("%s_%d" % (n, _uid[0]), list(sh), dt))
        gains_sb = sbt(top, "gains_sb", [128, nlayers, 4, KD], F32)
        convw_sb = sbt(top, "convw_sb", [128, nlayers, 4, 2 * NFF], F32)
        hflag_sb = sbt(top, "hflag_sb", [128, 1], F32)
        cbf_sb = sbt(top, "cbf_sb", [128, 3, 128], BF16)
        psum = [top.enter_context(nc.psum_tensor("ps%d" % i, [128, 512], F32)) for i in range(8)]
        PB = [Buf("ps%d" % i) for i in range(8)]
        Bconst = Buf("const")
        dconst = [sch.dsem("const%d" % i) for i in range(4)]
        sch.add(SP, lambda e: e.dma_start(out=gains_sb[:], in_=gains), writes=[Bconst], dsem=dconst[0])
        sch.add(SP, lambda e: e.dma_start(out=convw_sb[:], in_=convw), writes=[Bconst], dsem=dconst[1])
        sch.add(SP, lambda e: e.dma_start(out=hflag_sb[:], in_=hflag), writes=[Bconst], dsem=dconst[2])
        sch.add(POOL, lambda e: e.dma_start(out=cbf_sb[:], in_=cbf), writes=[Bconst], dsem=dconst[3])
        NTm = cbf_sb[:, 0, :]
        NONESm = cbf_sb[:, 1, :]
        ONESm = cbf_sb[:, 2, :]

        import os as _os
        _skip_cast = _os.environ.get("K_SKIP_CAST") == "1"
        with ExitStack() as es:
            NCB = 3
            cst = [sbt(es, "cast%d" % i, [128, 4, 2048], BF16) for i in range(NCB)]
            cstB = [Buf() for _ in range(NCB)]
            dci = [sch.dsem("casti%d" % i) for i in range(NCB)]
            dco = [sch.dsem("casto%d" % i) for i in range(NCB)]
            ci = 0
            for n, r, c in ([] if _skip_cast else WSPECS):
                rows = nlayers * r * c // 4 // 2048
                src = wsh[n]
                dst = wshbf[n]
                for r0 in range(0, rows, 512):
                    nr = min(512, rows - r0)
                    pc = 128 if nr % 128 == 0 else nr
                    assert pc <= 128
                    q = nr // pc
                    s_ = ci % NCB
                    ci += 1
                    sap = src[r0:r0 + nr, :].rearrange("(q p) b -> p q b", p=pc)
                    dap = dst[r0:r0 + nr, :].rearrange("(q p) b -> p q b", p=pc)
                    sch.add(POOL, (lambda e, s_=s_, q=q, sap=sap, pc=pc: e.dma_start(out=cst[s_][0:pc, 0:q, :], in_=sap)),
                            writes=[cstB[s_]], dsem=dci[s_])
                    sch.add(SP, (lambda e, s_=s_, q=q, dap=dap, pc=pc: e.dma_start(out=dap, in_=cst[s_][0:pc, 0:q, :])),
                            reads=[cstB[s_]], writes=[wshB[n][s_]], dsem=dco[s_])
        sch.barrier()

        def gather_weight(n, r, c):
            dccw = sch.dsem("cc_w_" + n, step=1, async_=True)
            for (st, sz) in wchunks(nlayers * r * c):
                a0, na = st // 4 // 2048, sz // 4 // 2048
                b0, nb_ = st // 2048, sz // 2048
                sch.add(POOL, (lambda e, n=n, a0=a0, na=na, b0=b0, nb_=nb_: e.collective_compute(
                    "AllGather", ALU.bypass, replica_groups=GROUPS, ins=[wshbf[n][a0:a0 + na, :]],
                    outs=[wflat[n][b0:b0 + nb_, :]])), reads=wshB[n], writes=[wflB[n]], dsem=dccw, nobar=True)

        if not _skip_cast:
            gather_weight(*WSPECS[0])

        ring = {"ps": 0}

        def next_ps(n=7):
            i = ring["ps"] % n
            ring["ps"] += 1
            return i

        def mm(out, lhsT, rhs, start, stop, reads, pb):
            sch.add(PE, (lambda e: e.matmul(out, lhsT, rhs, start=start, stop=stop)), reads=reads, writes=[pb])

        def act(out, in_, func, reads, writes, **kw):
            sch.add(ACT, (lambda e: e.activation(out=out, in_=in_, func=func, **kw)), reads=reads, writes=writes)

        def norm_rstd(es_bufs, src_aps, src_bufs, Wt):
            sq, sqB, rstd, rstdB = es_bufs
            n = len(src_aps)
            for k in range(n):
                s = k % len(sq)
                act(sq[s][:, 0:Wt], src_aps[k], AF.Square, reads=[src_bufs[k]], writes=[sqB[s]])
                mm(psum[7][:, 0:Wt], ONESm, sq[s][:, 0:Wt], k == 0, k == n - 1, [sqB[s], Bconst], PB[7])
            act(rstd[:, 0:Wt], psum[7][:, 0:Wt], AF.Sqrt, reads=[PB[7]], writes=[rstdB], scale=1.0 / D, bias=RMS_EPS)
            sch.add(DVE, (lambda e: e.reciprocal(out=rstd[:, 0:Wt], in_=rstd[:, 0:Wt])), reads=[rstdB], writes=[rstdB])
            return rstd[:, 0:Wt], rstdB

        with ExitStack() as es:
            hT = sbt(es, "p0_h", [128, KD, W], F32)
            uT = sbt(es, "p0_u", [128, KD, W], BF16)
            sq = [sbt(es, "p0_sq%d" % i, [128, W], BF16) for i in range(2)]
            rstd = sbt(es, "p0_rstd", [128, W], F32)
            hB = [Buf() for _ in range(KD)]
            uB = [Buf() for _ in range(KD)]
            sqB = [Buf(), Buf()]
            rstdB = Buf()
            dh, du = sch.dsem("p0h"), sch.dsem("p0u")
            for tt in range(0 if stop_after == "cast" else NT):
                t0 = tt * W
                sch.add(SP, (lambda e, t0=t0: e.dma_start(out=hT[:], in_=xT[:, :, t0:t0 + W])), writes=hB, dsem=dh)
                r_ap, rB = norm_rstd((sq, sqB, rstd, rstdB), [hT[:, k, :] for k in range(KD)], hB, W)
                for k in range(KD):
                    sch.add(DVE, (lambda e, k=k: e.scalar_tensor_tensor(
                        out=uT[:, k, :], in0=hT[:, k, :], scalar=gains_sb[:, 0, 0, k:k + 1], in1=r_ap,
                        op0=ALU.mult, op1=ALU.mult)), reads=[hB[k], rB, Bconst], writes=[uB[k]])
                for q in range(2):
                    sch.add(SP, (lambda e, tt=tt, q=q: e.dma_start(
                        out=ubuf_own[2 * tt + q].rearrange("(k p) t -> p k t", p=128),
                        in_=uT[:, :, q * 256:(q + 1) * 256])),
                        reads=uB, writes=[ubufB[2 * tt + q]], dsem=du)
                for q in range(2):
                    ag_u(2 * tt + q)
            sch.barrier()
        for spec in WSPECS[1:]:
            if not _skip_cast:
                gather_weight(*spec)

        for l in range(nlayers):
            if stop_after in ("cast", "p0"):
                break
            if stop_after == "ag_u":
                break

            with ExitStack() as es:
                wq = sbt(es, "p2a_w", [128, KD, 1536], BF16)
                uT2 = [sbt(es, "p2a_u%d" % i, [128, KD, W], BF16) for i in range(2)]
                stq = [sbt(es, "p2a_sq%d" % i, [128, W], BF16) for i in range(4)]
                stv = [sbt(es, "p2a_sv%d" % i, [128, 512], BF16) for i in range(4)]
                wqB = Buf()
                uB2 = [Buf(), Buf()]
                stqB = [Buf() for _ in range(4)]
                stvB = [Buf() for _ in range(4)]
                dw = sch.dsem("p2aw")
                du2 = [sch.dsem("p2au%d" % i) for i in range(2)]
                dsq = [sch.dsem("p2asq%d" % i) for i in range(4)]
                dsv = [sch.dsem("p2asv%d" % i) for i in range(4)]
                wqmB = Buf()
                dwm = sch.dsem("p2awm")
                sch.add(SP, (lambda e, l=l: e.dma_start(
                    out=wq_mine, in_=wbf["win"][l].rearrange("r (G j c) -> r G j c", j=4, c=128)[
                        :, 0:12, bass.ds(PID["i"], 1), :].rearrange("r G o c -> r G (o c)"))),
                    reads=[wflB["win"]], writes=[wqmB], dsem=dwm)
                for (ga, gb_, c0_) in ((0, 6, 0), (9, 11, 768), (6, 9, 1024), (11, 12, 1408)):
                    sch.add(SP, (lambda e, ga=ga, gb_=gb_, c0_=c0_: e.dma_start(
                        out=wq[:, :, c0_:c0_ + (gb_ - ga) * 128],
                        in_=wq_mine[:, ga:gb_, :].rearrange("(k p) G c -> p k (G c)", p=128))),
                        reads=[wqmB], writes=[wqB], dsem=dw)
                it = 0
                ev = 0
                for r in range(4):
                    for tt in range(NT):
                        s = it % 2
                        it += 1
                        tok0 = r * TC + tt * W
                        for q in range(2):
                            sch.add(SP, (lambda e, r=r, tt=tt, s=s, q=q: e.dma_start(
                                out=uT2[s][:, :, q * 256:(q + 1) * 256],
                                in_=uall[2 * tt + q, r * D:(r + 1) * D, :].rearrange(
                                    "(k p) t -> p k t", p=128))), reads=[uallB[2 * tt + q]], writes=[uB2[s]], dsem=du2[s])
                        for hd in range(8):
                            pi = next_ps()
                            for k in range(KD):
                                mm(psum[pi][:, :], wq[:, k, hd * 128:(hd + 1) * 128], uT2[s][:, k, :],
                                   k == 0, k == KD - 1, [wqB, uB2[s]], PB[pi])
                            q_ = ev % 4
                            ev += 1
                            if ev % 2 == 0:
                                act(stq[q_][:], psum[pi][:, :], AF.Copy, reads=[PB[pi]], writes=[stqB[q_]])
                            else:
                                sch.add(DVE, (lambda e, q_=q_, pi=pi: e.tensor_copy(out=stq[q_][:], in_=psum[pi][:, :])),
                                        reads=[PB[pi]], writes=[stqB[q_]])
                            sch.add(SP, (lambda e, hd=hd, q_=q_, tok0=tok0: e.dma_start(
                                out=qk[hd, :, tok0:tok0 + W], in_=stq[q_][:])), reads=[stqB[q_]], dsem=dsq[q_])
                        for sub in range(4):
                            pi = next_ps()
                            for k in range(KD):
                                mm(psum[pi][:, :], uT2[s][:, k, sub * 128:(sub + 1) * 128], wq[:, k, 1024:1536],
                                   k == 0, k == KD - 1, [wqB, uB2[s]], PB[pi])
                            q_ = ev % 4
                            ev += 1
                            if ev % 2 == 0:
                                act(stv[q_][:], psum[pi][:, :], AF.Copy, reads=[PB[pi]], writes=[stvB[q_]])
                            else:
                                sch.add(DVE, (lambda e, q_=q_, pi=pi: e.tensor_copy(out=stv[q_][:], in_=psum[pi][:, :])),
                                        reads=[PB[pi]], writes=[stvB[q_]])
                            sch.add(SP, (lambda e, sub=sub, q_=q_, tok0=tok0: e.dma_start(
                                out=vbuf[tok0 + sub * 128:tok0 + (sub + 1) * 128, :], in_=stv[q_][:])),
                                reads=[stvB[q_]], dsem=dsv[q_])
                sch.barrier()
            if stop_after == "p2a":
                break

            with ExitStack() as es:
                ab_sb = sbt(es, "pa_bias", [128, 3, 256], F32)
                qTg = [sbt(es, "pa_q%d" % g, [128, 2048], BF16) for g in range(3)]
                kTg = [sbt(es, "pa_k%d" % g, [128, 2, 2048], BF16) for g in range(3)]
                Vg = [sbt(es, "pa_v%d" % g, [128, 2, 16, 128], BF16) for g in range(3)]
                accUS = sbt(es, "pa_accUS", [128, 2, 2048], F32)
                accU = accUS[:, 0, :]
                accS = accUS[:, 1, :]
                sc = [sbt(es, "pa_sc%d" % i, [128, 256], F32) for i in range(4)]
                Pt = [sbt(es, "pa_P%d" % i, [128, 256], BF16) for i in range(4)]
                yst = sbt(es, "pa_y", [128, 2048], BF16)
                abB = Buf()
                qB = [Buf() for _ in range(3)]
                kB = [[Buf(), Buf()] for _ in range(3)]
                vB = [[Buf(), Buf()] for _ in range(3)]
                accUB, accSB, ystB = Buf(), Buf(), Buf()
                scB = [Buf() for _ in range(4)]
                PtB = [Buf() for _ in range(4)]
                dab = sch.dsem("pa_ab")
                dq = [sch.dsem("pa_q%d" % g) for g in range(3)]
                dk = [[sch.dsem("pa_k%d_%d" % (g, s)) for s in range(2)] for g in range(3)]
                dv = [[sch.dsem("pa_v%d_%d" % (g, s)) for s in range(2)] for g in range(3)]
                dy = sch.dsem("pa_y")
                sch.add(SP, lambda e: e.dma_start(out=ab_sb[:], in_=abias), writes=[abB], dsem=dab)
                DIL = (1, 4, 16)
                bi = 0
                for sb in range(NSB):
                    slot = sb % 2
                    c0 = sb * 2048
                    for g in range(3):
                        d = DIL[g]
                        sch.add(SP, (lambda e, g=g, c0=c0: e.dma_start(out=qTg[g][:], in_=qk[g, :, c0:c0 + 2048])),
                                writes=[qB[g]], dsem=dq[g])
                        sch.add(SP, (lambda e, g=g, c0=c0, slot=slot: e.dma_start(
                            out=kTg[g][:, slot, :], in_=qk[3 + g, :, c0:c0 + 2048])),
                            writes=[kB[g][slot]], dsem=dk[g][slot])
                        srcv = vbuf[c0:c0 + 2048, g * 128:(g + 1) * 128].rearrange(
                            "(nn i r) c -> i r nn c", i=128, r=d)
                        nn_ = 16 // d
                        if d == 4:
                            for r in range(d):
                                sch.add(SP, (lambda e, g=g, slot=slot, srcv=srcv, r=r, nn_=nn_: e.dma_start(
                                    out=Vg[g][:, slot, r * nn_:(r + 1) * nn_, :], in_=srcv[:, r, :, :])),
                                    writes=[vB[g][slot]], dsem=dv[g][slot])
                        else:
                            sv = srcv[:, 0, :, :] if d == 1 else srcv[:, :, 0, :]
                            sch.add(SP, (lambda e, g=g, slot=slot, sv=sv: e.dma_start(
                                out=Vg[g][:, slot, :, :], in_=sv)),
                                writes=[vB[g][slot]], dsem=dv[g][slot])
                    for g in range(3):
                        d = DIL[g]
                        nn_ = 16 // d

                        def colsel(t, nn, r, d=d):
                            return t[:, nn * 128 * d:(nn + 1) * 128 * d].rearrange("p (i r) -> p r i", r=d)[:, r, :]

                        for r in range(d):
                            for nn in range(nn_):
                                first = (sb == 0 and nn == 0)
                                q_ap = colsel(qTg[g][:], nn, r)
                                kc_ap = colsel(kTg[g][:, slot, :], nn, r)
                                vc_ap = Vg[g][:, slot, r * nn_ + nn, :]
                                rd = [qB[g], kB[g][slot]]
                                vrd = [vB[g][slot]]
                                if not first:
                                    if nn >= 1:
                                        kp_ap = colsel(kTg[g][:, slot, :], nn - 1, r)
                                        vp_ap = Vg[g][:, slot, r * nn_ + nn - 1, :]
                                    else:
                                        kp_ap = colsel(kTg[g][:, 1 - slot, :], nn_ - 1, r)
                                        vp_ap = Vg[g][:, 1 - slot, r * nn_ + nn_ - 1, :]
                                        rd = rd + [kB[g][1 - slot]]
                                        vrd = vrd + [vB[g][1 - slot]]
                                ncol = 128 if first else 256
                                pi = next_ps()
                                b_ = bi % 4
                                bi += 1
                                mm(psum[pi][:, 0:128], kc_ap, q_ap, True, True, rd, PB[pi])
                                if not first:
                                    mm(psum[pi][:, 128:256], kp_ap, q_ap, True, True, rd, PB[pi])
                                sch.add(DVE, (lambda e, pi=pi, b_=b_, g=g, ncol=ncol: e.scalar_tensor_tensor(
                                    out=sc[b_][:, 0:ncol], in0=psum[pi][:, 0:ncol], scalar=QSCALE,
                                    in1=ab_sb[:, g, 0:ncol], op0=ALU.mult, op1=ALU.add)),
                                    reads=[PB[pi], abB], writes=[scB[b_]])
                                act(Pt[b_][:, 0:ncol], sc[b_][:, 0:ncol], AF.Exp, reads=[scB[b_]], writes=[PtB[b_]])
                                po = next_ps()
                                mm(psum[po][:, 0:128], vc_ap, Pt[b_][:, 0:128], True, first, vrd + [PtB[b_]], PB[po])
                                if not first:
                                    mm(psum[po][:, 0:128], vp_ap, Pt[b_][:, 128:256], False, True, vrd + [PtB[b_]], PB[po])
                                mm(psum[po][:, 128:256], ONESm, Pt[b_][:, 0:128], True, first, [PtB[b_], Bconst], PB[po])
                                if not first:
                                    mm(psum[po][:, 128:256], ONESm, Pt[b_][:, 128:256], False, True, [PtB[b_], Bconst], PB[po])
                                aUS = accUS[:, :, nn * 128 * d:(nn + 1) * 128 * d].rearrange(
                                    "p t (i r) -> p t r i", r=d)[:, :, r, :]
                                pUS = psum[po][:, 0:256].rearrange("p (t i) -> p t i", t=2)
                                if g == 0:
                                    sch.add(DVE, (lambda e, aUS=aUS, pUS=pUS: e.tensor_copy(out=aUS, in_=pUS)),
                                            reads=[PB[po]], writes=[accUB])
                                else:
                                    sch.add(DVE, (lambda e, aUS=aUS, pUS=pUS: e.tensor_tensor(
                                        out=aUS, in0=aUS, in1=pUS, op=ALU.add)),
                                        reads=[PB[po], accUB], writes=[accUB])
                    sch.add(DVE, (lambda e: e.reciprocal(out=accS, in_=accS)), reads=[accUB], writes=[accUB])
                    sch.add(DVE, (lambda e: e.tensor_tensor(out=yst[:], in0=accU, in1=accS, op=ALU.mult)),
                            reads=[accUB], writes=[ystB])
                    for qq in range(2048 // ACW):
                        sch.add(SP, (lambda e, c0=c0, qq=qq: e.dma_start(
                            out=att_own[(c0 + qq * ACW) // ACW, 0:128, :], in_=yst[:, qq * ACW:(qq + 1) * ACW])),
                            reads=[ystB], writes=[attoB[(c0 + qq * ACW) // ACW]], dsem=dy)
                sch.barrier()
            if stop_after == "p2bA":
                break

            with ExitStack() as es:
                kTB = sbt(es, "pb_k", [128, S], BF16)
                qTB = sbt(es, "pb_q", [128, S], BF16)
                VB = sbt(es, "pb_v", [128, NCH, 128], BF16)
                bm_sb = sbt(es, "pb_mask", [128, 4, 512], BF16)
                ez = [sbt(es, "pb_ez%d" % i, [128, 512], F32) for i in range(6)]
                sp = [sbt(es, "pb_sp%d" % i, [128, 512], BF16) for i in range(6)]
                wt = [sbt(es, "pb_w%d" % i, [128, 512], BF16) for i in range(3)]
                Lr = sbt(es, "pb_Lr", [128, 512], F32)
                Lrb = [sbt(es, "pb_Lrb%d" % i, [128, 512], BF16) for i in range(4)]
                ost = [sbt(es, "pb_o%d" % i, [128, 512], BF16) for i in range(2)]
                kvB, bmB = Buf(), Buf()
                ezB = [Buf() for _ in range(6)]
                spB = [Buf() for _ in range(6)]
                wtB = [Buf() for _ in range(3)]
                LrB = Buf()
                LrbB = [Buf() for _ in range(4)]
                ostB = [Buf(), Buf()]
                dkv = [sch.dsem("pb_kv%d" % i) for i in range(4)]
                do = [sch.dsem("pb_o%d" % i) for i in range(2)]
                sch.add(SP, lambda e: e.dma_start(out=kTB[:], in_=qk[7, :, :]), writes=[kvB], dsem=dkv[0])
                sch.add(SP, lambda e: e.dma_start(out=qTB[:], in_=qk[6, :, :]), writes=[kvB], dsem=dkv[1])
                sch.add(SP, lambda e: e.dma_start(
                    out=VB[:], in_=vbuf[:, 384:512].rearrange("(c p) d -> p c d", p=128)), writes=[kvB], dsem=dkv[2])
                sch.add(POOL, lambda e: e.dma_start(out=bm_sb[:], in_=bmask), writes=[bmB], dsem=dkv[3])
                zi = 0
                si = 0
                for gq in range(S // 512):
                    n0 = 4 * gq
                    q_ap = qTB[:, n0 * 128:n0 * 128 + 512]
                    steps = list(range(n0 + 3, -1, -1))
                    po = 6 + (gq % 2)
                    ob = gq % 2
                    stA = {}

                    def stageA1(c):
                        nonlocal zi
                        pz = zi % 6
                        e_ = zi % 6
                        zi += 1
                        mm(psum[pz][:, :], kTB[:, c * 128:(c + 1) * 128], q_ap, True, False, [kvB], PB[pz])
                        act(ez[e_][:], psum[pz][:, :], AF.Exp, reads=[PB[pz]], writes=[ezB[e_]], scale=QSCALE)
                        stA[c] = (pz, e_)

                    def stageA2(c):
                        nonlocal si
                        pz, e_ = stA[c]
                        s_ = si % 6
                        si += 1
                        act(sp[s_][:], ez[e_][:], AF.Ln, reads=[ezB[e_]], writes=[spB[s_]], bias=1.0)
                        if c >= n0:
                            j = c - n0
                            sch.add(DVE, (lambda e, s_=s_, j=j: e.tensor_tensor(
                                out=sp[s_][:], in0=sp[s_][:], in1=bm_sb[:, j, :], op=ALU.mult)),
                                reads=[spB[s_], bmB], writes=[spB[s_]])
                        stA[c] = (pz, s_)

                    stB2 = {}

                    def stageB2(c):
                        w_, firststep, laststep = stB2.pop(c)
                        mm(psum[po][:, :], VB[:, c, :], wt[w_][:], firststep, laststep, [kvB, wtB[w_]], PB[po])

                    def stageB(c, idx):
                        pz, s_ = stA.pop(c)
                        firststep = idx == 0
                        laststep = idx == len(steps) - 1
                        lb = idx % 4
                        mm(psum[pz][:, :], NTm, sp[s_][:], False, firststep, [spB[s_], Bconst, PB[pz]], PB[pz])
                        if not firststep:
                            mm(psum[pz][:, :], NONESm, Lrb[lb][:], False, True, [LrbB[lb], Bconst, PB[pz]], PB[pz])
                        w_ = idx % 3
                        act(wt[w_][:], psum[pz][:, :], AF.Exp, reads=[PB[pz]], writes=[wtB[w_]], scale=QSCALE2)
                        if c >= n0:
                            j = c - n0
                            sch.add(DVE, (lambda e, w_=w_, j=j: e.tensor_tensor(
                                out=wt[w_][:], in0=wt[w_][:], in1=bm_sb[:, j, :], op=ALU.mult)),
                                reads=[wtB[w_], bmB], writes=[wtB[w_]])
                        stB2[c] = (w_, firststep, laststep)
                        if not laststep:
                            if firststep:
                                sch.add(DVE, (lambda e, s_=s_: e.tensor_copy(out=Lr[:], in_=sp[s_][:])),
                                        reads=[spB[s_]], writes=[LrB])
                            else:
                                sch.add(DVE, (lambda e, s_=s_: e.tensor_tensor(
                                    out=Lr[:], in0=Lr[:], in1=sp[s_][:], op=ALU.add)),
                                    reads=[spB[s_], LrB], writes=[LrB])
                            nb_ = (idx + 1) % 4
                            sch.add(DVE, (lambda e, nb_=nb_: e.tensor_copy(out=Lrb[nb_][:], in_=Lr[:])),
                                    reads=[LrB], writes=[LrbB[nb_]])

                    GB_ = 3
                    batches = [steps[i:i + GB_] for i in range(0, len(steps), GB_)]
                    for c in batches[0]:
                        stageA1(c)
                    for c in batches[0]:
                        stageA2(c)
                    idx = 0
                    for bi_, batch in enumerate(batches):
                        if bi_ + 1 < len(batches):
                            for c in batches[bi_ + 1]:
                                stageA1(c)
                        for c in batch:
                            stageB(c, idx)
                            idx += 1
                        for c in batch:
                            stageB2(c)
                        if bi_ + 1 < len(batches):
                            for c in batches[bi_ + 1]:
                                stageA2(c)
                    if gq % 2 == 0:
                        act(ost[ob][:], psum[po][:, :], AF.Copy, reads=[PB[po]], writes=[ostB[ob]])
                    else:
                        sch.add(DVE, (lambda e, ob=ob, po=po: e.tensor_copy(out=ost[ob][:], in_=psum[po][:, :])),
                                reads=[PB[po]], writes=[ostB[ob]])
                    sch.add(SP, (lambda e, ob=ob, n0=n0: e.dma_start(
                        out=att_own[(n0 * 128) // ACW, 128:256, (n0 * 128) % ACW:(n0 * 128) % ACW + 512], in_=ost[ob][:])),
                        reads=[ostB[ob]], writes=[attoB[(n0 * 128) // ACW]], dsem=do[ob])
                    if ((n0 * 128) + 512) % ACW == 0:
                        ag_att((n0 * 128) // ACW)
                sch.barrier()
            if stop_after == "p2bB":
                break

            if stop_after == "ag_a":
                break

            last = (l == nlayers - 1)
            with ExitStack() as es:
                hT = sbt(es, "p3_h", [128, KD, W], F32)
                uT = sbt(es, "p3_u", [128, KD, W], BF16)
                cp = sbt(es, "p3_cp", [128, NFF, W], BF16)
                nb = sbt(es, "p3_nb", [128, KD, W], F32)
                pTt = sbt(es, "p3_p", [128, 2, W], BF16)
                wsl = [sbt(es, "p3_w%d" % i, [128, KD, 256], BF16) for i in range(4)]
                sq = [sbt(es, "p3_sq%d" % i, [128, W], BF16) for i in range(2)]
                rstd = sbt(es, "p3_rstd", [128, W], F32)
                tmp = [sbt(es, "p3_t%d" % i, [128, W + 2], F32) for i in range(6)]
                uph = sbt(es, "p3_uph", [128, 2 * NFF, 2], F32)
                hB = [Buf() for _ in range(KD)]
                uB = [Buf() for _ in range(KD)]
                cB = [Buf() for _ in range(NFF)]
                nB = [Buf() for _ in range(KD)]
                pB = Buf()
                wB = [Buf() for _ in range(4)]
                sqB = [Buf(), Buf()]
                rstdB = Buf()
                tB = [Buf() for _ in range(6)]
                uphB = [Buf() for _ in range(2 * NFF)]
                dW = [sch.dsem("p3w%d" % i) for i in range(4)]
                dH, dU, dA, dP, dO, dUo, dHH = [sch.dsem("p3%s" % n) for n in "h u a p o uo hh".split()]
                wst = {"i": 0, "t": 0}

                def slab(name, r0, nr, c0, ncol=256):
                    s = wst["i"] % 4
                    wst["i"] += 1
                    nk = nr // 128
                    if name == "wg":
                        name, c0 = "win", c0 + 6144
                    src = wbf[name][l, r0:r0 + nr, c0:c0 + ncol].rearrange("(k p) c -> p k c", p=128)
                    if not (_os.environ.get("K_NOSLAB") == "1" and wst["i"] > 4):
                        sch.add(SP, (lambda e, s=s, nk=nk, src=src, ncol=ncol: e.dma_start(
                            out=wsl[s][:, 0:nk, 0:ncol], in_=src)), reads=[wflB[name]], writes=[wB[s]], dsem=dW[s])
                    return s

                def next_t():
                    i = wst["t"] % 6
                    wst["t"] += 1
                    return i

                ne = (sq, sqB, rstd, rstdB)

                def p3_tile(t0, Wt, halo, l=l, last=last):
                    def ld_h(e):
                        if halo:
                            if l == 0:
                                return e.dma_start(out=hT[:, :, 0:2], in_=xh)
                            return e.dma_start(out=hT[:, :, 0:2], in_=hh_all.rearrange("(r p) c -> r p c", r=4)[
                                bass.ds(PID["im1"], 1), :, :].rearrange("o p (k t) -> p k (o t)", t=2))
                        src = xT if l == 0 else hbuf
                        return e.dma_start(out=hT[:], in_=src[:, :, t0:t0 + W])
                    sch.add(SP, ld_h, writes=hB, dsem=dH)

                    def ld_u(e):
                        if halo:
                            return e.dma_start(out=uT[:, :, 0:2], in_=uall[NQ - 1].rearrange("(r f) t -> r f t", r=4)[
                                bass.ds(PID["im1"], 1), :, 254:256].rearrange("o (k p) t -> p k (o t)", p=128))
                        return None
                    if halo:
                        sch.add(SP, ld_u, reads=[uallB[NQ - 1]], writes=uB, dsem=dU)
                    else:
                        for q in range(2):
                            sch.add(SP, (lambda e, q=q: e.dma_start(
                                out=uT[:, :, q * 256:(q + 1) * 256],
                                in_=ubuf_own[2 * (t0 // W) + q].rearrange("(k p) t -> p k t", p=128))),
                                reads=[ubufB[2 * (t0 // W) + q]], writes=uB, dsem=dU)

                    for a_ in range(2):
                        def ld_a(e, a_=a_):
                            if halo:
                                src = att_halo.rearrange("(s a p) t -> p a s t", s=4, a=2)[:, a_, :, :]
                            else:
                                src = att_mine[t0 // ACW, :, t0 % ACW:t0 % ACW + Wt].rearrange(
                                    "(s a p) t -> p a s t", s=4, a=2)[:, a_, :, :]
                            return e.dma_start(out=cp[:, a_ * 4:(a_ + 1) * 4, 0:Wt], in_=src)
                        sch.add(SP, ld_a, reads=[attmB], writes=cB[0:8], dsem=dA)
                    if not halo:
                        sch.add(POOL, (lambda e: e.dma_start(out=pTt[:], in_=pT[l, :, :, t0:t0 + W])),
                                writes=[pB], dsem=dP)

                    for co2 in range(8):
                        sGA = slab("wg", 0, D, co2 * 256)
                        sGB = slab("wg", 0, D, D + co2 * 256)
                        sAB = slab("wa", 0, 512, co2 * 256)
                        sBB = slab("wb", 0, 512, co2 * 256)
                        for cc in range(2):
                            co = co2 * 2 + cc
                            cs = slice(cc * 128, (cc + 1) * 128)
                            pa, pb_, pya, pyb = next_ps(), next_ps(), next_ps(), next_ps()
                            for k in range(KD):
                                mm(psum[pa][:, 0:Wt], wsl[sGA][:, k, cs], uT[:, k, 0:Wt], k == 0, k == KD - 1,
                                   [wB[sGA], uB[k]], PB[pa])
                            for k in range(KD):
                                mm(psum[pb_][:, 0:Wt], wsl[sGB][:, k, cs], uT[:, k, 0:Wt], k == 0, k == KD - 1,
                                   [wB[sGB], uB[k]], PB[pb_])
                            for s4 in range(4):
                                mm(psum[pya][:, 0:Wt], wsl[sAB][:, s4, cs], cp[:, s4, 0:Wt], s4 == 0, s4 == 3,
                                   [wB[sAB], cB[s4]], PB[pya])
                            for s4 in range(4):
                                mm(psum[pyb][:, 0:Wt], wsl[sBB][:, s4, cs], cp[:, 4 + s4, 0:Wt], s4 == 0, s4 == 3,
                                   [wB[sBB], cB[4 + s4]], PB[pyb])
                            ta, tb = next_t(), next_t()
                            act(tmp[ta][:, 0:Wt], psum[pa][:, 0:Wt], AF.Sigmoid, reads=[PB[pa]], writes=[tB[ta]])
                            act(tmp[tb][:, 0:Wt], psum[pb_][:, 0:Wt], AF.Sigmoid, reads=[PB[pb_]], writes=[tB[tb]])
                            sch.add(DVE, (lambda e, ta=ta, pya=pya: e.tensor_tensor(
                                out=tmp[ta][:, 0:Wt], in0=tmp[ta][:, 0:Wt], in1=psum[pya][:, 0:Wt], op=ALU.mult)),
                                reads=[tB[ta], PB[pya]], writes=[tB[ta]])
                            sch.add(DVE, (lambda e, tb=tb, pyb=pyb: e.tensor_tensor(
                                out=tmp[tb][:, 0:Wt], in0=tmp[tb][:, 0:Wt], in1=psum[pyb][:, 0:Wt], op=ALU.mult)),
                                reads=[tB[tb], PB[pyb]], writes=[tB[tb]])
                            sch.add(POOL, (lambda e, ta=ta, tb=tb, co=co: e.tensor_tensor(
                                out=cp[:, 8 + co, 0:Wt], in0=tmp[ta][:, 0:Wt], in1=tmp[tb][:, 0:Wt], op=ALU.add)),
                                reads=[tB[ta], tB[tb]], writes=[cB[8 + co]])
                    for co2 in range(8):
                        sO = slab("wo", 0, D, co2 * 256)
                        for cc in range(2):
                            co = co2 * 2 + cc
                            cs = slice(cc * 128, (cc + 1) * 128)
                            pi = next_ps()
                            for k in range(KD):
                                mm(psum[pi][:, 0:Wt], wsl[sO][:, k, cs], cp[:, 8 + k, 0:Wt], k == 0, k == KD - 1,
                                   [wB[sO], cB[8 + k]], PB[pi])
                            sch.add(DVE, (lambda e, co=co, pi=pi: e.tensor_copy(out=nb[:, co, 0:Wt], in_=psum[pi][:, 0:Wt])),
                                    reads=[PB[pi]], writes=[nB[co]])
                    r_ap, rB = norm_rstd(ne, [nb[:, k, 0:Wt] for k in range(KD)], nB, Wt)
                    for k in range(KD):
                        ti = next_t()
                        sch.add(DVE, (lambda e, k=k, ti=ti: e.scalar_tensor_tensor(
                            out=tmp[ti][:, 0:Wt], in0=nb[:, k, 0:Wt], scalar=gains_sb[:, l, 1, k:k + 1], in1=r_ap,
                            op0=ALU.mult, op1=ALU.mult)), reads=[nB[k], rB, Bconst], writes=[tB[ti]])
                        sch.add(POOL, (lambda e, k=k, ti=ti: e.tensor_tensor(
                            out=hT[:, k, 0:Wt], in0=hT[:, k, 0:Wt], in1=tmp[ti][:, 0:Wt], op=ALU.add)),
                            reads=[hB[k], tB[ti]], writes=[hB[k]])
                    if "hmid" in dbg and not halo and t0 == 0 and l == 0:
                        tdb = dbg_tensor("hmid", [128, KD, W])
                        sch.add(SP, (lambda e: e.dma_start(out=tdb, in_=hT[:])), reads=hB, dsem=sch.dsem("dbg2"))
                    if "mT" in dbg and not halo and t0 == 0 and l == 0:
                        tdb2 = dbg_tensor("mT", [128, KD, W], BF16)
                        sch.add(SP, (lambda e: e.dma_start(out=tdb2, in_=cp[:, 8:24, :])), reads=cB[8:24], dsem=sch.dsem("dbg3"))
                    r_ap, rB = norm_rstd(ne, [hT[:, k, 0:Wt] for k in range(KD)], hB, Wt)
                    for k in range(KD):
                        sch.add(DVE, (lambda e, k=k: e.scalar_tensor_tensor(
                            out=uT[:, k, 0:Wt], in0=hT[:, k, 0:Wt], scalar=gains_sb[:, l, 2, k:k + 1], in1=r_ap,
                            op0=ALU.mult, op1=ALU.mult)), reads=[hB[k], rB, Bconst], writes=[uB[k]])
                    for c2 in range(NFF // 2):
                        sUG = slab("wup", 0, D, c2 * 256)
                        sUV = slab("wup", 0, D, DFF + c2 * 256)
                        for cc in range(2):
                            c = c2 * 2 + cc
                            cs = slice(cc * 128, (cc + 1) * 128)
                            ys = []
                            for (sw, ch) in ((sUG, c), (sUV, NFF + c)):
                                pi = next_ps()
                                for k in range(KD):
                                    mm(psum[pi][:, 0:Wt], wsl[sw][:, k, cs], uT[:, k, 0:Wt], k == 0, k == KD - 1,
                                       [wB[sw], uB[k]], PB[pi])
                                if halo:
                                    sch.add(DVE, (lambda e, ch=ch, pi=pi: e.tensor_scalar(
                                        out=uph[:, ch, :], in0=psum[pi][:, 0:2], scalar1=hflag_sb[:, 0:1], scalar2=None,
                                        op0=ALU.mult)), reads=[PB[pi], Bconst], writes=[uphB[ch]])
                                    continue
                                xi, yi = next_t(), next_t()
                                sch.add(POOL, (lambda e, xi=xi, ch=ch: e.tensor_copy(out=tmp[xi][:, 0:2], in_=uph[:, ch, :])),
                                        reads=[uphB[ch]], writes=[tB[xi]])
                                act(tmp[xi][:, 2:2 + W], psum[pi][:, :], AF.Copy, reads=[PB[pi]], writes=[tB[xi]])
                                sch.add(POOL, (lambda e, xi=xi, ch=ch: e.tensor_copy(out=uph[:, ch, :], in_=tmp[xi][:, W:W + 2])),
                                        reads=[tB[xi]], writes=[uphB[ch]])
                                sch.add(DVE, (lambda e, xi=xi, yi=yi, ch=ch: e.tensor_scalar(
                                    out=tmp[yi][:, 0:W], in0=tmp[xi][:, 2:2 + W], scalar1=convw_sb[:, l, 2, ch:ch + 1],
                                    scalar2=convw_sb[:, l, 3, ch:ch + 1], op0=ALU.mult, op1=ALU.add)),
                                    reads=[tB[xi], Bconst], writes=[tB[yi]])
                                sch.add(DVE, (lambda e, xi=xi, yi=yi, ch=ch: e.scalar_tensor_tensor(
                                    out=tmp[yi][:, 0:W], in0=tmp[xi][:, 1:1 + W], scalar=convw_sb[:, l, 1, ch:ch + 1],
                                    in1=tmp[yi][:, 0:W], op0=ALU.mult, op1=ALU.add)),
                                    reads=[tB[xi], tB[yi], Bconst], writes=[tB[yi]])
                                sch.add(DVE, (lambda e, xi=xi, yi=yi, ch=ch: e.scalar_tensor_tensor(
                                    out=tmp[yi][:, 0:W], in0=tmp[xi][:, 0:W], scalar=convw_sb[:, l, 0, ch:ch + 1],
                                    in1=tmp[yi][:, 0:W], op0=ALU.mult, op1=ALU.add)),
                                    reads=[tB[xi], tB[yi], Bconst], writes=[tB[yi]])
                                ys.append(yi)
                            if halo:
                                continue
                            yg, yv = ys
                            act(tmp[yg][:, 0:W], tmp[yg][:, 0:W], AF.Gelu_apprx_tanh, reads=[tB[yg]], writes=[tB[yg]])
                            sch.add(POOL, (lambda e, yg=yg, yv=yv, c=c: e.tensor_tensor(
                                out=cp[:, c, :], in0=tmp[yg][:, 0:W], in1=tmp[yv][:, 0:W], op=ALU.mult)),
                                reads=[tB[yg], tB[yv]], writes=[cB[c]])
                    if halo:
                        return
                    for co2 in range(8):
                        pis = [next_ps(), next_ps()]
                        kparts = [(0, 16), (16, 16), (32, 12)]
                        for (k0, nk) in kparts:
                            sD = slab("wd", k0 * 128, nk * 128, co2 * 256)
                            for cc in range(2):
                                cs = slice(cc * 128, (cc + 1) * 128)
                                for kk in range(nk):
                                    k = k0 + kk
                                    mm(psum[pis[cc]][:, :], wsl[sD][:, kk, cs], cp[:, k, :], k == 0, k == NFF - 1,
                                       [wB[sD], cB[k]], PB[pis[cc]])
                        for cc in range(2):
                            co = co2 * 2 + cc
                            sch.add(DVE, (lambda e, co=co, pi=pis[cc]: e.tensor_copy(out=nb[:, co, :], in_=psum[pi][:, :])),
                                    reads=[PB[pis[cc]]], writes=[nB[co]])
                    r_ap, rB = norm_rstd(ne, [nb[:, k, :] for k in range(KD)], nB, W)
                    for k in range(KD):
                        ti = next_t()
                        sch.add(DVE, (lambda e, k=k, ti=ti: e.scalar_tensor_tensor(
                            out=tmp[ti][:, 0:W], in0=nb[:, k, :], scalar=gains_sb[:, l, 3, k:k + 1], in1=r_ap,
                            op0=ALU.mult, op1=ALU.mult)), reads=[nB[k], rB, Bconst], writes=[tB[ti]])
                        sch.add(POOL, (lambda e, k=k, ti=ti: e.tensor_tensor(
                            out=hT[:, k, :], in0=hT[:, k, :], in1=tmp[ti][:, 0:W], op=ALU.add)),
                            reads=[hB[k], tB[ti]], writes=[hB[k]])
                        sch.add(DVE, (lambda e, k=k: e.tensor_copy(out=uT[:, k, :], in_=hT[:, k, :])),
                                reads=[hB[k]], writes=[uB[k]])
                    if "hffn" in dbg and t0 == 0 and l == 0:
                        tdb3 = dbg_tensor("hffn", [128, KD, W])
                        sch.add(SP, (lambda e: e.dma_start(out=tdb3, in_=hT[:])), reads=hB, dsem=sch.dsem("dbg4"))
                    for co2 in range(8):
                        sG = slab("wpg", 0, D, co2 * 256)
                        sI = slab("wpi", 0, PLE, co2 * 256)
                        for cc in range(2):
                            co = co2 * 2 + cc
                            cs = slice(cc * 128, (cc + 1) * 128)
                            pg, pe = next_ps(), next_ps()
                            for k in range(KD):
                                mm(psum[pg][:, :], wsl[sG][:, k, cs], uT[:, k, :], k == 0, k == KD - 1,
                                   [wB[sG], uB[k]], PB[pg])
                            for k in range(2):
                                mm(psum[pe][:, :], wsl[sI][:, k, cs], pTt[:, k, :], k == 0, k == 1,
                                   [wB[sI], pB], PB[pe])
                            ti = next_t()
                            act(tmp[ti][:, 0:W], psum[pg][:, :], AF.Sigmoid, reads=[PB[pg]], writes=[tB[ti]])
                            sch.add(DVE, (lambda e, ti=ti, pe=pe: e.tensor_tensor(
                                out=tmp[ti][:, 0:W], in0=tmp[ti][:, 0:W], in1=psum[pe][:, :], op=ALU.mult)),
                                reads=[tB[ti], PB[pe]], writes=[tB[ti]])
                            sch.add(POOL, (lambda e, co=co, ti=ti: e.tensor_tensor(
                                out=hT[:, co, :], in0=hT[:, co, :], in1=tmp[ti][:, 0:W], op=ALU.add)),
                                reads=[hB[co], tB[ti]], writes=[hB[co]])
                    dst = outT if last else hbuf
                    sch.add(SP, (lambda e: e.dma_start(out=dst[:, :, t0:t0 + W], in_=hT[:])), reads=hB, dsem=dO)
                    if not last:
                        if t0 + W == TC:
                            sch.add(SP, (lambda e: e.dma_start(
                                out=hh_own.rearrange("p (k t) -> p k t", t=2), in_=hT[:, :, W - 2:W])),
                                reads=hB, dsem=dHH)
                        r_ap2, rB2 = norm_rstd(ne, [hT[:, k, :] for k in range(KD)], hB, W)
                        for k in range(KD):
                            sch.add(DVE, (lambda e, k=k: e.scalar_tensor_tensor(
                                out=uT[:, k, :], in0=hT[:, k, :], scalar=gains_sb[:, l + 1, 0, k:k + 1], in1=r_ap2,
                                op0=ALU.mult, op1=ALU.mult)), reads=[hB[k], rB2, Bconst], writes=[uB[k]])
                        for q in range(2):
                            sch.add(SP, (lambda e, q=q: e.dma_start(
                                out=ubuf_own[2 * (t0 // W) + q].rearrange("(k p) t -> p k t", p=128),
                                in_=uT[:, :, q * 256:(q + 1) * 256])),
                                reads=uB, writes=[ubufB[2 * (t0 // W) + q]], dsem=dUo)
                        for q in range(2):
                            ag_u(2 * (t0 // W) + q)

                attmB = Buf()
                dam = sch.dsem("p3am")
                sch.add(SP, (lambda e: e.dma_start(
                    out=att_mine.rearrange("c r t -> (c r) t"),
                    in_=att_all.rearrange("(i c) r t -> i (c r) t", i=4)[bass.ds(PID["i"], 1), :, :].rearrange(
                        "o x t -> (o x) t"))), reads=attaB, writes=[attmB], dsem=dam)
                sch.add(SP, (lambda e: e.dma_start(
                    out=att_halo,
                    in_=att_all.rearrange("(i c) r t -> i c r t", i=4)[
                        bass.ds(PID["im1"], 1), TC // ACW - 1, :, ACW - 2:ACW].rearrange("o r t -> (o r) t"))),
                    reads=attaB, writes=[attmB], dsem=dam)
                p3_tile(0, 2, True)
                for tt in range(NT):
                    p3_tile(tt * W, W, False)
                sch.barrier()
            if not last:
                dcc3 = sch.dsem("cc_h", step=1)
                sch.add(POOL, (lambda e: e.collective_compute(
                    "AllGather", ALU.bypass, replica_groups=GROUPS, ins=[hh_own], outs=[hh_all])), dsem=dcc3)
                sch.barrier()

        ddbg = sch.dsem("dbg")
        srcs = {"uall": (uall, [NQ, 4 * D, 256], BF16), "qk": (qk, [8, 128, S], BF16), "vbuf": (vbuf, [S, 512], BF16),
                "att_own": (att_own, [NAC, 256, ACW], BF16), "att_all": (att_all, [NAC, 1024, ACW], BF16),
                "hbuf": (hbuf, [128, KD, TC], F32), "ubuf_own": (ubuf_own, [NQ, D, 256], BF16)}
        for name in dbg:
            if name not in srcs:
                continue
            src, shape, dt = srcs[name]
            t = dbg_tensor(name, shape, dt)
            sch.add(SP, (lambda e, t=t, src=src: e.dma_start(out=t, in_=src)), dsem=ddbg)
        sch.final_wait(SP)
        stats = sch.emit(nc)
    return nc, stats, list(dbg_out.keys())


def make_inputs(S, nl, x, p, g_mix_pre, w_in, w_branch_a, w_branch_b, w_out, g_mix_post, g_ffn_pre, w_up,
                conv_w, conv_b, w_down, g_ffn_post, w_ple_in, w_ple_gate):
    TC = S // 4
    f32 = np.float32
    WA, WB_ = 1536, 512
    gains = np.stack([g_mix_pre[:nl], g_mix_post[:nl], g_ffn_pre[:nl], g_ffn_post[:nl]], axis=1)
    gains = np.ascontiguousarray(gains.reshape(nl, 4, KD, 128).transpose(3, 0, 1, 2)).astype(f32)
    cw = np.concatenate([conv_w[:nl], conv_b[:nl, None, :]], axis=1)
    cw = np.ascontiguousarray(cw.reshape(nl, 4, 2 * NFF, 128).transpose(3, 0, 1, 2)).astype(f32)
    wfull = {"win": w_in[:nl], "wa": w_branch_a[:nl], "wb": w_branch_b[:nl], "wo": w_out[:nl], "wup": w_up[:nl],
             "wd": w_down[:nl], "wpi": w_ple_in[:nl], "wpg": w_ple_gate[:nl]}
    wshards = [dict() for _ in range(4)]
    for n, r, c in WSPECS:
        flat = np.ascontiguousarray(wfull[n]).reshape(-1).astype(f32, copy=False)
        pieces = [[] for _ in range(4)]
        for (st, sz) in wchunks(flat.size):
            q = sz // 4
            for i in range(4):
                pieces[i].append(flat[st + i * q: st + (i + 1) * q])
        for i in range(4):
            wshards[i][n + "_sh"] = np.concatenate(pieces[i]).reshape(-1, 2048)
    common = {"gains": gains, "convw": cw}
    idx = np.arange(128)
    kk, kq = idx[:, None], idx[None, :]
    cbf = np.zeros((128, 3, 128), f32)
    cbf[:, 0, :] = np.where(kk >= kq, -NEGC, 0.0)
    cbf[:, 1, :] = -NEGC
    cbf[:, 2, :] = 1.0
    common["cbf"] = cbf
    bm = np.zeros((128, 4, 512), f32)
    for j in range(4):
        for b in range(4):
            if b > j:
                bm[:, j, b * 128:(b + 1) * 128] = 1.0
            elif b == j:
                bm[:, j, b * 128:(b + 1) * 128] = (kk < kq).astype(f32)
    common["bmask"] = bm
    DIL = (1, 4, 16)
    in_maps = []
    for c in range(8):
        b, i = divmod(c, 4)
        j = i
        m = dict(common)
        xs = x[b, i * TC:(i + 1) * TC, :]
        m["xT"] = np.ascontiguousarray(xs.reshape(TC, KD, 128).transpose(2, 1, 0)).astype(f32)
        if i == 0:
            m["xh"] = np.zeros((128, KD, 2), f32)
        else:
            m["xh"] = np.ascontiguousarray(x[b, i * TC - 2:i * TC, :].reshape(2, KD, 128).transpose(2, 1, 0)).astype(f32)
        ps_ = p[:nl, b, i * TC:(i + 1) * TC, :]
        m["pT"] = np.ascontiguousarray(ps_.reshape(nl, TC, 2, 128).transpose(0, 3, 2, 1)).astype(f32)
        m.update(wshards[i])
        m["hflag"] = np.full((128, 1), 0.0 if i == 0 else 1.0, f32)
        ab = np.zeros((128, 3, 256), f32)
        for g in range(3):
            h = g * 4 + j
            slope = 2.0 ** (-8.0 * (h + 1) / 12.0)
            d = DIL[g]
            st_cur = (kq - kk).astype(np.float64)
            cur = np.where(st_cur >= 0, -slope * st_cur * d, MASKV)
            st_prev = (128 + kq - kk).astype(np.float64)
            prev = np.where(st_prev <= 128, -slope * st_prev * d, MASKV)
            ab[:, g, 0:128] = cur
            ab[:, g, 128:256] = prev
        m["abias"] = ab
        in_maps.append(m)
    return in_maps


_CACHE = {}


def kernel(**inputs):
    inputs = {k: np.asarray(v) for k, v in inputs.items()}
    x = inputs["x"]
    B, S, _ = x.shape
    nl = inputs["w_in"].shape[0]
    key = (S, nl)
    if key not in _CACHE:
        import os as _os2
        _CACHE[key] = build(S, nl, stop_after=_os2.environ.get("K_STOP"))[0]
    nc = _CACHE[key]
    in_maps = make_inputs(S, nl, **inputs)
    res = run_bass_kernel_spmd(nc, in_maps, core_ids=list(range(8)))
    TC = S // 4
    out = np.empty((B, S, D), np.float32)
    for c in range(8):
        b, i = divmod(c, 4)
        o = np.asarray(res.results[c]["outT"])
        out[b, i * TC:(i + 1) * TC, :] = o.transpose(2, 1, 0).reshape(TC, D)
    return out
```

```python
from contextlib import ExitStack
import math
import numpy as np
import ml_dtypes
import concourse.bass as bass
import concourse.mybir as mybir
from concourse.bass_utils import run_bass_kernel_spmd

F32 = mybir.dt.float32
BF16 = mybir.dt.bfloat16
AF = mybir.ActivationFunctionType
ALU = mybir.AluOpType

PE, ACT, DVE, POOL, SP = "tensor", "scalar", "vector", "gpsimd", "sync"
ENGS = (PE, ACT, DVE, POOL, SP)

D = 2048
KD = 16
DFF = 5632
NFF = 44
PLE = 256
L_ = 2
W = 512
GROUPS = [[0, 1, 2, 3], [4, 5, 6, 7]]
RMS_EPS = 1e-6
QSCALE = 1.0 / math.sqrt(128.0)
NEGC = 11.3125
QSCALE2 = 1.0 / NEGC
MASKV = -30000.0


class Buf:
    __slots__ = ("name", "w", "r", "rd")

    def __init__(self, name=""):
        self.name = name
        self.w = None
        self.r = {}
        self.rd = []


class DSem:
    __slots__ = ("name", "cnt", "h", "step", "async_")

    def __init__(self, name, step=16, async_=False):
        self.name = name
        self.cnt = 0
        self.h = None
        self.step = step
        self.async_ = async_


class Op:
    __slots__ = ("eng", "fn", "deps", "sig", "sidx", "dsem", "dval", "gid")

    def __init__(self, eng, fn, gid):
        self.eng = eng
        self.fn = fn
        self.deps = []
        self.sig = False
        self.sidx = 0
        self.dsem = None
        self.dval = 0
        self.gid = gid


PID = {}


class Sched:
    def __init__(self):
        self.ops = {e: [] for e in ENGS}
        self.gid = 0
        self.dsems = []
        self.bar = []
        self.last = {}
        self.lastdma = {}
        self.dsem_by_name = {}

    def dsem(self, name, step=16, async_=False):
        if name in self.dsem_by_name:
            return self.dsem_by_name[name]
        d = DSem(name, step, async_)
        self.dsems.append(d)
        self.dsem_by_name[name] = d
        return d

    def add(self, eng, fn, reads=(), writes=(), dsem=None, nobar=False):
        self.gid += 1
        op = Op(eng, fn, self.gid)
        is_dma = dsem is not None
        deps = {}

        def adddep(d):
            if d is None:
                return
            if d.dsem is not None:
                deps[("d", d.gid)] = d
            else:
                if d.eng == PE and eng == PE and not is_dma:
                    return
                k = d.eng
                if k not in deps or deps[k].gid < d.gid:
                    deps[k] = d

        for b in reads:
            adddep(b.w)
        for b in writes:
            if not (is_dma and b.w is not None and b.w.dsem is dsem):
                adddep(b.w)
            for d in b.r.values():
                adddep(d)
            for d in b.rd:
                adddep(d)
        if not nobar:
            for d in self.bar:
                adddep(d)
        op.deps = list(deps.values())
        for d in op.deps:
            if d.dsem is None:
                d.sig = True
        for b in reads:
            if is_dma:
                b.rd.append(op)
            else:
                b.r[eng] = op
        for b in writes:
            b.w = op
            b.r = {}
            b.rd = []
        if is_dma:
            dsem.cnt += dsem.step
            op.dsem = dsem
            op.dval = dsem.cnt
            self.lastdma[id(dsem)] = op
        else:
            self.last[eng] = op
        self.ops[eng].append(op)
        return op

    def barrier(self, full=False):
        deps = list(self.last.values()) + [o for o in self.lastdma.values() if full or not o.dsem.async_]
        for d in deps:
            if d.dsem is None:
                d.sig = True
        self.bar = deps

    def final_wait(self, eng=SP):
        self.barrier(full=True)
        self.add(eng, None)

    def emit(self, nc):
        for e in ENGS:
            c = 0
            for op in self.ops[e]:
                if op.dsem is None and op.sig:
                    c += 1
                    op.sidx = c
        stats = {}
        with ExitStack() as es:
            esem = {e: es.enter_context(nc.semaphore("s_" + e)) for e in ENGS}
            for d in self.dsems:
                d.h = es.enter_context(nc.semaphore("d_" + d.name))
            block = es.enter_context(nc.Block())

            def run(e_name):
                def body(e):
                    waited = {}
                    nw = 0
                    if e_name == SP:
                        PID["sp"] = e.partition_id()
                        PID["i"] = PID["sp"] % 4
                        PID["im1"] = (PID["sp"] + 3) % 4
                    for op in self.ops[e_name]:
                        for d in op.deps:
                            if d.dsem is not None:
                                key, h, val = id(d.dsem), d.dsem.h, d.dval
                            else:
                                key, h, val = d.eng, esem[d.eng], d.sidx
                            if waited.get(key, 0) < val:
                                e.wait_ge(h, val)
                                waited[key] = val
                                nw += 1
                        if op.fn is None:
                            continue
                        ins = op.fn(e)
                        if op.dsem is not None:
                            ins.then_inc(op.dsem.h, op.dsem.step)
                        elif op.sig:
                            ins.then_inc(esem[e_name], 1)
                    stats[e_name] = (len(self.ops[e_name]), nw)
                return body

            block.tensor(run(PE))
            block.scalar(run(ACT))
            block.vector(run(DVE))
            block.gpsimd(run(POOL))
            block.sync(run(SP))
        return stats


WSPECS = [
    ("win", D, 10240), ("wa", 512, D), ("wb", 512, D), ("wo", D, D),
    ("wup", D, 2 * DFF), ("wd", DFF, D), ("wpi", PLE, D), ("wpg", D, D),
]
CE = 2 * 1024 * 1024


def wchunks(n_total):
    out = []
    st = 0
    while st < n_total:
        sz = min(CE, n_total - st)
        assert sz % 8192 == 0
        out.append((st, sz))
        st += sz
    return out


def build(S, nlayers=L_, stop_after=None, dbg=()):
    TC = S // 4
    NT = TC // W
    NSB = S // 2048
    NCH = S // 128
    nc = bass.Bass("TRN2", target_bir_lowering=False)
    sch = Sched()
    ein = lambda n, sh, dt=F32: nc.dram_tensor(n, list(sh), dt, kind="ExternalInput").ap()

    xT = ein("xT", [128, KD, TC])
    xh = ein("xh", [128, KD, 2])
    pT = ein("pT", [nlayers, 128, 2, TC])
    wsh = {n: ein(n + "_sh", [nlayers * r * c // 4 // 2048, 2048]) for n, r, c in WSPECS}
    gains = ein("gains", [128, nlayers, 4, KD])
    convw = ein("convw", [128, nlayers, 4, 2 * NFF])
    hflag = ein("hflag", [128, 1])
    cbf = ein("cbf", [128, 3, 128])
    abias = ein("abias", [128, 3, 256])
    bmask = ein("bmask", [128, 4, 512])
    outT = nc.dram_tensor("outT", [128, KD, TC], F32, kind="ExternalOutput").ap()
    dbg_out = {}
    dt_ = lambda n, sh, dt: nc.dram_tensor(n, list(sh), dt).ap()
    wshbf = {n: dt_(n + "_shbf", [nlayers * r * c // 4 // 2048, 2048], BF16) for n, r, c in WSPECS}
    wflat = {n: dt_(n + "_bf", [nlayers * r * c // 2048, 2048], BF16) for n, r, c in WSPECS}
    wbf = {n: wflat[n].rearrange("a b -> (a b)").rearrange("(l r c) -> l r c", l=nlayers, r=r, c=c)
           for n, r, c in WSPECS}
    NQ = TC // 256
    ubuf_own = dt_("ubuf_own", [NQ, D, 256], BF16)
    uall = dt_("uall", [NQ, 4 * D, 256], BF16)
    qk = dt_("qk", [8, 128, S], BF16)
    vbuf = dt_("vbuf", [S, 512], BF16)
    ACW = min(2048, TC)
    NAC = S // ACW
    att_own = dt_("att_own", [NAC, 256, ACW], BF16)
    att_all = dt_("att_all", [NAC, 4 * 256, ACW], BF16)
    wq_mine = dt_("wq_mine", [D, 12, 128], BF16)
    att_mine = dt_("att_mine", [TC // ACW, 1024, ACW], BF16)
    att_halo = dt_("att_halo", [1024, 2], BF16)
    hbuf = dt_("hbuf", [128, KD, TC], F32)
    hh_own = dt_("hh_own", [128, KD * 2], F32)
    hh_all = dt_("hh_all", [4 * 128, KD * 2], F32)

    wshB = {n: [Buf(), Buf(), Buf()] for n, _, _ in WSPECS}
    wflB = {n: Buf() for n, _, _ in WSPECS}
    ubufB = [Buf() for _ in range(NQ)]
    uallB = [Buf() for _ in range(NQ)]
    attoB = [Buf() for _ in range(NAC)]
    attaB = [Buf() for _ in range(NAC)]

    def ag_u(q):
        sch.add(POOL, (lambda e, q=q: e.collective_compute(
            "AllGather", ALU.bypass, replica_groups=GROUPS, ins=[ubuf_own[q]], outs=[uall[q]])),
            reads=[ubufB[q]], writes=[uallB[q]], dsem=sch.dsem("cc_u%d" % (q % 8), step=1, async_=True), nobar=True)

    def ag_att(q):
        sch.add(POOL, (lambda e, q=q: e.collective_compute(
            "AllGather", ALU.bypass, replica_groups=GROUPS, ins=[att_own[q]], outs=[att_all[q]])),
            reads=[attoB[q]], writes=[attaB[q]], dsem=sch.dsem("cc_a%d" % (q % 8), step=1, async_=True), nobar=True)

    def dbg_tensor(name, shape, dt=F32):
        t = nc.dram_tensor("dbg_" + name, list(shape), dt, kind="ExternalOutput").ap()
        dbg_out[name] = t
        return t

    with ExitStack() as top:
        _uid = [0]

        def sbt(es, n, sh, dt):
            _uid[0] += 1
            return es.enter_context(nc.sbuf_tensor("%s_%d" % (n, _uid[0]), list(sh), dt))
        gains_sb = sbt(top, "gains_sb", [128, nlayers, 4, KD], F32)
        convw_sb = sbt(top, "convw_sb", [128, nlayers, 4, 2 * NFF], F32)
        hflag_sb = sbt(top, "hflag_sb", [128, 1], F32)
        cbf_sb = sbt(top, "cbf_sb", [128, 3, 128], BF16)
        psum = [top.enter_context(nc.psum_tensor("ps%d" % i, [128, 512], F32)) for i in range(8)]
        PB = [Buf("ps%d" % i) for i in range(8)]
        Bconst = Buf("const")
        dconst = [sch.dsem("const%d" % i) for i in range(4)]
        sch.add(SP, lambda e: e.dma_start(out=gains_sb[:], in_=gains), writes=[Bconst], dsem=dconst[0])
        sch.add(SP, lambda e: e.dma_start(out=convw_sb[:], in_=convw), writes=[Bconst], dsem=dconst[1])
        sch.add(SP, lambda e: e.dma_start(out=hflag_sb[:], in_=hflag), writes=[Bconst], dsem=dconst[2])
        sch.add(POOL, lambda e: e.dma_start(out=cbf_sb[:], in_=cbf), writes=[Bconst], dsem=dconst[3])
        NTm = cbf_sb[:, 0, :]
        NONESm = cbf_sb[:, 1, :]
        ONESm = cbf_sb[:, 2, :]

        import os as _os
        _skip_cast = _os.environ.get("K_SKIP_CAST") == "1"
        with ExitStack() as es:
            NCB = 3
            cst = [sbt(es, "cast%d" % i, [128, 4, 2048], BF16) for i in range(NCB)]
            cstB = [Buf() for _ in range(NCB)]
            dci = [sch.dsem("casti%d" % i) for i in range(NCB)]
            dco = [sch.dsem("casto%d" % i) for i in range(NCB)]
            ci = 0
            for n, r, c in ([] if _skip_cast else WSPECS):
                rows = nlayers * r * c // 4 // 2048
                src = wsh[n]
                dst = wshbf[n]
                for r0 in range(0, rows, 512):
                    nr = min(512, rows - r0)
                    pc = 128 if nr % 128 == 0 else nr
                    assert pc <= 128
                    q = nr // pc
                    s_ = ci % NCB
                    ci += 1
                    sap = src[r0:r0 + nr, :].rearrange("(q p) b -> p q b", p=pc)
                    dap = dst[r0:r0 + nr, :].rearrange("(q p) b -> p q b", p=pc)
                    sch.add(POOL, (lambda e, s_=s_, q=q, sap=sap, pc=pc: e.dma_start(out=cst[s_][0:pc, 0:q, :], in_=sap)),
                            writes=[cstB[s_]], dsem=dci[s_])
                    sch.add(SP, (lambda e, s_=s_, q=q, dap=dap, pc=pc: e.dma_start(out=dap, in_=cst[s_][0:pc, 0:q, :])),
                            reads=[cstB[s_]], writes=[wshB[n][s_]], dsem=dco[s_])
        sch.barrier()

        def gather_weight(n, r, c):
            dccw = sch.dsem("cc_w_" + n, step=1, async_=True)
            for (st, sz) in wchunks(nlayers * r * c):
                a0, na = st // 4 // 2048, sz // 4 // 2048
                b0, nb_ = st // 2048, sz // 2048
                sch.add(POOL, (lambda e, n=n, a0=a0, na=na, b0=b0, nb_=nb_: e.collective_compute(
                    "AllGather", ALU.bypass, replica_groups=GROUPS, ins=[wshbf[n][a0:a0 + na, :]],
                    outs=[wflat[n][b0:b0 + nb_, :]])), reads=wshB[n], writes=[wflB[n]], dsem=dccw, nobar=True)

        if not _skip_cast:
            gather_weight(*WSPECS[0])

        ring = {"ps": 0}

        def next_ps(n=7):
            i = ring["ps"] % n
            ring["ps"] += 1
            return i

        def mm(out, lhsT, rhs, start, stop, reads, pb):
            sch.add(PE, (lambda e: e.matmul(out, lhsT, rhs, start=start, stop=stop)), reads=reads, writes=[pb])

        def act(out, in_, func, reads, writes, **kw):
            sch.add(ACT, (lambda e: e.activation(out=out, in_=in_, func=func, **kw)), reads=reads, writes=writes)

        def norm_rstd(es_bufs, src_aps, src_bufs, Wt):
            sq, sqB, rstd, rstdB = es_bufs
            n = len(src_aps)
            for k in range(n):
                s = k % len(sq)
                act(sq[s][:, 0:Wt], src_aps[k], AF.Square, reads=[src_bufs[k]], writes=[sqB[s]])
                mm(psum[7][:, 0:Wt], ONESm, sq[s][:, 0:Wt], k == 0, k == n - 1, [sqB[s], Bconst], PB[7])
            act(rstd[:, 0:Wt], psum[7][:, 0:Wt], AF.Sqrt, reads=[PB[7]], writes=[rstdB], scale=1.0 / D, bias=RMS_EPS)
            sch.add(DVE, (lambda e: e.reciprocal(out=rstd[:, 0:Wt], in_=rstd[:, 0:Wt])), reads=[rstdB], writes=[rstdB])
            return rstd[:, 0:Wt], rstdB

        with ExitStack() as es:
            hT = sbt(es, "p0_h", [128, KD, W], F32)
            uT = sbt(es, "p0_u", [128, KD, W], BF16)
            sq = [sbt(es, "p0_sq%d" % i, [128, W], BF16) for i in range(2)]
            rstd = sbt(es, "p0_rstd", [128, W], F32)
            hB = [Buf() for _ in range(KD)]
            uB = [Buf() for _ in range(KD)]
            sqB = [Buf(), Buf()]
            rstdB = Buf()
            dh, du = sch.dsem("p0h"), sch.dsem("p0u")
            for tt in range(0 if stop_after == "cast" else NT):
                t0 = tt * W
                sch.add(SP, (lambda e, t0=t0: e.dma_start(out=hT[:], in_=xT[:, :, t0:t0 + W])), writes=hB, dsem=dh)
                r_ap, rB = norm_rstd((sq, sqB, rstd, rstdB), [hT[:, k, :] for k in range(KD)], hB, W)
                for k in range(KD):
                    sch.add(DVE, (lambda e, k=k: e.scalar_tensor_tensor(
                        out=uT[:, k, :], in0=hT[:, k, :], scalar=gains_sb[:, 0, 0, k:k + 1], in1=r_ap,
                        op0=ALU.mult, op1=ALU.mult)), reads=[hB[k], rB, Bconst], writes=[uB[k]])
                for q in range(2):
                    sch.add(SP, (lambda e, tt=tt, q=q: e.dma_start(
                        out=ubuf_own[2 * tt + q].rearrange("(k p) t -> p k t", p=128),
                        in_=uT[:, :, q * 256:(q + 1) * 256])),
                        reads=uB, writes=[ubufB[2 * tt + q]], dsem=du)
                for q in range(2):
                    ag_u(2 * tt + q)
            sch.barrier()
        for spec in WSPECS[1:]:
            if not _skip_cast:
                gather_weight(*spec)

        for l in range(nlayers):
            if stop_after in ("cast", "p0"):
                break
            if stop_after == "ag_u":
                break

            with ExitStack() as es:
                wq = sbt(es, "p2a_w", [128, KD, 1536], BF16)
                uT2 = [sbt(es, "p2a_u%d" % i, [128, KD, W], BF16) for i in range(2)]
                stq = [sbt(es, "p2a_sq%d" % i, [128, W], BF16) for i in range(4)]
                stv = [sbt(es, "p2a_sv%d" % i, [128, 512], BF16) for i in range(4)]
                wqB = Buf()
                uB2 = [Buf(), Buf()]
                stqB = [Buf() for _ in range(4)]
                stvB = [Buf() for _ in range(4)]
                dw = sch.dsem("p2aw")
                du2 = [sch.dsem("p2au%d" % i) for i in range(2)]
                dsq = [sch.dsem("p2asq%d" % i) for i in range(4)]
                dsv = [sch.dsem("p2asv%d" % i) for i in range(4)]
                wqmB = Buf()
                dwm = sch.dsem("p2awm")
                sch.add(SP, (lambda e, l=l: e.dma_start(
                    out=wq_mine, in_=wbf["win"][l].rearrange("r (G j c) -> r G j c", j=4, c=128)[
                        :, 0:12, bass.ds(PID["i"], 1), :].rearrange("r G o c -> r G (o c)"))),
                    reads=[wflB["win"]], writes=[wqmB], dsem=dwm)
                for (ga, gb_, c0_) in ((0, 6, 0), (9, 11, 768), (6, 9, 1024), (11, 12, 1408)):
                    sch.add(SP, (lambda e, ga=ga, gb_=gb_, c0_=c0_: e.dma_start(
                        out=wq[:, :, c0_:c0_ + (gb_ - ga) * 128],
                        in_=wq_mine[:, ga:gb_, :].rearrange("(k p) G c -> p k (G c)", p=128))),
                        reads=[wqmB], writes=[wqB], dsem=dw)
                it = 0
                ev = 0
                for r in range(4):
                    for tt in range(NT):
                        s = it % 2
                        it += 1
                        tok0 = r * TC + tt * W
                        for q in range(2):
                            sch.add(SP, (lambda e, r=r, tt=tt, s=s, q=q: e.dma_start(
                                out=uT2[s][:, :, q * 256:(q + 1) * 256],
                                in_=uall[2 * tt + q, r * D:(r + 1) * D, :].rearrange(
                                    "(k p) t -> p k t", p=128))), reads=[uallB[2 * tt + q]], writes=[uB2[s]], dsem=du2[s])
                        for hd in range(8):
                            pi = next_ps()
                            for k in range(KD):
                                mm(psum[pi][:, :], wq[:, k, hd * 128:(hd + 1) * 128], uT2[s][:, k, :],
                                   k == 0, k == KD - 1, [wqB, uB2[s]], PB[pi])
                            q_ = ev % 4
                            ev += 1
                            if ev % 2 == 0:
                                act(stq[q_][:], psum[pi][:, :], AF.Copy, reads=[PB[pi]], writes=[stqB[q_]])
                            else:
                                sch.add(DVE, (lambda e, q_=q_, pi=pi: e.tensor_copy(out=stq[q_][:], in_=psum[pi][:, :])),
                                        reads=[PB[pi]], writes=[stqB[q_]])
                            sch.add(SP, (lambda e, hd=hd, q_=q_, tok0=tok0: e.dma_start(
                                out=qk[hd, :, tok0:tok0 + W], in_=stq[q_][:])), reads=[stqB[q_]], dsem=dsq[q_])
                        for sub in range(4):
                            pi = next_ps()
                            for k in range(KD):
                                mm(psum[pi][:, :], uT2[s][:, k, sub * 128:(sub + 1) * 128], wq[:, k, 1024:1536],
                                   k == 0, k == KD - 1, [wqB, uB2[s]], PB[pi])
                            q_ = ev % 4
                            ev += 1
                            if ev % 2 == 0:
                                act(stv[q_][:], psum[pi][:, :], AF.Copy, reads=[PB[pi]], writes=[stvB[q_]])
                            else:
                                sch.add(DVE, (lambda e, q_=q_, pi=pi: e.tensor_copy(out=stv[q_][:], in_=psum[pi][:, :])),
                                        reads=[PB[pi]], writes=[stvB[q_]])
                            sch.add(SP, (lambda e, sub=sub, q_=q_, tok0=tok0: e.dma_start(
                                out=vbuf[tok0 + sub * 128:tok0 + (sub + 1) * 128, :], in_=stv[q_][:])),
                                reads=[stvB[q_]], dsem=dsv[q_])
                sch.barrier()
            if stop_after == "p2a":
                break

            with ExitStack() as es:
                ab_sb = sbt(es, "pa_bias", [128, 3, 256], F32)
                qTg = [sbt(es, "pa_q%d" % g, [128, 2048], BF16) for g in range(3)]
                kTg = [sbt(es, "pa_k%d" % g, [128, 2, 2048], BF16) for g in range(3)]
                Vg = [sbt(es, "pa_v%d" % g, [128, 2, 16, 128], BF16) for g in range(3)]
                accUS = sbt(es, "pa_accUS", [128, 2, 2048], F32)
                accU = accUS[:, 0, :]
                accS = accUS[:, 1, :]
                sc = [sbt(es, "pa_sc%d" % i, [128, 256], F32) for i in range(4)]
                Pt = [sbt(es, "pa_P%d" % i, [128, 256], BF16) for i in range(4)]
                yst = sbt(es, "pa_y", [128, 2048], BF16)
                abB = Buf()
                qB = [Buf() for _ in range(3)]
                kB = [[Buf(), Buf()] for _ in range(3)]
                vB = [[Buf(), Buf()] for _ in range(3)]
                accUB, accSB, ystB = Buf(), Buf(), Buf()
                scB = [Buf() for _ in range(4)]
                PtB = [Buf() for _ in range(4)]
                dab = sch.dsem("pa_ab")
                dq = [sch.dsem("pa_q%d" % g) for g in range(3)]
                dk = [[sch.dsem("pa_k%d_%d" % (g, s)) for s in range(2)] for g in range(3)]
                dv = [[sch.dsem("pa_v%d_%d" % (g, s)) for s in range(2)] for g in range(3)]
                dy = sch.dsem("pa_y")
                sch.add(SP, lambda e: e.dma_start(out=ab_sb[:], in_=abias), writes=[abB], dsem=dab)
                DIL = (1, 4, 16)
                bi = 0
                for sb in range(NSB):
                    slot = sb % 2
                    c0 = sb * 2048
                    for g in range(3):
                        d = DIL[g]
                        sch.add(SP, (lambda e, g=g, c0=c0: e.dma_start(out=qTg[g][:], in_=qk[g, :, c0:c0 + 2048])),
                                writes=[qB[g]], dsem=dq[g])
                        sch.add(SP, (lambda e, g=g, c0=c0, slot=slot: e.dma_start(
                            out=kTg[g][:, slot, :], in_=qk[3 + g, :, c0:c0 + 2048])),
                            writes=[kB[g][slot]], dsem=dk[g][slot])
                        srcv = vbuf[c0:c0 + 2048, g * 128:(g + 1) * 128].rearrange(
                            "(nn i r) c -> i r nn c", i=128, r=d)
                        nn_ = 16 // d
                        if d == 4:
                            for r in range(d):
                                sch.add(SP, (lambda e, g=g, slot=slot, srcv=srcv, r=r, nn_=nn_: e.dma_start(
                                    out=Vg[g][:, slot, r * nn_:(r + 1) * nn_, :], in_=srcv[:, r, :, :])),
                                    writes=[vB[g][slot]], dsem=dv[g][slot])
                        else:
                            sv = srcv[:, 0, :, :] if d == 1 else srcv[:, :, 0, :]
                            sch.add(SP, (lambda e, g=g, slot=slot, sv=sv: e.dma_start(
                                out=Vg[g][:, slot, :, :], in_=sv)),
                                writes=[vB[g][slot]], dsem=dv[g][slot])
                    for g in range(3):
                        d = DIL[g]
                        nn_ = 16 // d

                        def colsel(t, nn, r, d=d):
                            return t[:, nn * 128 * d:(nn + 1) * 128 * d].rearrange("p (i r) -> p r i", r=d)[:, r, :]

                        a_pend = []

                        def a_stage2(st_):
                            (b_, first, vc_ap, vp_ap, vrd, nn, r, g, d) = st_
                            po = next_ps()
                            mm(psum[po][:, 0:128], vc_ap, Pt[b_][:, 0:128], True, first, vrd + [PtB[b_]], PB[po])
                            if not first:
                                mm(psum[po][:, 0:128], vp_ap, Pt[b_][:, 128:256], False, True, vrd + [PtB[b_]], PB[po])
                            mm(psum[po][:, 128:256], ONESm, Pt[b_][:, 0:128], True, first, [PtB[b_], Bconst], PB[po])
                            if not first:
                                mm(psum[po][:, 128:256], ONESm, Pt[b_][:, 128:256], False, True, [PtB[b_], Bconst], PB[po])
                            aUS = accUS[:, :, nn * 128 * d:(nn + 1) * 128 * d].rearrange(
                                "p t (i r) -> p t r i", r=d)[:, :, r, :]
                            pUS = psum[po][:, 0:256].rearrange("p (t i) -> p t i", t=2)
                            if g == 0:
                                sch.add(DVE, (lambda e, aUS=aUS, pUS=pUS: e.tensor_copy(out=aUS, in_=pUS)),
                                        reads=[PB[po]], writes=[accUB])
                            else:
                                sch.add(DVE, (lambda e, aUS=aUS, pUS=pUS: e.tensor_tensor(
                                    out=aUS, in0=aUS, in1=pUS, op=ALU.add)),
                                    reads=[PB[po], accUB], writes=[accUB])

                        for r in range(d):
                            for nn in range(nn_):
                                first = (sb == 0 and nn == 0)
                                q_ap = colsel(qTg[g][:], nn, r)
                                kc_ap = colsel(kTg[g][:, slot, :], nn, r)
                                vc_ap = Vg[g][:, slot, r * nn_ + nn, :]
                                rd = [qB[g], kB[g][slot]]
                                vrd = [vB[g][slot]]
                                if not first:
                                    if nn >= 1:
                                        kp_ap = colsel(kTg[g][:, slot, :], nn - 1, r)
                                        vp_ap = Vg[g][:, slot, r * nn_ + nn - 1, :]
                                    else:
                                        kp_ap = colsel(kTg[g][:, 1 - slot, :], nn_ - 1, r)
                                        vp_ap = Vg[g][:, 1 - slot, r * nn_ + nn_ - 1, :]
                                        rd = rd + [kB[g][1 - slot]]
                                        vrd = vrd + [vB[g][1 - slot]]
                                ncol = 128 if first else 256
                                pi = next_ps()
                                b_ = bi % 4
                                bi += 1
                                mm(psum[pi][:, 0:128], kc_ap, q_ap, True, True, rd, PB[pi])
                                if not first:
                                    mm(psum[pi][:, 128:256], kp_ap, q_ap, True, True, rd, PB[pi])
                                sch.add(DVE, (lambda e, pi=pi, b_=b_, g=g, ncol=ncol: e.scalar_tensor_tensor(
                                    out=sc[b_][:, 0:ncol], in0=psum[pi][:, 0:ncol], scalar=QSCALE,
                                    in1=ab_sb[:, g, 0:ncol], op0=ALU.mult, op1=ALU.add)),
                                    reads=[PB[pi], abB], writes=[scB[b_]])
                                act(Pt[b_][:, 0:ncol], sc[b_][:, 0:ncol], AF.Exp, reads=[scB[b_]], writes=[PtB[b_]])
                                a_pend.append((b_, first, vc_ap, vp_ap if not first else None, vrd, nn, r, g, d))
                                if len(a_pend) > 2:
                                    a_stage2(a_pend.pop(0))
                        while a_pend:
                            a_stage2(a_pend.pop(0))
                    sch.add(DVE, (lambda e: e.reciprocal(out=accS, in_=accS)), reads=[accUB], writes=[accUB])
                    sch.add(DVE, (lambda e: e.tensor_tensor(out=yst[:], in0=accU, in1=accS, op=ALU.mult)),
                            reads=[accUB], writes=[ystB])
                    for qq in range(2048 // ACW):
                        sch.add(SP, (lambda e, c0=c0, qq=qq: e.dma_start(
                            out=att_own[(c0 + qq * ACW) // ACW, 0:128, :], in_=yst[:, qq * ACW:(qq + 1) * ACW])),
                            reads=[ystB], writes=[attoB[(c0 + qq * ACW) // ACW]], dsem=dy)
                sch.barrier()
            if stop_after == "p2bA":
                break

            with ExitStack() as es:
                kTB = sbt(es, "pb_k", [128, S], BF16)
                qTB = sbt(es, "pb_q", [128, S], BF16)
                VB = sbt(es, "pb_v", [128, NCH, 128], BF16)
                bm_sb = sbt(es, "pb_mask", [128, 4, 512], BF16)
                ez = [sbt(es, "pb_ez%d" % i, [128, 512], F32) for i in range(6)]
                sp = [sbt(es, "pb_sp%d" % i, [128, 512], BF16) for i in range(6)]
                wt = [sbt(es, "pb_w%d" % i, [128, 512], BF16) for i in range(3)]
                Lr = sbt(es, "pb_Lr", [128, 512], F32)
                Lrb = [sbt(es, "pb_Lrb%d" % i, [128, 512], BF16) for i in range(4)]
                ost = [sbt(es, "pb_o%d" % i, [128, 512], BF16) for i in range(2)]
                kvB, bmB = Buf(), Buf()
                ezB = [Buf() for _ in range(6)]
                spB = [Buf() for _ in range(6)]
                wtB = [Buf() for _ in range(3)]
                LrB = Buf()
                LrbB = [Buf() for _ in range(4)]
                ostB = [Buf(), Buf()]
                dkv = [sch.dsem("pb_kv%d" % i) for i in range(4)]
                do = [sch.dsem("pb_o%d" % i) for i in range(2)]
                sch.add(SP, lambda e: e.dma_start(out=kTB[:], in_=qk[7, :, :]), writes=[kvB], dsem=dkv[0])
                sch.add(SP, lambda e: e.dma_start(out=qTB[:], in_=qk[6, :, :]), writes=[kvB], dsem=dkv[1])
                sch.add(SP, lambda e: e.dma_start(
                    out=VB[:], in_=vbuf[:, 384:512].rearrange("(c p) d -> p c d", p=128)), writes=[kvB], dsem=dkv[2])
                sch.add(POOL, lambda e: e.dma_start(out=bm_sb[:], in_=bmask), writes=[bmB], dsem=dkv[3])
                zi = 0
                si = 0
                for gq in range(S // 512):
                    n0 = 4 * gq
                    q_ap = qTB[:, n0 * 128:n0 * 128 + 512]
                    steps = list(range(n0 + 3, -1, -1))
                    po = 6 + (gq % 2)
                    ob = gq % 2
                    stA = {}

                    def stageA1(c):
                        nonlocal zi
                        pz = zi % 6
                        e_ = zi % 6
                        zi += 1
                        mm(psum[pz][:, :], kTB[:, c * 128:(c + 1) * 128], q_ap, True, False, [kvB], PB[pz])
                        act(ez[e_][:], psum[pz][:, :], AF.Exp, reads=[PB[pz]], writes=[ezB[e_]], scale=QSCALE)
                        stA[c] = (pz, e_)

                    def stageA2(c):
                        nonlocal si
                        pz, e_ = stA[c]
                        s_ = si % 6
                        si += 1
                        act(sp[s_][:], ez[e_][:], AF.Ln, reads=[ezB[e_]], writes=[spB[s_]], bias=1.0)
                        if c >= n0:
                            j = c - n0
                            sch.add(DVE, (lambda e, s_=s_, j=j: e.tensor_tensor(
                                out=sp[s_][:], in0=sp[s_][:], in1=bm_sb[:, j, :], op=ALU.mult)),
                                reads=[spB[s_], bmB], writes=[spB[s_]])
                        stA[c] = (pz, s_)

                    stB2 = {}

                    def stageB2(c):
                        w_, firststep, laststep = stB2.pop(c)
                        mm(psum[po][:, :], VB[:, c, :], wt[w_][:], firststep, laststep, [kvB, wtB[w_]], PB[po])

                    def stageB(c, idx):
                        pz, s_ = stA.pop(c)
                        firststep = idx == 0
                        laststep = idx == len(steps) - 1
                        lb = idx % 4
                        mm(psum[pz][:, :], NTm, sp[s_][:], False, firststep, [spB[s_], Bconst, PB[pz]], PB[pz])
                        if not firststep:
                            mm(psum[pz][:, :], NONESm, Lrb[lb][:], False, True, [LrbB[lb], Bconst, PB[pz]], PB[pz])
                        w_ = idx % 3
                        act(wt[w_][:], psum[pz][:, :], AF.Exp, reads=[PB[pz]], writes=[wtB[w_]], scale=QSCALE2)
                        if c >= n0:
                            j = c - n0
                            sch.add(DVE, (lambda e, w_=w_, j=j: e.tensor_tensor(
                                out=wt[w_][:], in0=wt[w_][:], in1=bm_sb[:, j, :], op=ALU.mult)),
                                reads=[wtB[w_], bmB], writes=[wtB[w_]])
                        stB2[c] = (w_, firststep, laststep)
                        if not laststep:
                            if firststep:
                                sch.add(DVE, (lambda e, s_=s_: e.tensor_copy(out=Lr[:], in_=sp[s_][:])),
                                        reads=[spB[s_]], writes=[LrB])
                            else:
                                sch.add(DVE, (lambda e, s_=s_: e.tensor_tensor(
                                    out=Lr[:], in0=Lr[:], in1=sp[s_][:], op=ALU.add)),
                                    reads=[spB[s_], LrB], writes=[LrB])
                            nb_ = (idx + 1) % 4
                            sch.add(DVE, (lambda e, nb_=nb_: e.tensor_copy(out=Lrb[nb_][:], in_=Lr[:])),
                                    reads=[LrB], writes=[LrbB[nb_]])

                    GB_ = 3
                    batches = [steps[i:i + GB_] for i in range(0, len(steps), GB_)]
                    for c in batches[0]:
                        stageA1(c)
                    for c in batches[0]:
                        stageA2(c)
                    idx = 0
                    for bi_, batch in enumerate(batches):
                        if bi_ + 1 < len(batches):
                            for c in batches[bi_ + 1]:
                                stageA1(c)
                        for c in batch:
                            stageB(c, idx)
                            idx += 1
                        for c in batch:
                            stageB2(c)
                        if bi_ + 1 < len(batches):
                            for c in batches[bi_ + 1]:
                                stageA2(c)
                    if gq % 2 == 0:
                        act(ost[ob][:], psum[po][:, :], AF.Copy, reads=[PB[po]], writes=[ostB[ob]])
                    else:
                        sch.add(DVE, (lambda e, ob=ob, po=po: e.tensor_copy(out=ost[ob][:], in_=psum[po][:, :])),
                                reads=[PB[po]], writes=[ostB[ob]])
                    sch.add(SP, (lambda e, ob=ob, n0=n0: e.dma_start(
                        out=att_own[(n0 * 128) // ACW, 128:256, (n0 * 128) % ACW:(n0 * 128) % ACW + 512], in_=ost[ob][:])),
                        reads=[ostB[ob]], writes=[attoB[(n0 * 128) // ACW]], dsem=do[ob])
                    if ((n0 * 128) + 512) % ACW == 0:
                        ag_att((n0 * 128) // ACW)
                sch.barrier()
            if stop_after == "p2bB":
                break

            if stop_after == "ag_a":
                break

            last = (l == nlayers - 1)
            with ExitStack() as es:
                hT = sbt(es, "p3_h", [128, KD, W], F32)
                uT = sbt(es, "p3_u", [128, KD, W], BF16)
                cp = sbt(es, "p3_cp", [128, NFF, W], BF16)
                nb = sbt(es, "p3_nb", [128, KD, W], F32)
                pTt = sbt(es, "p3_p", [128, 2, W], BF16)
                wsl = [sbt(es, "p3_w%d" % i, [128, KD, 256], BF16) for i in range(4)]
                sq = [sbt(es, "p3_sq%d" % i, [128, W], BF16) for i in range(2)]
                rstd = sbt(es, "p3_rstd", [128, W], F32)
                tmp = [sbt(es, "p3_t%d" % i, [128, W + 2], F32) for i in range(6)]
                uph = sbt(es, "p3_uph", [128, 2 * NFF, 2], F32)
                hB = [Buf() for _ in range(KD)]
                uB = [Buf() for _ in range(KD)]
                cB = [Buf() for _ in range(NFF)]
                nB = [Buf() for _ in range(KD)]
                pB = Buf()
                wB = [Buf() for _ in range(4)]
                sqB = [Buf(), Buf()]
                rstdB = Buf()
                tB = [Buf() for _ in range(6)]
                uphB = [Buf() for _ in range(2 * NFF)]
                dW = [sch.dsem("p3w%d" % i) for i in range(4)]
                dH, dU, dA, dP, dO, dUo, dHH = [sch.dsem("p3%s" % n) for n in "h u a p o uo hh".split()]
                wst = {"i": 0, "t": 0}

                def slab(name, r0, nr, c0, ncol=256):
                    s = wst["i"] % 4
                    wst["i"] += 1
                    nk = nr // 128
                    if name == "wg":
                        name, c0 = "win", c0 + 6144
                    src = wbf[name][l, r0:r0 + nr, c0:c0 + ncol].rearrange("(k p) c -> p k c", p=128)
                    if not (_os.environ.get("K_NOSLAB") == "1" and wst["i"] > 4):
                        sch.add(SP, (lambda e, s=s, nk=nk, src=src, ncol=ncol: e.dma_start(
                            out=wsl[s][:, 0:nk, 0:ncol], in_=src)), reads=[wflB[name]], writes=[wB[s]], dsem=dW[s])
                    return s

                def next_t():
                    i = wst["t"] % 6
                    wst["t"] += 1
                    return i

                ne = (sq, sqB, rstd, rstdB)

                def p3_tile(t0, Wt, halo, l=l, last=last):
                    def ld_h(e):
                        if halo:
                            if l == 0:
                                return e.dma_start(out=hT[:, :, 0:2], in_=xh)
                            return e.dma_start(out=hT[:, :, 0:2], in_=hh_all.rearrange("(r p) c -> r p c", r=4)[
                                bass.ds(PID["im1"], 1), :, :].rearrange("o p (k t) -> p k (o t)", t=2))
                        src = xT if l == 0 else hbuf
                        return e.dma_start(out=hT[:], in_=src[:, :, t0:t0 + W])
                    sch.add(SP, ld_h, writes=hB, dsem=dH)

                    def ld_u(e):
                        if halo:
                            return e.dma_start(out=uT[:, :, 0:2], in_=uall[NQ - 1].rearrange("(r f) t -> r f t", r=4)[
                                bass.ds(PID["im1"], 1), :, 254:256].rearrange("o (k p) t -> p k (o t)", p=128))
                        return None
                    if halo:
                        sch.add(SP, ld_u, reads=[uallB[NQ - 1]], writes=uB, dsem=dU)
                    else:
                        for q in range(2):
                            sch.add(SP, (lambda e, q=q: e.dma_start(
                                out=uT[:, :, q * 256:(q + 1) * 256],
                                in_=ubuf_own[2 * (t0 // W) + q].rearrange("(k p) t -> p k t", p=128))),
                                reads=[ubufB[2 * (t0 // W) + q]], writes=uB, dsem=dU)

                    for a_ in range(2):
                        def ld_a(e, a_=a_):
                            if halo:
                                src = att_halo.rearrange("(s a p) t -> p a s t", s=4, a=2)[:, a_, :, :]
                            else:
                                src = att_mine[t0 // ACW, :, t0 % ACW:t0 % ACW + Wt].rearrange(
                                    "(s a p) t -> p a s t", s=4, a=2)[:, a_, :, :]
                            return e.dma_start(out=cp[:, a_ * 4:(a_ + 1) * 4, 0:Wt], in_=src)
                        sch.add(SP, ld_a, reads=[attmB], writes=cB[0:8], dsem=dA)
                    if not halo:
                        sch.add(POOL, (lambda e: e.dma_start(out=pTt[:], in_=pT[l, :, :, t0:t0 + W])),
                                writes=[pB], dsem=dP)

                    for co2 in range(8):
                        sGA = slab("wg", 0, D, co2 * 256)
                        sGB = slab("wg", 0, D, D + co2 * 256)
                        sAB = slab("wa", 0, 512, co2 * 256)
                        sBB = slab("wb", 0, 512, co2 * 256)
                        for cc in range(2):
                            co = co2 * 2 + cc
                            cs = slice(cc * 128, (cc + 1) * 128)
                            pa, pb_, pya, pyb = next_ps(), next_ps(), next_ps(), next_ps()
                            for k in range(KD):
                                mm(psum[pa][:, 0:Wt], wsl[sGA][:, k, cs], uT[:, k, 0:Wt], k == 0, k == KD - 1,
                                   [wB[sGA], uB[k]], PB[pa])
                            for k in range(KD):
                                mm(psum[pb_][:, 0:Wt], wsl[sGB][:, k, cs], uT[:, k, 0:Wt], k == 0, k == KD - 1,
                                   [wB[sGB], uB[k]], PB[pb_])
                            for s4 in range(4):
                                mm(psum[pya][:, 0:Wt], wsl[sAB][:, s4, cs], cp[:, s4, 0:Wt], s4 == 0, s4 == 3,
                                   [wB[sAB], cB[s4]], PB[pya])
                            for s4 in range(4):
                                mm(psum[pyb][:, 0:Wt], wsl[sBB][:, s4, cs], cp[:, 4 + s4, 0:Wt], s4 == 0, s4 == 3,
                                   [wB[sBB], cB[4 + s4]], PB[pyb])
                            ta, tb = next_t(), next_t()
                            act(tmp[ta][:, 0:Wt], psum[pa][:, 0:Wt], AF.Sigmoid, reads=[PB[pa]], writes=[tB[ta]])
                            act(tmp[tb][:, 0:Wt], psum[pb_][:, 0:Wt], AF.Sigmoid, reads=[PB[pb_]], writes=[tB[tb]])
                            sch.add(DVE, (lambda e, ta=ta, pya=pya: e.tensor_tensor(
                                out=tmp[ta][:, 0:Wt], in0=tmp[ta][:, 0:Wt], in1=psum[pya][:, 0:Wt], op=ALU.mult)),
                                reads=[tB[ta], PB[pya]], writes=[tB[ta]])
                            sch.add(DVE, (lambda e, tb=tb, pyb=pyb: e.tensor_tensor(
                                out=tmp[tb][:, 0:Wt], in0=tmp[tb][:, 0:Wt], in1=psum[pyb][:, 0:Wt], op=ALU.mult)),
                                reads=[tB[tb], PB[pyb]], writes=[tB[tb]])
                            sch.add(POOL, (lambda e, ta=ta, tb=tb, co=co: e.tensor_tensor(
                                out=cp[:, 8 + co, 0:Wt], in0=tmp[ta][:, 0:Wt], in1=tmp[tb][:, 0:Wt], op=ALU.add)),
                                reads=[tB[ta], tB[tb]], writes=[cB[8 + co]])
                    for co2 in range(8):
                        sO = slab("wo", 0, D, co2 * 256)
                        for cc in range(2):
                            co = co2 * 2 + cc
                            cs = slice(cc * 128, (cc + 1) * 128)
                            pi = next_ps()
                            for k in range(KD):
                                mm(psum[pi][:, 0:Wt], wsl[sO][:, k, cs], cp[:, 8 + k, 0:Wt], k == 0, k == KD - 1,
                                   [wB[sO], cB[8 + k]], PB[pi])
                            sch.add(DVE, (lambda e, co=co, pi=pi: e.tensor_copy(out=nb[:, co, 0:Wt], in_=psum[pi][:, 0:Wt])),
                                    reads=[PB[pi]], writes=[nB[co]])
                    r_ap, rB = norm_rstd(ne, [nb[:, k, 0:Wt] for k in range(KD)], nB, Wt)
                    for k in range(KD):
                        ti = next_t()
                        sch.add(DVE, (lambda e, k=k, ti=ti: e.scalar_tensor_tensor(
                            out=tmp[ti][:, 0:Wt], in0=nb[:, k, 0:Wt], scalar=gains_sb[:, l, 1, k:k + 1], in1=r_ap,
                            op0=ALU.mult, op1=ALU.mult)), reads=[nB[k], rB, Bconst], writes=[tB[ti]])
                        sch.add(POOL, (lambda e, k=k, ti=ti: e.tensor_tensor(
                            out=hT[:, k, 0:Wt], in0=hT[:, k, 0:Wt], in1=tmp[ti][:, 0:Wt], op=ALU.add)),
                            reads=[hB[k], tB[ti]], writes=[hB[k]])
                    if "hmid" in dbg and not halo and t0 == 0 and l == 0:
                        tdb = dbg_tensor("hmid", [128, KD, W])
                        sch.add(SP, (lambda e: e.dma_start(out=tdb, in_=hT[:])), reads=hB, dsem=sch.dsem("dbg2"))
                    if "mT" in dbg and not halo and t0 == 0 and l == 0:
                        tdb2 = dbg_tensor("mT", [128, KD, W], BF16)
                        sch.add(SP, (lambda e: e.dma_start(out=tdb2, in_=cp[:, 8:24, :])), reads=cB[8:24], dsem=sch.dsem("dbg3"))
                    r_ap, rB = norm_rstd(ne, [hT[:, k, 0:Wt] for k in range(KD)], hB, Wt)
                    for k in range(KD):
                        sch.add(DVE, (lambda e, k=k: e.scalar_tensor_tensor(
                            out=uT[:, k, 0:Wt], in0=hT[:, k, 0:Wt], scalar=gains_sb[:, l, 2, k:k + 1], in1=r_ap,
                            op0=ALU.mult, op1=ALU.mult)), reads=[hB[k], rB, Bconst], writes=[uB[k]])
                    for c2 in range(NFF // 2):
                        sUG = slab("wup", 0, D, c2 * 256)
                        sUV = slab("wup", 0, D, DFF + c2 * 256)
                        for cc in range(2):
                            c = c2 * 2 + cc
                            cs = slice(cc * 128, (cc + 1) * 128)
                            ys = []
                            for (sw, ch) in ((sUG, c), (sUV, NFF + c)):
                                pi = next_ps()
                                for k in range(KD):
                                    mm(psum[pi][:, 0:Wt], wsl[sw][:, k, cs], uT[:, k, 0:Wt], k == 0, k == KD - 1,
                                       [wB[sw], uB[k]], PB[pi])
                                if halo:
                                    sch.add(DVE, (lambda e, ch=ch, pi=pi: e.tensor_scalar(
                                        out=uph[:, ch, :], in0=psum[pi][:, 0:2], scalar1=hflag_sb[:, 0:1], scalar2=None,
                                        op0=ALU.mult)), reads=[PB[pi], Bconst], writes=[uphB[ch]])
                                    continue
                                xi, yi = next_t(), next_t()
                                sch.add(POOL, (lambda e, xi=xi, ch=ch: e.tensor_copy(out=tmp[xi][:, 0:2], in_=uph[:, ch, :])),
                                        reads=[uphB[ch]], writes=[tB[xi]])
                                act(tmp[xi][:, 2:2 + W], psum[pi][:, :], AF.Copy, reads=[PB[pi]], writes=[tB[xi]])
                                sch.add(POOL, (lambda e, xi=xi, ch=ch: e.tensor_copy(out=uph[:, ch, :], in_=tmp[xi][:, W:W + 2])),
                                        reads=[tB[xi]], writes=[uphB[ch]])
                                sch.add(DVE, (lambda e, xi=xi, yi=yi, ch=ch: e.tensor_scalar(
                                    out=tmp[yi][:, 0:W], in0=tmp[xi][:, 2:2 + W], scalar1=convw_sb[:, l, 2, ch:ch + 1],
                                    scalar2=convw_sb[:, l, 3, ch:ch + 1], op0=ALU.mult, op1=ALU.add)),
                                    reads=[tB[xi], Bconst], writes=[tB[yi]])
                                sch.add(DVE, (lambda e, xi=xi, yi=yi, ch=ch: e.scalar_tensor_tensor(
                                    out=tmp[yi][:, 0:W], in0=tmp[xi][:, 1:1 + W], scalar=convw_sb[:, l, 1, ch:ch + 1],
                                    in1=tmp[yi][:, 0:W], op0=ALU.mult, op1=ALU.add)),
                                    reads=[tB[xi], tB[yi], Bconst], writes=[tB[yi]])
                                sch.add(DVE, (lambda e, xi=xi, yi=yi, ch=ch: e.scalar_tensor_tensor(
                                    out=tmp[yi][:, 0:W], in0=tmp[xi][:, 0:W], scalar=convw_sb[:, l, 0, ch:ch + 1],
                                    in1=tmp[yi][:, 0:W], op0=ALU.mult, op1=ALU.add)),
                                    reads=[tB[xi], tB[yi], Bconst], writes=[tB[yi]])
                                ys.append(yi)
                            if halo:
                                continue
                            yg, yv = ys
                            act(tmp[yg][:, 0:W], tmp[yg][:, 0:W], AF.Gelu_apprx_tanh, reads=[tB[yg]], writes=[tB[yg]])
                            sch.add(POOL, (lambda e, yg=yg, yv=yv, c=c: e.tensor_tensor(
                                out=cp[:, c, :], in0=tmp[yg][:, 0:W], in1=tmp[yv][:, 0:W], op=ALU.mult)),
                                reads=[tB[yg], tB[yv]], writes=[cB[c]])
                    if halo:
                        return
                    for co2 in range(8):
                        pis = [next_ps(), next_ps()]
                        kparts = [(0, 16), (16, 16), (32, 12)]
                        for (k0, nk) in kparts:
                            sD = slab("wd", k0 * 128, nk * 128, co2 * 256)
                            for cc in range(2):
                                cs = slice(cc * 128, (cc + 1) * 128)
                                for kk in range(nk):
                                    k = k0 + kk
                                    mm(psum[pis[cc]][:, :], wsl[sD][:, kk, cs], cp[:, k, :], k == 0, k == NFF - 1,
                                       [wB[sD], cB[k]], PB[pis[cc]])
                        for cc in range(2):
                            co = co2 * 2 + cc
                            sch.add(DVE, (lambda e, co=co, pi=pis[cc]: e.tensor_copy(out=nb[:, co, :], in_=psum[pi][:, :])),
                                    reads=[PB[pis[cc]]], writes=[nB[co]])
                    r_ap, rB = norm_rstd(ne, [nb[:, k, :] for k in range(KD)], nB, W)
                    for k in range(KD):
                        ti = next_t()
                        sch.add(DVE, (lambda e, k=k, ti=ti: e.scalar_tensor_tensor(
                            out=tmp[ti][:, 0:W], in0=nb[:, k, :], scalar=gains_sb[:, l, 3, k:k + 1], in1=r_ap,
                            op0=ALU.mult, op1=ALU.mult)), reads=[nB[k], rB, Bconst], writes=[tB[ti]])
                        sch.add(POOL, (lambda e, k=k, ti=ti: e.tensor_tensor(
                            out=hT[:, k, :], in0=hT[:, k, :], in1=tmp[ti][:, 0:W], op=ALU.add)),
                            reads=[hB[k], tB[ti]], writes=[hB[k]])
                        sch.add(DVE, (lambda e, k=k: e.tensor_copy(out=uT[:, k, :], in_=hT[:, k, :])),
                                reads=[hB[k]], writes=[uB[k]])
                    if "hffn" in dbg and t0 == 0 and l == 0:
                        tdb3 = dbg_tensor("hffn", [128, KD, W])
                        sch.add(SP, (lambda e: e.dma_start(out=tdb3, in_=hT[:])), reads=hB, dsem=sch.dsem("dbg4"))
                    for co2 in range(8):
                        sG = slab("wpg", 0, D, co2 * 256)
                        sI = slab("wpi", 0, PLE, co2 * 256)
                        for cc in range(2):
                            co = co2 * 2 + cc
                            cs = slice(cc * 128, (cc + 1) * 128)
                            pg, pe = next_ps(), next_ps()
                            for k in range(KD):
                                mm(psum[pg][:, :], wsl[sG][:, k, cs], uT[:, k, :], k == 0, k == KD - 1,
                                   [wB[sG], uB[k]], PB[pg])
                            for k in range(2):
                                mm(psum[pe][:, :], wsl[sI][:, k, cs], pTt[:, k, :], k == 0, k == 1,
                                   [wB[sI], pB], PB[pe])
                            ti = next_t()
                            act(tmp[ti][:, 0:W], psum[pg][:, :], AF.Sigmoid, reads=[PB[pg]], writes=[tB[ti]])
                            sch.add(DVE, (lambda e, ti=ti, pe=pe: e.tensor_tensor(
                                out=tmp[ti][:, 0:W], in0=tmp[ti][:, 0:W], in1=psum[pe][:, :], op=ALU.mult)),
                                reads=[tB[ti], PB[pe]], writes=[tB[ti]])
                            sch.add(POOL, (lambda e, co=co, ti=ti: e.tensor_tensor(
                                out=hT[:, co, :], in0=hT[:, co, :], in1=tmp[ti][:, 0:W], op=ALU.add)),
                                reads=[hB[co], tB[ti]], writes=[hB[co]])
                    dst = outT if last else hbuf
                    sch.add(SP, (lambda e: e.dma_start(out=dst[:, :, t0:t0 + W], in_=hT[:])), reads=hB, dsem=dO)
                    if not last:
                        if t0 + W == TC:
                            sch.add(SP, (lambda e: e.dma_start(
                                out=hh_own.rearrange("p (k t) -> p k t", t=2), in_=hT[:, :, W - 2:W])),
                                reads=hB, dsem=dHH)
                        r_ap2, rB2 = norm_rstd(ne, [hT[:, k, :] for k in range(KD)], hB, W)
                        for k in range(KD):
                            sch.add(DVE, (lambda e, k=k: e.scalar_tensor_tensor(
                                out=uT[:, k, :], in0=hT[:, k, :], scalar=gains_sb[:, l + 1, 0, k:k + 1], in1=r_ap2,
                                op0=ALU.mult, op1=ALU.mult)), reads=[hB[k], rB2, Bconst], writes=[uB[k]])
                        for q in range(2):
                            sch.add(SP, (lambda e, q=q: e.dma_start(
                                out=ubuf_own[2 * (t0 // W) + q].rearrange("(k p) t -> p k t", p=128),
                                in_=uT[:, :, q * 256:(q + 1) * 256])),
                                reads=uB, writes=[ubufB[2 * (t0 // W) + q]], dsem=dUo)
                        for q in range(2):
                            ag_u(2 * (t0 // W) + q)

                attmB = Buf()
                dam = sch.dsem("p3am")
                sch.add(SP, (lambda e: e.dma_start(
                    out=att_mine.rearrange("c r t -> (c r) t"),
                    in_=att_all.rearrange("(i c) r t -> i (c r) t", i=4)[bass.ds(PID["i"], 1), :, :].rearrange(
                        "o x t -> (o x) t"))), reads=attaB, writes=[attmB], dsem=dam)
                sch.add(SP, (lambda e: e.dma_start(
                    out=att_halo,
                    in_=att_all.rearrange("(i c) r t -> i c r t", i=4)[
                        bass.ds(PID["im1"], 1), TC // ACW - 1, :, ACW - 2:ACW].rearrange("o r t -> (o r) t"))),
                    reads=attaB, writes=[attmB], dsem=dam)
                p3_tile(0, 2, True)
                for tt in range(NT):
                    p3_tile(tt * W, W, False)
                sch.barrier()
            if not last:
                dcc3 = sch.dsem("cc_h", step=1)
                sch.add(POOL, (lambda e: e.collective_compute(
                    "AllGather", ALU.bypass, replica_groups=GROUPS, ins=[hh_own], outs=[hh_all])), dsem=dcc3)
                sch.barrier()

        ddbg = sch.dsem("dbg")
        srcs = {"uall": (uall, [NQ, 4 * D, 256], BF16), "qk": (qk, [8, 128, S], BF16), "vbuf": (vbuf, [S, 512], BF16),
                "att_own": (att_own, [NAC, 256, ACW], BF16), "att_all": (att_all, [NAC, 1024, ACW], BF16),
                "hbuf": (hbuf, [128, KD, TC], F32), "ubuf_own": (ubuf_own, [NQ, D, 256], BF16)}
        for name in dbg:
            if name not in srcs:
                continue
            src, shape, dt = srcs[name]
            t = dbg_tensor(name, shape, dt)
            sch.add(SP, (lambda e, t=t, src=src: e.dma_start(out=t, in_=src)), dsem=ddbg)
        sch.final_wait(SP)
        stats = sch.emit(nc)
    return nc, stats, list(dbg_out.keys())


def make_inputs(S, nl, x, p, g_mix_pre, w_in, w_branch_a, w_branch_b, w_out, g_mix_post, g_ffn_pre, w_up,
                conv_w, conv_b, w_down, g_ffn_post, w_ple_in, w_ple_gate):
    TC = S // 4
    f32 = np.float32
    WA, WB_ = 1536, 512
    gains = np.stack([g_mix_pre[:nl], g_mix_post[:nl], g_ffn_pre[:nl], g_ffn_post[:nl]], axis=1)
    gains = np.ascontiguousarray(gains.reshape(nl, 4, KD, 128).transpose(3, 0, 1, 2)).astype(f32)
    cw = np.concatenate([conv_w[:nl], conv_b[:nl, None, :]], axis=1)
    cw = np.ascontiguousarray(cw.reshape(nl, 4, 2 * NFF, 128).transpose(3, 0, 1, 2)).astype(f32)
    wfull = {"win": w_in[:nl], "wa": w_branch_a[:nl], "wb": w_branch_b[:nl], "wo": w_out[:nl], "wup": w_up[:nl],
             "wd": w_down[:nl], "wpi": w_ple_in[:nl], "wpg": w_ple_gate[:nl]}
    wshards = [dict() for _ in range(4)]
    for n, r, c in WSPECS:
        flat = np.ascontiguousarray(wfull[n]).reshape(-1).astype(f32, copy=False)
        pieces = [[] for _ in range(4)]
        for (st, sz) in wchunks(flat.size):
            q = sz // 4
            for i in range(4):
                pieces[i].append(flat[st + i * q: st + (i + 1) * q])
        for i in range(4):
            wshards[i][n + "_sh"] = np.concatenate(pieces[i]).reshape(-1, 2048)
    common = {"gains": gains, "convw": cw}
    idx = np.arange(128)
    kk, kq = idx[:, None], idx[None, :]
    cbf = np.zeros((128, 3, 128), f32)
    cbf[:, 0, :] = np.where(kk >= kq, -NEGC, 0.0)
    cbf[:, 1, :] = -NEGC
    cbf[:, 2, :] = 1.0
    common["cbf"] = cbf
    bm = np.zeros((128, 4, 512), f32)
    for j in range(4):
        for b in range(4):
            if b > j:
                bm[:, j, b * 128:(b + 1) * 128] = 1.0
            elif b == j:
                bm[:, j, b * 128:(b + 1) * 128] = (kk < kq).astype(f32)
    common["bmask"] = bm
    DIL = (1, 4, 16)
    in_maps = []
    for c in range(8):
        b, i = divmod(c, 4)
        j = i
        m = dict(common)
        xs = x[b, i * TC:(i + 1) * TC, :]
        m["xT"] = np.ascontiguousarray(xs.reshape(TC, KD, 128).transpose(2, 1, 0)).astype(f32)
        if i == 0:
            m["xh"] = np.zeros((128, KD, 2), f32)
        else:
            m["xh"] = np.ascontiguousarray(x[b, i * TC - 2:i * TC, :].reshape(2, KD, 128).transpose(2, 1, 0)).astype(f32)
        ps_ = p[:nl, b, i * TC:(i + 1) * TC, :]
        m["pT"] = np.ascontiguousarray(ps_.reshape(nl, TC, 2, 128).transpose(0, 3, 2, 1)).astype(f32)
        m.update(wshards[i])
        m["hflag"] = np.full((128, 1), 0.0 if i == 0 else 1.0, f32)
        ab = np.zeros((128, 3, 256), f32)
        for g in range(3):
            h = g * 4 + j
            slope = 2.0 ** (-8.0 * (h + 1) / 12.0)
            d = DIL[g]
            st_cur = (kq - kk).astype(np.float64)
            cur = np.where(st_cur >= 0, -slope * st_cur * d, MASKV)
            st_prev = (128 + kq - kk).astype(np.float64)
            prev = np.where(st_prev <= 128, -slope * st_prev * d, MASKV)
            ab[:, g, 0:128] = cur
            ab[:, g, 128:256] = prev
        m["abias"] = ab
        in_maps.append(m)
    return in_maps


_CACHE = {}


def kernel(**inputs):
    inputs = {k: np.asarray(v) for k, v in inputs.items()}
    x = inputs["x"]
    B, S, _ = x.shape
    nl = inputs["w_in"].shape[0]
    key = (S, nl)
    if key not in _CACHE:
        import os as _os2
        _CACHE[key] = build(S, nl, stop_after=_os2.environ.get("K_STOP"))[0]
    nc = _CACHE[key]
    in_maps = make_inputs(S, nl, **inputs)
    res = run_bass_kernel_spmd(nc, in_maps, core_ids=list(range(8)))
    TC = S // 4
    out = np.empty((B, S, D), np.float32)
    for c in range(8):
        b, i = divmod(c, 4)
        o = np.asarray(res.results[c]["outT"])
        out[b, i * TC:(i + 1) * TC, :] = o.transpose(2, 1, 0).reshape(TC, D)
    return out
```

```python
from contextlib import ExitStack
import math
import numpy as np
import ml_dtypes
import concourse.bass as bass
import concourse.mybir as mybir
from concourse.bass_utils import run_bass_kernel_spmd

F32 = mybir.dt.float32
BF16 = mybir.dt.bfloat16
AF = mybir.ActivationFunctionType
ALU = mybir.AluOpType

PE, ACT, DVE, POOL, SP = "tensor", "scalar", "vector", "gpsimd", "sync"
ENGS = (PE, ACT, DVE, POOL, SP)

D = 2048
KD = 16
DFF = 5632
NFF = 44
PLE = 256
L_ = 2
W = 512
GROUPS = [[0, 1, 2, 3], [4, 5, 6, 7]]
RMS_EPS = 1e-6
QSCALE = 1.0 / math.sqrt(128.0)
NEGC = 11.3125
QSCALE2 = 1.0 / NEGC
MASKV = -30000.0


class Buf:
    __slots__ = ("name", "w", "r", "rd")

    def __init__(self, name=""):
        self.name = name
        self.w = None
        self.r = {}
        self.rd = []


class DSem:
    __slots__ = ("name", "cnt", "h", "step", "async_")

    def __init__(self, name, step=16, async_=False):
        self.name = name
        self.cnt = 0
        self.h = None
        self.step = step
        self.async_ = async_


class Op:
    __slots__ = ("eng", "fn", "deps", "sig", "sidx", "dsem", "dval", "gid")

    def __init__(self, eng, fn, gid):
        self.eng = eng
        self.fn = fn
        self.deps = []
        self.sig = False
        self.sidx = 0
        self.dsem = None
        self.dval = 0
        self.gid = gid


PID = {}


class Sched:
    def __init__(self):
        self.ops = {e: [] for e in ENGS}
        self.gid = 0
        self.dsems = []
        self.bar = []
        self.last = {}
        self.lastdma = {}
        self.dsem_by_name = {}

    def dsem(self, name, step=16, async_=False):
        if name in self.dsem_by_name:
            return self.dsem_by_name[name]
        d = DSem(name, step, async_)
        self.dsems.append(d)
        self.dsem_by_name[name] = d
        return d

    def add(self, eng, fn, reads=(), writes=(), dsem=None, nobar=False):
        self.gid += 1
        op = Op(eng, fn, self.gid)
        is_dma = dsem is not None
        deps = {}

        def adddep(d):
            if d is None:
                return
            if d.dsem is not None:
                deps[("d", d.gid)] = d
            else:
                if d.eng == PE and eng == PE and not is_dma:
                    return
                k = d.eng
                if k not in deps or deps[k].gid < d.gid:
                    deps[k] = d

        for b in reads:
            adddep(b.w)
        for b in writes:
            if not (is_dma and b.w is not None and b.w.dsem is dsem):
                adddep(b.w)
            for d in b.r.values():
                adddep(d)
            for d in b.rd:
                adddep(d)
        if not nobar:
            for d in self.bar:
                adddep(d)
        op.deps = list(deps.values())
        for d in op.deps:
            if d.dsem is None:
                d.sig = True
        for b in reads:
            if is_dma:
                b.rd.append(op)
            else:
                b.r[eng] = op
        for b in writes:
            b.w = op
            b.r = {}
            b.rd = []
        if is_dma:
            dsem.cnt += dsem.step
            op.dsem = dsem
            op.dval = dsem.cnt
            self.lastdma[id(dsem)] = op
        else:
            self.last[eng] = op
        self.ops[eng].append(op)
        return op

    def barrier(self, full=False):
        deps = list(self.last.values()) + [o for o in self.lastdma.values() if full or not o.dsem.async_]
        for d in deps:
            if d.dsem is None:
                d.sig = True
        self.bar = deps

    def final_wait(self, eng=SP):
        self.barrier(full=True)
        self.add(eng, None)

    def emit(self, nc):
        for e in ENGS:
            c = 0
            for op in self.ops[e]:
                if op.dsem is None and op.sig:
                    c += 1
                    op.sidx = c
        stats = {}
        with ExitStack() as es:
            esem = {e: es.enter_context(nc.semaphore("s_" + e)) for e in ENGS}
            for d in self.dsems:
                d.h = es.enter_context(nc.semaphore("d_" + d.name))
            block = es.enter_context(nc.Block())

            def run(e_name):
                def body(e):
                    waited = {}
                    nw = 0
                    if e_name == SP:
                        PID["sp"] = e.partition_id()
                        PID["i"] = PID["sp"] % 4
                        PID["im1"] = (PID["sp"] + 3) % 4
                    for op in self.ops[e_name]:
                        for d in op.deps:
                            if d.dsem is not None:
                                key, h, val = id(d.dsem), d.dsem.h, d.dval
                            else:
                                key, h, val = d.eng, esem[d.eng], d.sidx
                            if waited.get(key, 0) < val:
                                e.wait_ge(h, val)
                                waited[key] = val
                                nw += 1
                        if op.fn is None:
                            continue
                        ins = op.fn(e)
                        if op.dsem is not None:
                            ins.then_inc(op.dsem.h, op.dsem.step)
                        elif op.sig:
                            ins.then_inc(esem[e_name], 1)
                    stats[e_name] = (len(self.ops[e_name]), nw)
                return body

            block.tensor(run(PE))
            block.scalar(run(ACT))
            block.vector(run(DVE))
            block.gpsimd(run(POOL))
            block.sync(run(SP))
        return stats


WSPECS = [
    ("win", D, 10240), ("wa", 512, D), ("wb", 512, D), ("wo", D, D),
    ("wup", D, 2 * DFF), ("wd", DFF, D), ("wpi", PLE, D), ("wpg", D, D),
]
CE = 2 * 1024 * 1024


def wchunks(n_total):
    out = []
    st = 0
    while st < n_total:
        sz = min(CE, n_total - st)
        assert sz % 8192 == 0
        out.append((st, sz))
        st += sz
    return out


def build(S, nlayers=L_, stop_after=None, dbg=()):
    TC = S // 4
    NT = TC // W
    NSB = S // 2048
    NCH = S // 128
    nc = bass.Bass("TRN2", target_bir_lowering=False)
    sch = Sched()
    ein = lambda n, sh, dt=F32: nc.dram_tensor(n, list(sh), dt, kind="ExternalInput").ap()

    xT = ein("xT", [128, KD, TC])
    xh = ein("xh", [128, KD, 2])
    pT = ein("pT", [nlayers, 128, 2, TC])
    wsh = {n: ein(n + "_sh", [nlayers * r * c // 4 // 2048, 2048]) for n, r, c in WSPECS}
    gains = ein("gains", [128, nlayers, 4, KD])
    convw = ein("convw", [128, nlayers, 4, 2 * NFF])
    hflag = ein("hflag", [128, 1])
    cbf = ein("cbf", [128, 3, 128])
    abias = ein("abias", [128, 3, 256])
    bmask = ein("bmask", [128, 4, 512])
    outT = nc.dram_tensor("outT", [128, KD, TC], F32, kind="ExternalOutput").ap()
    dbg_out = {}
    dt_ = lambda n, sh, dt: nc.dram_tensor(n, list(sh), dt).ap()
    wshbf = {n: dt_(n + "_shbf", [nlayers * r * c // 4 // 2048, 2048], BF16) for n, r, c in WSPECS}
    wflat = {n: dt_(n + "_bf", [nlayers * r * c // 2048, 2048], BF16) for n, r, c in WSPECS}
    wbf = {n: wflat[n].rearrange("a b -> (a b)").rearrange("(l r c) -> l r c", l=nlayers, r=r, c=c)
           for n, r, c in WSPECS}
    NQ = TC // 256
    ubuf_own = dt_("ubuf_own", [NQ, D, 256], BF16)
    uall = dt_("uall", [NQ, 4 * D, 256], BF16)
    qk = dt_("qk", [8, 128, S], BF16)
    vbuf = dt_("vbuf", [S, 512], BF16)
    ACW = min(2048, TC)
    NAC = S // ACW
    att_own = dt_("att_own", [NAC, 256, ACW], BF16)
    att_all = dt_("att_all", [NAC, 4 * 256, ACW], BF16)
    wq_mine = dt_("wq_mine", [D, 12, 128], BF16)
    att_mine = dt_("att_mine", [TC // ACW, 1024, ACW], BF16)
    att_halo = dt_("att_halo", [1024, 2], BF16)
    hbuf = dt_("hbuf", [128, KD, TC], F32)
    hh_own = dt_("hh_own", [128, KD * 2], F32)
    hh_all = dt_("hh_all", [4 * 128, KD * 2], F32)

    wshB = {n: [Buf(), Buf(), Buf()] for n, _, _ in WSPECS}
    wflB = {n: Buf() for n, _, _ in WSPECS}
    ubufB = [Buf() for _ in range(NQ)]
    uallB = [Buf() for _ in range(NQ)]
    attoB = [Buf() for _ in range(NAC)]
    attaB = [Buf() for _ in range(NAC)]

    def ag_u(q):
        sch.add(POOL, (lambda e, q=q: e.collective_compute(
            "AllGather", ALU.bypass, replica_groups=GROUPS, ins=[ubuf_own[q]], outs=[uall[q]])),
            reads=[ubufB[q]], writes=[uallB[q]], dsem=sch.dsem("cc_u%d" % (q % 8), step=1, async_=True), nobar=True)

    def ag_att(q):
        sch.add(POOL, (lambda e, q=q: e.collective_compute(
            "AllGather", ALU.bypass, replica_groups=GROUPS, ins=[att_own[q]], outs=[att_all[q]])),
            reads=[attoB[q]], writes=[attaB[q]], dsem=sch.dsem("cc_a%d" % (q % 8), step=1, async_=True), nobar=True)

    def dbg_tensor(name, shape, dt=F32):
        t = nc.dram_tensor("dbg_" + name, list(shape), dt, kind="ExternalOutput").ap()
        dbg_out[name] = t
        return t

    with ExitStack() as top:
        _uid = [0]

        def sbt(es, n, sh, dt):
            _uid[0] += 1
            return es.enter_context(nc.sbuf_tensor("%s_%d" % (n, _uid[0]), list(sh), dt))
        gains_sb = sbt(top, "gains_sb", [128, nlayers, 4, KD], F32)
        convw_sb = sbt(top, "convw_sb", [128, nlayers, 4, 2 * NFF], F32)
        hflag_sb = sbt(top, "hflag_sb", [128, 1], F32)
        cbf_sb = sbt(top, "cbf_sb", [128, 3, 128], BF16)
        psum = [top.enter_context(nc.psum_tensor("ps%d" % i, [128, 512], F32)) for i in range(8)]
        PB = [Buf("ps%d" % i) for i in range(8)]
        Bconst = Buf("const")
        dconst = [sch.dsem("const%d" % i) for i in range(4)]
        sch.add(SP, lambda e: e.dma_start(out=gains_sb[:], in_=gains), writes=[Bconst], dsem=dconst[0])
        sch.add(SP, lambda e: e.dma_start(out=convw_sb[:], in_=convw), writes=[Bconst], dsem=dconst[1])
        sch.add(SP, lambda e: e.dma_start(out=hflag_sb[:], in_=hflag), writes=[Bconst], dsem=dconst[2])
        sch.add(POOL, lambda e: e.dma_start(out=cbf_sb[:], in_=cbf), writes=[Bconst], dsem=dconst[3])
        NTm = cbf_sb[:, 0, :]
        NONESm = cbf_sb[:, 1, :]
        ONESm = cbf_sb[:, 2, :]

        import os as _os
        _skip_cast = _os.environ.get("K_SKIP_CAST") == "1"
        with ExitStack() as es:
            NCB = 3
            cst = [sbt(es, "cast%d" % i, [128, 4, 2048], BF16) for i in range(NCB)]
            cstB = [Buf() for _ in range(NCB)]
            dci = [sch.dsem("casti%d" % i) for i in range(NCB)]
            dco = [sch.dsem("casto%d" % i) for i in range(NCB)]
            ci = 0
            for n, r, c in ([] if _skip_cast else WSPECS):
                rows = nlayers * r * c // 4 // 2048
                src = wsh[n]
                dst = wshbf[n]
                for r0 in range(0, rows, 512):
                    nr = min(512, rows - r0)
                    pc = 128 if nr % 128 == 0 else nr
                    assert pc <= 128
                    q = nr // pc
                    s_ = ci % NCB
                    ci += 1
                    sap = src[r0:r0 + nr, :].rearrange("(q p) b -> p q b", p=pc)
                    dap = dst[r0:r0 + nr, :].rearrange("(q p) b -> p q b", p=pc)
                    sch.add(POOL, (lambda e, s_=s_, q=q, sap=sap, pc=pc: e.dma_start(out=cst[s_][0:pc, 0:q, :], in_=sap)),
                            writes=[cstB[s_]], dsem=dci[s_])
                    sch.add(SP, (lambda e, s_=s_, q=q, dap=dap, pc=pc: e.dma_start(out=dap, in_=cst[s_][0:pc, 0:q, :])),
                            reads=[cstB[s_]], writes=[wshB[n][s_]], dsem=dco[s_])
        sch.barrier()

        def gather_weight(n, r, c):
            dccw = sch.dsem("cc_w_" + n, step=1, async_=True)
            for (st, sz) in wchunks(nlayers * r * c):
                a0, na = st // 4 // 2048, sz // 4 // 2048
                b0, nb_ = st // 2048, sz // 2048
                sch.add(POOL, (lambda e, n=n, a0=a0, na=na, b0=b0, nb_=nb_: e.collective_compute(
                    "AllGather", ALU.bypass, replica_groups=GROUPS, ins=[wshbf[n][a0:a0 + na, :]],
                    outs=[wflat[n][b0:b0 + nb_, :]])), reads=wshB[n], writes=[wflB[n]], dsem=dccw, nobar=True)

        if not _skip_cast:
            gather_weight(*WSPECS[0])

        ring = {"ps": 0}

        def next_ps(n=7):
            i = ring["ps"] % n
            ring["ps"] += 1
            return i

        def mm(out, lhsT, rhs, start, stop, reads, pb):
            sch.add(PE, (lambda e: e.matmul(out, lhsT, rhs, start=start, stop=stop)), reads=reads, writes=[pb])

        def act(out, in_, func, reads, writes, **kw):
            sch.add(ACT, (lambda e: e.activation(out=out, in_=in_, func=func, **kw)), reads=reads, writes=writes)

        def norm_rstd(es_bufs, src_aps, src_bufs, Wt):
            sq, sqB, rstd, rstdB = es_bufs
            n = len(src_aps)
            for k in range(n):
                s = k % len(sq)
                act(sq[s][:, 0:Wt], src_aps[k], AF.Square, reads=[src_bufs[k]], writes=[sqB[s]])
                mm(psum[7][:, 0:Wt], ONESm, sq[s][:, 0:Wt], k == 0, k == n - 1, [sqB[s], Bconst], PB[7])
            act(rstd[:, 0:Wt], psum[7][:, 0:Wt], AF.Sqrt, reads=[PB[7]], writes=[rstdB], scale=1.0 / D, bias=RMS_EPS)
            sch.add(DVE, (lambda e: e.reciprocal(out=rstd[:, 0:Wt], in_=rstd[:, 0:Wt])), reads=[rstdB], writes=[rstdB])
            return rstd[:, 0:Wt], rstdB

        with ExitStack() as es:
            hT = sbt(es, "p0_h", [128, KD, W], F32)
            uT = sbt(es, "p0_u", [128, KD, W], BF16)
            sq = [sbt(es, "p0_sq%d" % i, [128, W], BF16) for i in range(2)]
            rstd = sbt(es, "p0_rstd", [128, W], F32)
            hB = [Buf() for _ in range(KD)]
            uB = [Buf() for _ in range(KD)]
            sqB = [Buf(), Buf()]
            rstdB = Buf()
            dh, du = sch.dsem("p0h"), sch.dsem("p0u")
            for tt in range(0 if stop_after == "cast" else NT):
                t0 = tt * W
                sch.add(SP, (lambda e, t0=t0: e.dma_start(out=hT[:], in_=xT[:, :, t0:t0 + W])), writes=hB, dsem=dh)
                r_ap, rB = norm_rstd((sq, sqB, rstd, rstdB), [hT[:, k, :] for k in range(KD)], hB, W)
                for k in range(KD):
                    sch.add(DVE, (lambda e, k=k: e.scalar_tensor_tensor(
                        out=uT[:, k, :], in0=hT[:, k, :], scalar=gains_sb[:, 0, 0, k:k + 1], in1=r_ap,
                        op0=ALU.mult, op1=ALU.mult)), reads=[hB[k], rB, Bconst], writes=[uB[k]])
                for q in range(2):
                    sch.add(SP, (lambda e, tt=tt, q=q: e.dma_start(
                        out=ubuf_own[2 * tt + q].rearrange("(k p) t -> p k t", p=128),
                        in_=uT[:, :, q * 256:(q + 1) * 256])),
                        reads=uB, writes=[ubufB[2 * tt + q]], dsem=du)
                for q in range(2):
                    ag_u(2 * tt + q)
            sch.barrier()
        for spec in WSPECS[1:]:
            if not _skip_cast:
                gather_weight(*spec)

        for l in range(nlayers):
            if stop_after in ("cast", "p0"):
                break
            if stop_after == "ag_u":
                break

            with ExitStack() as es:
                wq = sbt(es, "p2a_w", [128, KD, 1536], BF16)
                uT2 = [sbt(es, "p2a_u%d" % i, [128, KD, W], BF16) for i in range(2)]
                stq = [sbt(es, "p2a_sq%d" % i, [128, W], BF16) for i in range(4)]
                stv = [sbt(es, "p2a_sv%d" % i, [128, 512], BF16) for i in range(4)]
                wqB = Buf()
                uB2 = [Buf(), Buf()]
                stqB = [Buf() for _ in range(4)]
                stvB = [Buf() for _ in range(4)]
                dw = sch.dsem("p2aw")
                du2 = [sch.dsem("p2au%d" % i) for i in range(2)]
                dsq = [sch.dsem("p2asq%d" % i) for i in range(4)]
                dsv = [sch.dsem("p2asv%d" % i) for i in range(4)]
                wqmB = Buf()
                dwm = sch.dsem("p2awm")
                sch.add(SP, (lambda e, l=l: e.dma_start(
                    out=wq_mine, in_=wbf["win"][l].rearrange("r (G j c) -> r G j c", j=4, c=128)[
                        :, 0:12, bass.ds(PID["i"], 1), :].rearrange("r G o c -> r G (o c)"))),
                    reads=[wflB["win"]], writes=[wqmB], dsem=dwm)
                for (ga, gb_, c0_) in ((0, 6, 0), (9, 11, 768), (6, 9, 1024), (11, 12, 1408)):
                    sch.add(SP, (lambda e, ga=ga, gb_=gb_, c0_=c0_: e.dma_start(
                        out=wq[:, :, c0_:c0_ + (gb_ - ga) * 128],
                        in_=wq_mine[:, ga:gb_, :].rearrange("(k p) G c -> p k (G c)", p=128))),
                        reads=[wqmB], writes=[wqB], dsem=dw)
                it = 0
                ev = 0
                for r in range(4):
                    for tt in range(NT):
                        s = it % 2
                        it += 1
                        tok0 = r * TC + tt * W
                        for q in range(2):
                            sch.add(SP, (lambda e, r=r, tt=tt, s=s, q=q: e.dma_start(
                                out=uT2[s][:, :, q * 256:(q + 1) * 256],
                                in_=uall[2 * tt + q, r * D:(r + 1) * D, :].rearrange(
                                    "(k p) t -> p k t", p=128))), reads=[uallB[2 * tt + q]], writes=[uB2[s]], dsem=du2[s])
                        for hd in range(8):
                            pi = next_ps()
                            for k in range(KD):
                                mm(psum[pi][:, :], wq[:, k, hd * 128:(hd + 1) * 128], uT2[s][:, k, :],
                                   k == 0, k == KD - 1, [wqB, uB2[s]], PB[pi])
                            q_ = ev % 4
                            ev += 1
                            if ev % 2 == 0:
                                act(stq[q_][:], psum[pi][:, :], AF.Copy, reads=[PB[pi]], writes=[stqB[q_]])
                            else:
                                sch.add(DVE, (lambda e, q_=q_, pi=pi: e.tensor_copy(out=stq[q_][:], in_=psum[pi][:, :])),
                                        reads=[PB[pi]], writes=[stqB[q_]])
                            sch.add(SP, (lambda e, hd=hd, q_=q_, tok0=tok0: e.dma_start(
                                out=qk[hd, :, tok0:tok0 + W], in_=stq[q_][:])), reads=[stqB[q_]], dsem=dsq[q_])
                        for sub in range(4):
                            pi = next_ps()
                            for k in range(KD):
                                mm(psum[pi][:, :], uT2[s][:, k, sub * 128:(sub + 1) * 128], wq[:, k, 1024:1536],
                                   k == 0, k == KD - 1, [wqB, uB2[s]], PB[pi])
                            q_ = ev % 4
                            ev += 1
                            if ev % 2 == 0:
                                act(stv[q_][:], psum[pi][:, :], AF.Copy, reads=[PB[pi]], writes=[stvB[q_]])
                            else:
                                sch.add(DVE, (lambda e, q_=q_, pi=pi: e.tensor_copy(out=stv[q_][:], in_=psum[pi][:, :])),
                                        reads=[PB[pi]], writes=[stvB[q_]])
                            sch.add(SP, (lambda e, sub=sub, q_=q_, tok0=tok0: e.dma_start(
                                out=vbuf[tok0 + sub * 128:tok0 + (sub + 1) * 128, :], in_=stv[q_][:])),
                                reads=[stvB[q_]], dsem=dsv[q_])
                sch.barrier()
            if stop_after == "p2a":
                break

            with ExitStack() as es:
                ab_sb = sbt(es, "pa_bias", [128, 3, 256], F32)
                qTg = [sbt(es, "pa_q%d" % g, [128, 2048], BF16) for g in range(3)]
                kTg = [sbt(es, "pa_k%d" % g, [128, 2, 2048], BF16) for g in range(3)]
                Vg = [sbt(es, "pa_v%d" % g, [128, 2, 16, 128], BF16) for g in range(3)]
                accUS = sbt(es, "pa_accUS", [128, 2, 2048], F32)
                accU = accUS[:, 0, :]
                accS = accUS[:, 1, :]
                sc = [sbt(es, "pa_sc%d" % i, [128, 256], F32) for i in range(4)]
                Pt = [sbt(es, "pa_P%d" % i, [128, 256], BF16) for i in range(4)]
                yst = sbt(es, "pa_y", [128, 2048], BF16)
                abB = Buf()
                qB = [Buf() for _ in range(3)]
                kB = [[Buf(), Buf()] for _ in range(3)]
                vB = [[Buf(), Buf()] for _ in range(3)]
                accUB, accSB, ystB = Buf(), Buf(), Buf()
                scB = [Buf() for _ in range(4)]
                PtB = [Buf() for _ in range(4)]
                dab = sch.dsem("pa_ab")
                dq = [sch.dsem("pa_q%d" % g) for g in range(3)]
                dk = [[sch.dsem("pa_k%d_%d" % (g, s)) for s in range(2)] for g in range(3)]
                dv = [[sch.dsem("pa_v%d_%d" % (g, s)) for s in range(2)] for g in range(3)]
                dy = sch.dsem("pa_y")
                sch.add(SP, lambda e: e.dma_start(out=ab_sb[:], in_=abias), writes=[abB], dsem=dab)
                DIL = (1, 4, 16)
                bi = 0
                for sb in range(NSB):
                    slot = sb % 2
                    c0 = sb * 2048
                    for g in range(3):
                        d = DIL[g]
                        sch.add(SP, (lambda e, g=g, c0=c0: e.dma_start(out=qTg[g][:], in_=qk[g, :, c0:c0 + 2048])),
                                writes=[qB[g]], dsem=dq[g])
                        sch.add(SP, (lambda e, g=g, c0=c0, slot=slot: e.dma_start(
                            out=kTg[g][:, slot, :], in_=qk[3 + g, :, c0:c0 + 2048])),
                            writes=[kB[g][slot]], dsem=dk[g][slot])
                        srcv = vbuf[c0:c0 + 2048, g * 128:(g + 1) * 128].rearrange(
                            "(nn i r) c -> i r nn c", i=128, r=d)
                        nn_ = 16 // d
                        if d == 4:
                            for r in range(d):
                                sch.add(SP, (lambda e, g=g, slot=slot, srcv=srcv, r=r, nn_=nn_: e.dma_start(
                                    out=Vg[g][:, slot, r * nn_:(r + 1) * nn_, :], in_=srcv[:, r, :, :])),
                                    writes=[vB[g][slot]], dsem=dv[g][slot])
                        else:
                            sv = srcv[:, 0, :, :] if d == 1 else srcv[:, :, 0, :]
                            sch.add(SP, (lambda e, g=g, slot=slot, sv=sv: e.dma_start(
                                out=Vg[g][:, slot, :, :], in_=sv)),
                                writes=[vB[g][slot]], dsem=dv[g][slot])
                    for g in range(3):
                        d = DIL[g]
                        nn_ = 16 // d

                        def colsel(t, nn, r, d=d):
                            return t[:, nn * 128 * d:(nn + 1) * 128 * d].rearrange("p (i r) -> p r i", r=d)[:, r, :]

                        a_pend = []

                        def a_stage2(st_):
                            (b_, first, vc_ap, vp_ap, vrd, nn, r, g, d) = st_
                            po = next_ps()
                            mm(psum[po][:, 0:128], vc_ap, Pt[b_][:, 0:128], True, first, vrd + [PtB[b_]], PB[po])
                            if not first:
                                mm(psum[po][:, 0:128], vp_ap, Pt[b_][:, 128:256], False, True, vrd + [PtB[b_]], PB[po])
                            mm(psum[po][:, 128:256], ONESm, Pt[b_][:, 0:128], True, first, [PtB[b_], Bconst], PB[po])
                            if not first:
                                mm(psum[po][:, 128:256], ONESm, Pt[b_][:, 128:256], False, True, [PtB[b_], Bconst], PB[po])
                            aUS = accUS[:, :, nn * 128 * d:(nn + 1) * 128 * d].rearrange(
                                "p t (i r) -> p t r i", r=d)[:, :, r, :]
                            pUS = psum[po][:, 0:256].rearrange("p (t i) -> p t i", t=2)
                            if g == 0:
                                sch.add(DVE, (lambda e, aUS=aUS, pUS=pUS: e.tensor_copy(out=aUS, in_=pUS)),
                                        reads=[PB[po]], writes=[accUB])
                            else:
                                sch.add(DVE, (lambda e, aUS=aUS, pUS=pUS: e.tensor_tensor(
                                    out=aUS, in0=aUS, in1=pUS, op=ALU.add)),
                                    reads=[PB[po], accUB], writes=[accUB])

                        for r in range(d):
                            for nn in range(nn_):
                                first = (sb == 0 and nn == 0)
                                q_ap = colsel(qTg[g][:], nn, r)
                                kc_ap = colsel(kTg[g][:, slot, :], nn, r)
                                vc_ap = Vg[g][:, slot, r * nn_ + nn, :]
                                rd = [qB[g], kB[g][slot]]
                                vrd = [vB[g][slot]]
                                if not first:
                                    if nn >= 1:
                                        kp_ap = colsel(kTg[g][:, slot, :], nn - 1, r)
                                        vp_ap = Vg[g][:, slot, r * nn_ + nn - 1, :]
                                    else:
                                        kp_ap = colsel(kTg[g][:, 1 - slot, :], nn_ - 1, r)
                                        vp_ap = Vg[g][:, 1 - slot, r * nn_ + nn_ - 1, :]
                                        rd = rd + [kB[g][1 - slot]]
                                        vrd = vrd + [vB[g][1 - slot]]
                                ncol = 128 if first else 256
                                pi = next_ps()
                                b_ = bi % 4
                                bi += 1
                                mm(psum[pi][:, 0:128], kc_ap, q_ap, True, True, rd, PB[pi])
                                if not first:
                                    mm(psum[pi][:, 128:256], kp_ap, q_ap, True, True, rd, PB[pi])
                                sch.add(DVE, (lambda e, pi=pi, b_=b_, g=g, ncol=ncol: e.scalar_tensor_tensor(
                                    out=sc[b_][:, 0:ncol], in0=psum[pi][:, 0:ncol], scalar=QSCALE,
                                    in1=ab_sb[:, g, 0:ncol], op0=ALU.mult, op1=ALU.add)),
                                    reads=[PB[pi], abB], writes=[scB[b_]])
                                act(Pt[b_][:, 0:ncol], sc[b_][:, 0:ncol], AF.Exp, reads=[scB[b_]], writes=[PtB[b_]])
                                a_pend.append((b_, first, vc_ap, vp_ap if not first else None, vrd, nn, r, g, d))
                                if len(a_pend) > 2:
                                    a_stage2(a_pend.pop(0))
                        while a_pend:
                            a_stage2(a_pend.pop(0))
                    sch.add(DVE, (lambda e: e.reciprocal(out=accS, in_=accS)), reads=[accUB], writes=[accUB])
                    sch.add(DVE, (lambda e: e.tensor_tensor(out=yst[:], in0=accU, in1=accS, op=ALU.mult)),
                            reads=[accUB], writes=[ystB])
                    for qq in range(2048 // ACW):
                        sch.add(SP, (lambda e, c0=c0, qq=qq: e.dma_start(
                            out=att_own[(c0 + qq * ACW) // ACW, 0:128, :], in_=yst[:, qq * ACW:(qq + 1) * ACW])),
                            reads=[ystB], writes=[attoB[(c0 + qq * ACW) // ACW]], dsem=dy)
                sch.barrier()
            if stop_after == "p2bA":
                break

            with ExitStack() as es:
                kTB = sbt(es, "pb_k", [128, S], BF16)
                qTB = sbt(es, "pb_q", [128, S], BF16)
                VB = sbt(es, "pb_v", [128, NCH, 128], BF16)
                bm_sb = sbt(es, "pb_mask", [128, 4, 512], BF16)
                ez = [sbt(es, "pb_ez%d" % i, [128, 512], F32) for i in range(6)]
                sp = [sbt(es, "pb_sp%d" % i, [128, 512], BF16) for i in range(6)]
                wt = [sbt(es, "pb_w%d" % i, [128, 512], BF16) for i in range(3)]
                Lr = sbt(es, "pb_Lr", [128, 512], F32)
                Lrb = [sbt(es, "pb_Lrb%d" % i, [128, 512], BF16) for i in range(4)]
                ost = [sbt(es, "pb_o%d" % i, [128, 512], BF16) for i in range(2)]
                kvB, bmB = Buf(), Buf()
                ezB = [Buf() for _ in range(6)]
                spB = [Buf() for _ in range(6)]
                wtB = [Buf() for _ in range(3)]
                LrB = Buf()
                LrbB = [Buf() for _ in range(4)]
                ostB = [Buf(), Buf()]
                dkv = [sch.dsem("pb_kv%d" % i) for i in range(4)]
                do = [sch.dsem("pb_o%d" % i) for i in range(2)]
                sch.add(SP, lambda e: e.dma_start(out=kTB[:], in_=qk[7, :, :]), writes=[kvB], dsem=dkv[0])
                sch.add(SP, lambda e: e.dma_start(out=qTB[:], in_=qk[6, :, :]), writes=[kvB], dsem=dkv[1])
                sch.add(SP, lambda e: e.dma_start(
                    out=VB[:], in_=vbuf[:, 384:512].rearrange("(c p) d -> p c d", p=128)), writes=[kvB], dsem=dkv[2])
                sch.add(POOL, lambda e: e.dma_start(out=bm_sb[:], in_=bmask), writes=[bmB], dsem=dkv[3])
                zi = 0
                si = 0
                for gq in range(S // 512):
                    n0 = 4 * gq
                    q_ap = qTB[:, n0 * 128:n0 * 128 + 512]
                    steps = list(range(n0 + 3, -1, -1))
                    po = 6 + (gq % 2)
                    ob = gq % 2
                    stA = {}

                    def stageA1(c):
                        nonlocal zi
                        pz = zi % 6
                        e_ = zi % 6
                        zi += 1
                        mm(psum[pz][:, :], kTB[:, c * 128:(c + 1) * 128], q_ap, True, False, [kvB], PB[pz])
                        act(ez[e_][:], psum[pz][:, :], AF.Exp, reads=[PB[pz]], writes=[ezB[e_]], scale=QSCALE)
                        stA[c] = (pz, e_)

                    def stageA2(c):
                        nonlocal si
                        pz, e_ = stA[c]
                        s_ = si % 6
                        si += 1
                        act(sp[s_][:], ez[e_][:], AF.Ln, reads=[ezB[e_]], writes=[spB[s_]], bias=1.0)
                        if c >= n0:
                            j = c - n0
                            sch.add(DVE, (lambda e, s_=s_, j=j: e.tensor_tensor(
                                out=sp[s_][:], in0=sp[s_][:], in1=bm_sb[:, j, :], op=ALU.mult)),
                                reads=[spB[s_], bmB], writes=[spB[s_]])
                        stA[c] = (pz, s_)

                    stB2 = {}

                    def stageB2(c):
                        w_, firststep, laststep = stB2.pop(c)
                        mm(psum[po][:, :], VB[:, c, :], wt[w_][:], firststep, laststep, [kvB, wtB[w_]], PB[po])

                    def stageB(c, idx):
                        pz, s_ = stA.pop(c)
                        firststep = idx == 0
                        laststep = idx == len(steps) - 1
                        lb = idx % 4
                        mm(psum[pz][:, :], NTm, sp[s_][:], False, firststep, [spB[s_], Bconst, PB[pz]], PB[pz])
                        if not firststep:
                            mm(psum[pz][:, :], NONESm, Lrb[lb][:], False, True, [LrbB[lb], Bconst, PB[pz]], PB[pz])
                        w_ = idx % 3
                        act(wt[w_][:], psum[pz][:, :], AF.Exp, reads=[PB[pz]], writes=[wtB[w_]], scale=QSCALE2)
                        if c >= n0:
                            j = c - n0
                            sch.add(DVE, (lambda e, w_=w_, j=j: e.tensor_tensor(
                                out=wt[w_][:], in0=wt[w_][:], in1=bm_sb[:, j, :], op=ALU.mult)),
                                reads=[wtB[w_], bmB], writes=[wtB[w_]])
                        stB2[c] = (w_, firststep, laststep)
                        if not laststep:
                            if firststep:
                                sch.add(DVE, (lambda e, s_=s_: e.tensor_copy(out=Lr[:], in_=sp[s_][:])),
                                        reads=[spB[s_]], writes=[LrB])
                            else:
                                sch.add(DVE, (lambda e, s_=s_: e.tensor_tensor(
                                    out=Lr[:], in0=Lr[:], in1=sp[s_][:], op=ALU.add)),
                                    reads=[spB[s_], LrB], writes=[LrB])
                            nb_ = (idx + 1) % 4
                            sch.add(DVE, (lambda e, nb_=nb_: e.tensor_copy(out=Lrb[nb_][:], in_=Lr[:])),
                                    reads=[LrB], writes=[LrbB[nb_]])

                    GB_ = 3
                    batches = [steps[i:i + GB_] for i in range(0, len(steps), GB_)]
                    for c in batches[0]:
                        stageA1(c)
                    for c in batches[0]:
                        stageA2(c)
                    idx = 0
                    for bi_, batch in enumerate(batches):
                        if bi_ + 1 < len(batches):
                            for c in batches[bi_ + 1]:
                                stageA1(c)
                        for c in batch:
                            stageB(c, idx)
                            idx += 1
                        for c in batch:
                            stageB2(c)
                        if bi_ + 1 < len(batches):
                            for c in batches[bi_ + 1]:
                                stageA2(c)
                    if gq % 2 == 0:
                        act(ost[ob][:], psum[po][:, :], AF.Copy, reads=[PB[po]], writes=[ostB[ob]])
                    else:
                        sch.add(DVE, (lambda e, ob=ob, po=po: e.tensor_copy(out=ost[ob][:], in_=psum[po][:, :])),
                                reads=[PB[po]], writes=[ostB[ob]])
                    sch.add(SP, (lambda e, ob=ob, n0=n0: e.dma_start(
                        out=att_own[(n0 * 128) // ACW, 128:256, (n0 * 128) % ACW:(n0 * 128) % ACW + 512], in_=ost[ob][:])),
                        reads=[ostB[ob]], writes=[attoB[(n0 * 128) // ACW]], dsem=do[ob])
                    if ((n0 * 128) + 512) % ACW == 0:
                        ag_att((n0 * 128) // ACW)
                sch.barrier()
            if stop_after == "p2bB":
                break

            if stop_after == "ag_a":
                break

            last = (l == nlayers - 1)
            with ExitStack() as es:
                hT = sbt(es, "p3_h", [128, KD, W], F32)
                uT = sbt(es, "p3_u", [128, KD, W], BF16)
                cp = sbt(es, "p3_cp", [128, NFF, W], BF16)
                nb = sbt(es, "p3_nb", [128, KD, W], F32)
                pTt = sbt(es, "p3_p", [128, 2, W], BF16)
                wsl = [sbt(es, "p3_w%d" % i, [128, KD, 256], BF16) for i in range(4)]
                sq = [sbt(es, "p3_sq%d" % i, [128, W], BF16) for i in range(2)]
                rstd = sbt(es, "p3_rstd", [128, W], F32)
                tmp = [sbt(es, "p3_t%d" % i, [128, W + 2], F32) for i in range(6)]
                uph = sbt(es, "p3_uph", [128, 2 * NFF, 2], F32)
                hB = [Buf() for _ in range(KD)]
                uB = [Buf() for _ in range(KD)]
                cB = [Buf() for _ in range(NFF)]
                nB = [Buf() for _ in range(KD)]
                pB = Buf()
                wB = [Buf() for _ in range(4)]
                sqB = [Buf(), Buf()]
                rstdB = Buf()
                tB = [Buf() for _ in range(6)]
                uphB = [Buf() for _ in range(2 * NFF)]
                dW = [sch.dsem("p3w%d" % i) for i in range(4)]
                dH, dU, dA, dP, dO, dUo, dHH = [sch.dsem("p3%s" % n) for n in "h u a p o uo hh".split()]
                wst = {"i": 0, "t": 0}

                def slab(name, r0, nr, c0, ncol=256):
                    s = wst["i"] % 4
                    wst["i"] += 1
                    nk = nr // 128
                    if name == "wg":
                        name, c0 = "win", c0 + 6144
                    src = wbf[name][l, r0:r0 + nr, c0:c0 + ncol].rearrange("(k p) c -> p k c", p=128)
                    if not (_os.environ.get("K_NOSLAB") == "1" and wst["i"] > 4):
                        sch.add(SP, (lambda e, s=s, nk=nk, src=src, ncol=ncol: e.dma_start(
                            out=wsl[s][:, 0:nk, 0:ncol], in_=src)), reads=[wflB[name]], writes=[wB[s]], dsem=dW[s])
                    return s

                def next_t():
                    i = wst["t"] % 6
                    wst["t"] += 1
                    return i

                ne = (sq, sqB, rstd, rstdB)

                def p3_tile(t0, Wt, halo, l=l, last=last):
                    def ld_h(e):
                        if halo:
                            if l == 0:
                                return e.dma_start(out=hT[:, :, 0:2], in_=xh)
                            return e.dma_start(out=hT[:, :, 0:2], in_=hh_all.rearrange("(r p) c -> r p c", r=4)[
                                bass.ds(PID["im1"], 1), :, :].rearrange("o p (k t) -> p k (o t)", t=2))
                        src = xT if l == 0 else hbuf
                        return e.dma_start(out=hT[:], in_=src[:, :, t0:t0 + W])
                    sch.add(SP, ld_h, writes=hB, dsem=dH)

                    def ld_u(e):
                        if halo:
                            return e.dma_start(out=uT[:, :, 0:2], in_=uall[NQ - 1].rearrange("(r f) t -> r f t", r=4)[
                                bass.ds(PID["im1"], 1), :, 254:256].rearrange("o (k p) t -> p k (o t)", p=128))
                        return None
                    if halo:
                        sch.add(SP, ld_u, reads=[uallB[NQ - 1]], writes=uB, dsem=dU)
                    else:
                        for q in range(2):
                            sch.add(SP, (lambda e, q=q: e.dma_start(
                                out=uT[:, :, q * 256:(q + 1) * 256],
                                in_=ubuf_own[2 * (t0 // W) + q].rearrange("(k p) t -> p k t", p=128))),
                                reads=[ubufB[2 * (t0 // W) + q]], writes=uB, dsem=dU)

                    for a_ in range(2):
                        def ld_a(e, a_=a_):
                            if halo:
                                src = att_halo.rearrange("(s a p) t -> p a s t", s=4, a=2)[:, a_, :, :]
                            else:
                                src = att_mine[t0 // ACW, :, t0 % ACW:t0 % ACW + Wt].rearrange(
                                    "(s a p) t -> p a s t", s=4, a=2)[:, a_, :, :]
                            return e.dma_start(out=cp[:, a_ * 4:(a_ + 1) * 4, 0:Wt], in_=src)
                        sch.add(SP, ld_a, reads=[attmB], writes=cB[0:8], dsem=dA)
                    if not halo:
                        sch.add(POOL, (lambda e: e.dma_start(out=pTt[:], in_=pT[l, :, :, t0:t0 + W])),
                                writes=[pB], dsem=dP)

                    for co2 in range(8):
                        sGA = slab("wg", 0, D, co2 * 256)
                        sGB = slab("wg", 0, D, D + co2 * 256)
                        sAB = slab("wa", 0, 512, co2 * 256)
                        sBB = slab("wb", 0, 512, co2 * 256)
                        for cc in range(2):
                            co = co2 * 2 + cc
                            cs = slice(cc * 128, (cc + 1) * 128)
                            pa, pb_, pya, pyb = next_ps(), next_ps(), next_ps(), next_ps()
                            for k in range(KD):
                                mm(psum[pa][:, 0:Wt], wsl[sGA][:, k, cs], uT[:, k, 0:Wt], k == 0, k == KD - 1,
                                   [wB[sGA], uB[k]], PB[pa])
                            for k in range(KD):
                                mm(psum[pb_][:, 0:Wt], wsl[sGB][:, k, cs], uT[:, k, 0:Wt], k == 0, k == KD - 1,
                                   [wB[sGB], uB[k]], PB[pb_])
                            for s4 in range(4):
                                mm(psum[pya][:, 0:Wt], wsl[sAB][:, s4, cs], cp[:, s4, 0:Wt], s4 == 0, s4 == 3,
                                   [wB[sAB], cB[s4]], PB[pya])
                            for s4 in range(4):
                                mm(psum[pyb][:, 0:Wt], wsl[sBB][:, s4, cs], cp[:, 4 + s4, 0:Wt], s4 == 0, s4 == 3,
                                   [wB[sBB], cB[4 + s4]], PB[pyb])
                            ta, tb = next_t(), next_t()
                            act(tmp[ta][:, 0:Wt], psum[pa][:, 0:Wt], AF.Sigmoid, reads=[PB[pa]], writes=[tB[ta]])
                            act(tmp[tb][:, 0:Wt], psum[pb_][:, 0:Wt], AF.Sigmoid, reads=[PB[pb_]], writes=[tB[tb]])
                            sch.add(DVE, (lambda e, ta=ta, pya=pya: e.tensor_tensor(
                                out=tmp[ta][:, 0:Wt], in0=tmp[ta][:, 0:Wt], in1=psum[pya][:, 0:Wt], op=ALU.mult)),
                                reads=[tB[ta], PB[pya]], writes=[tB[ta]])
                            sch.add(DVE, (lambda e, tb=tb, pyb=pyb: e.tensor_tensor(
                                out=tmp[tb][:, 0:Wt], in0=tmp[tb][:, 0:Wt], in1=psum[pyb][:, 0:Wt], op=ALU.mult)),
                                reads=[tB[tb], PB[pyb]], writes=[tB[tb]])
                            sch.add(POOL, (lambda e, ta=ta, tb=tb, co=co: e.tensor_tensor(
                                out=cp[:, 8 + co, 0:Wt], in0=tmp[ta][:, 0:Wt], in1=tmp[tb][:, 0:Wt], op=ALU.add)),
                                reads=[tB[ta], tB[tb]], writes=[cB[8 + co]])
                    for co2 in range(8):
                        sO = slab("wo", 0, D, co2 * 256)
                        for cc in range(2):
                            co = co2 * 2 + cc
                            cs = slice(cc * 128, (cc + 1) * 128)
                            pi = next_ps()
                            for k in range(KD):
                                mm(psum[pi][:, 0:Wt], wsl[sO][:, k, cs], cp[:, 8 + k, 0:Wt], k == 0, k == KD - 1,
                                   [wB[sO], cB[8 + k]], PB[pi])
                            sch.add(DVE, (lambda e, co=co, pi=pi: e.tensor_copy(out=nb[:, co, 0:Wt], in_=psum[pi][:, 0:Wt])),
                                    reads=[PB[pi]], writes=[nB[co]])
                    r_ap, rB = norm_rstd(ne, [nb[:, k, 0:Wt] for k in range(KD)], nB, Wt)
                    for k in range(KD):
                        ti = next_t()
                        sch.add(DVE, (lambda e, k=k, ti=ti: e.scalar_tensor_tensor(
                            out=tmp[ti][:, 0:Wt], in0=nb[:, k, 0:Wt], scalar=gains_sb[:, l, 1, k:k + 1], in1=r_ap,
                            op0=ALU.mult, op1=ALU.mult)), reads=[nB[k], rB, Bconst], writes=[tB[ti]])
                        sch.add(DVE, (lambda e, k=k, ti=ti: e.tensor_tensor(
                            out=hT[:, k, 0:Wt], in0=hT[:, k, 0:Wt], in1=tmp[ti][:, 0:Wt], op=ALU.add)),
                            reads=[hB[k], tB[ti]], writes=[hB[k]])
                    if "hmid" in dbg and not halo and t0 == 0 and l == 0:
                        tdb = dbg_tensor("hmid", [128, KD, W])
                        sch.add(SP, (lambda e: e.dma_start(out=tdb, in_=hT[:])), reads=hB, dsem=sch.dsem("dbg2"))
                    if "mT" in dbg and not halo and t0 == 0 and l == 0:
                        tdb2 = dbg_tensor("mT", [128, KD, W], BF16)
                        sch.add(SP, (lambda e: e.dma_start(out=tdb2, in_=cp[:, 8:24, :])), reads=cB[8:24], dsem=sch.dsem("dbg3"))
                    r_ap, rB = norm_rstd(ne, [hT[:, k, 0:Wt] for k in range(KD)], hB, Wt)
                    for k in range(KD):
                        sch.add(DVE, (lambda e, k=k: e.scalar_tensor_tensor(
                            out=uT[:, k, 0:Wt], in0=hT[:, k, 0:Wt], scalar=gains_sb[:, l, 2, k:k + 1], in1=r_ap,
                            op0=ALU.mult, op1=ALU.mult)), reads=[hB[k], rB, Bconst], writes=[uB[k]])
                    for c2 in range(NFF // 2):
                        sUG = slab("wup", 0, D, c2 * 256)
                        sUV = slab("wup", 0, D, DFF + c2 * 256)
                        for cc in range(2):
                            c = c2 * 2 + cc
                            cs = slice(cc * 128, (cc + 1) * 128)
                            ys = []
                            for (sw, ch) in ((sUG, c), (sUV, NFF + c)):
                                pi = next_ps()
                                for k in range(KD):
                                    mm(psum[pi][:, 0:Wt], wsl[sw][:, k, cs], uT[:, k, 0:Wt], k == 0, k == KD - 1,
                                       [wB[sw], uB[k]], PB[pi])
                                if halo:
                                    sch.add(DVE, (lambda e, ch=ch, pi=pi: e.tensor_scalar(
                                        out=uph[:, ch, :], in0=psum[pi][:, 0:2], scalar1=hflag_sb[:, 0:1], scalar2=None,
                                        op0=ALU.mult)), reads=[PB[pi], Bconst], writes=[uphB[ch]])
                                    continue
                                xi, yi = next_t(), next_t()
                                sch.add(POOL, (lambda e, xi=xi, ch=ch: e.tensor_copy(out=tmp[xi][:, 0:2], in_=uph[:, ch, :])),
                                        reads=[uphB[ch]], writes=[tB[xi]])
                                act(tmp[xi][:, 2:2 + W], psum[pi][:, :], AF.Copy, reads=[PB[pi]], writes=[tB[xi]])
                                sch.add(POOL, (lambda e, xi=xi, ch=ch: e.tensor_copy(out=uph[:, ch, :], in_=tmp[xi][:, W:W + 2])),
                                        reads=[tB[xi]], writes=[uphB[ch]])
                                sch.add(DVE, (lambda e, xi=xi, yi=yi, ch=ch: e.tensor_scalar(
                                    out=tmp[yi][:, 0:W], in0=tmp[xi][:, 2:2 + W], scalar1=convw_sb[:, l, 2, ch:ch + 1],
                                    scalar2=convw_sb[:, l, 3, ch:ch + 1], op0=ALU.mult, op1=ALU.add)),
                                    reads=[tB[xi], Bconst], writes=[tB[yi]])
                                sch.add(DVE, (lambda e, xi=xi, yi=yi, ch=ch: e.scalar_tensor_tensor(
                                    out=tmp[yi][:, 0:W], in0=tmp[xi][:, 1:1 + W], scalar=convw_sb[:, l, 1, ch:ch + 1],
                                    in1=tmp[yi][:, 0:W], op0=ALU.mult, op1=ALU.add)),
                                    reads=[tB[xi], tB[yi], Bconst], writes=[tB[yi]])
                                sch.add(DVE, (lambda e, xi=xi, yi=yi, ch=ch: e.scalar_tensor_tensor(
                                    out=tmp[yi][:, 0:W], in0=tmp[xi][:, 0:W], scalar=convw_sb[:, l, 0, ch:ch + 1],
                                    in1=tmp[yi][:, 0:W], op0=ALU.mult, op1=ALU.add)),
                                    reads=[tB[xi], tB[yi], Bconst], writes=[tB[yi]])
                                ys.append(yi)
                            if halo:
                                continue
                            yg, yv = ys
                            act(tmp[yg][:, 0:W], tmp[yg][:, 0:W], AF.Gelu_apprx_tanh, reads=[tB[yg]], writes=[tB[yg]])
                            sch.add(POOL, (lambda e, yg=yg, yv=yv, c=c: e.tensor_tensor(
                                out=cp[:, c, :], in0=tmp[yg][:, 0:W], in1=tmp[yv][:, 0:W], op=ALU.mult)),
                                reads=[tB[yg], tB[yv]], writes=[cB[c]])
                    if halo:
                        return
                    for co2 in range(8):
                        pis = [next_ps(), next_ps()]
                        kparts = [(0, 16), (16, 16), (32, 12)]
                        for (k0, nk) in kparts:
                            sD = slab("wd", k0 * 128, nk * 128, co2 * 256)
                            for cc in range(2):
                                cs = slice(cc * 128, (cc + 1) * 128)
                                for kk in range(nk):
                                    k = k0 + kk
                                    mm(psum[pis[cc]][:, :], wsl[sD][:, kk, cs], cp[:, k, :], k == 0, k == NFF - 1,
                                       [wB[sD], cB[k]], PB[pis[cc]])
                        for cc in range(2):
                            co = co2 * 2 + cc
                            sch.add(DVE, (lambda e, co=co, pi=pis[cc]: e.tensor_copy(out=nb[:, co, :], in_=psum[pi][:, :])),
                                    reads=[PB[pis[cc]]], writes=[nB[co]])
                    r_ap, rB = norm_rstd(ne, [nb[:, k, :] for k in range(KD)], nB, W)
                    for k in range(KD):
                        ti = next_t()
                        sch.add(DVE, (lambda e, k=k, ti=ti: e.scalar_tensor_tensor(
                            out=tmp[ti][:, 0:W], in0=nb[:, k, :], scalar=gains_sb[:, l, 3, k:k + 1], in1=r_ap,
                            op0=ALU.mult, op1=ALU.mult)), reads=[nB[k], rB, Bconst], writes=[tB[ti]])
                        sch.add(DVE, (lambda e, k=k, ti=ti: e.tensor_tensor(
                            out=hT[:, k, :], in0=hT[:, k, :], in1=tmp[ti][:, 0:W], op=ALU.add)),
                            reads=[hB[k], tB[ti]], writes=[hB[k]])
                        sch.add(DVE, (lambda e, k=k: e.tensor_copy(out=uT[:, k, :], in_=hT[:, k, :])),
                                reads=[hB[k]], writes=[uB[k]])
                    if "hffn" in dbg and t0 == 0 and l == 0:
                        tdb3 = dbg_tensor("hffn", [128, KD, W])
                        sch.add(SP, (lambda e: e.dma_start(out=tdb3, in_=hT[:])), reads=hB, dsem=sch.dsem("dbg4"))
                    for co2 in range(8):
                        sG = slab("wpg", 0, D, co2 * 256)
                        sI = slab("wpi", 0, PLE, co2 * 256)
                        for cc in range(2):
                            co = co2 * 2 + cc
                            cs = slice(cc * 128, (cc + 1) * 128)
                            pg, pe = next_ps(), next_ps()
                            for k in range(KD):
                                mm(psum[pg][:, :], wsl[sG][:, k, cs], uT[:, k, :], k == 0, k == KD - 1,
                                   [wB[sG], uB[k]], PB[pg])
                            for k in range(2):
                                mm(psum[pe][:, :], wsl[sI][:, k, cs], pTt[:, k, :], k == 0, k == 1,
                                   [wB[sI], pB], PB[pe])
                            ti = next_t()
                            act(tmp[ti][:, 0:W], psum[pg][:, :], AF.Sigmoid, reads=[PB[pg]], writes=[tB[ti]])
                            sch.add(DVE, (lambda e, ti=ti, pe=pe: e.tensor_tensor(
                                out=tmp[ti][:, 0:W], in0=tmp[ti][:, 0:W], in1=psum[pe][:, :], op=ALU.mult)),
                                reads=[tB[ti], PB[pe]], writes=[tB[ti]])
                            sch.add(DVE, (lambda e, co=co, ti=ti: e.tensor_tensor(
                                out=hT[:, co, :], in0=hT[:, co, :], in1=tmp[ti][:, 0:W], op=ALU.add)),
                                reads=[hB[co], tB[ti]], writes=[hB[co]])
                    dst = outT if last else hbuf
                    sch.add(SP, (lambda e: e.dma_start(out=dst[:, :, t0:t0 + W], in_=hT[:])), reads=hB, dsem=dO)
                    if not last:
                        if t0 + W == TC:
                            sch.add(SP, (lambda e: e.dma_start(
                                out=hh_own.rearrange("p (k t) -> p k t", t=2), in_=hT[:, :, W - 2:W])),
                                reads=hB, dsem=dHH)
                        r_ap2, rB2 = norm_rstd(ne, [hT[:, k, :] for k in range(KD)], hB, W)
                        for k in range(KD):
                            sch.add(DVE, (lambda e, k=k: e.scalar_tensor_tensor(
                                out=uT[:, k, :], in0=hT[:, k, :], scalar=gains_sb[:, l + 1, 0, k:k + 1], in1=r_ap2,
                                op0=ALU.mult, op1=ALU.mult)), reads=[hB[k], rB2, Bconst], writes=[uB[k]])
                        for q in range(2):
                            sch.add(SP, (lambda e, q=q: e.dma_start(
                                out=ubuf_own[2 * (t0 // W) + q].rearrange("(k p) t -> p k t", p=128),
                                in_=uT[:, :, q * 256:(q + 1) * 256])),
                                reads=uB, writes=[ubufB[2 * (t0 // W) + q]], dsem=dUo)
                        for q in range(2):
                            ag_u(2 * (t0 // W) + q)

                attmB = Buf()
                dam = sch.dsem("p3am")
                sch.add(SP, (lambda e: e.dma_start(
                    out=att_mine.rearrange("c r t -> (c r) t"),
                    in_=att_all.rearrange("(i c) r t -> i (c r) t", i=4)[bass.ds(PID["i"], 1), :, :].rearrange(
                        "o x t -> (o x) t"))), reads=attaB, writes=[attmB], dsem=dam)
                sch.add(SP, (lambda e: e.dma_start(
                    out=att_halo,
                    in_=att_all.rearrange("(i c) r t -> i c r t", i=4)[
                        bass.ds(PID["im1"], 1), TC // ACW - 1, :, ACW - 2:ACW].rearrange("o r t -> (o r) t"))),
                    reads=attaB, writes=[attmB], dsem=dam)
                p3_tile(0, 2, True)
                for tt in range(NT):
                    p3_tile(tt * W, W, False)
                sch.barrier()
            if not last:
                dcc3 = sch.dsem("cc_h", step=1)
                sch.add(POOL, (lambda e: e.collective_compute(
                    "AllGather", ALU.bypass, replica_groups=GROUPS, ins=[hh_own], outs=[hh_all])), dsem=dcc3)
                sch.barrier()

        ddbg = sch.dsem("dbg")
        srcs = {"uall": (uall, [NQ, 4 * D, 256], BF16), "qk": (qk, [8, 128, S], BF16), "vbuf": (vbuf, [S, 512], BF16),
                "att_own": (att_own, [NAC, 256, ACW], BF16), "att_all": (att_all, [NAC, 1024, ACW], BF16),
                "hbuf": (hbuf, [128, KD, TC], F32), "ubuf_own": (ubuf_own, [NQ, D, 256], BF16)}
        for name in dbg:
            if name not in srcs:
                continue
            src, shape, dt = srcs[name]
            t = dbg_tensor(name, shape, dt)
            sch.add(SP, (lambda e, t=t, src=src: e.dma_start(out=t, in_=src)), dsem=ddbg)
        sch.final_wait(SP)
        stats = sch.emit(nc)
    return nc, stats, list(dbg_out.keys())


def make_inputs(S, nl, x, p, g_mix_pre, w_in, w_branch_a, w_branch_b, w_out, g_mix_post, g_ffn_pre, w_up,
                conv_w, conv_b, w_down, g_ffn_post, w_ple_in, w_ple_gate):
    TC = S // 4
    f32 = np.float32
    WA, WB_ = 1536, 512
    gains = np.stack([g_mix_pre[:nl], g_mix_post[:nl], g_ffn_pre[:nl], g_ffn_post[:nl]], axis=1)
    gains = np.ascontiguousarray(gains.reshape(nl, 4, KD, 128).transpose(3, 0, 1, 2)).astype(f32)
    cw = np.concatenate([conv_w[:nl], conv_b[:nl, None, :]], axis=1)
    cw = np.ascontiguousarray(cw.reshape(nl, 4, 2 * NFF, 128).transpose(3, 0, 1, 2)).astype(f32)
    wfull = {"win": w_in[:nl], "wa": w_branch_a[:nl], "wb": w_branch_b[:nl], "wo": w_out[:nl], "wup": w_up[:nl],
             "wd": w_down[:nl], "wpi": w_ple_in[:nl], "wpg": w_ple_gate[:nl]}
    wshards = [dict() for _ in range(4)]
    for n, r, c in WSPECS:
        flat = np.ascontiguousarray(wfull[n]).reshape(-1).astype(f32, copy=False)
        pieces = [[] for _ in range(4)]
        for (st, sz) in wchunks(flat.size):
            q = sz // 4
            for i in range(4):
                pieces[i].append(flat[st + i * q: st + (i + 1) * q])
        for i in range(4):
            wshards[i][n + "_sh"] = np.concatenate(pieces[i]).reshape(-1, 2048)
    common = {"gains": gains, "convw": cw}
    idx = np.arange(128)
    kk, kq = idx[:, None], idx[None, :]
    cbf = np.zeros((128, 3, 128), f32)
    cbf[:, 0, :] = np.where(kk >= kq, -NEGC, 0.0)
    cbf[:, 1, :] = -NEGC
    cbf[:, 2, :] = 1.0
    common["cbf"] = cbf
    bm = np.zeros((128, 4, 512), f32)
    for j in range(4):
        for b in range(4):
            if b > j:
                bm[:, j, b * 128:(b + 1) * 128] = 1.0
            elif b == j:
                bm[:, j, b * 128:(b + 1) * 128] = (kk < kq).astype(f32)
    common["bmask"] = bm
    DIL = (1, 4, 16)
    in_maps = []
    for c in range(8):
        b, i = divmod(c, 4)
        j = i
        m = dict(common)
        xs = x[b, i * TC:(i + 1) * TC, :]
        m["xT"] = np.ascontiguousarray(xs.reshape(TC, KD, 128).transpose(2, 1, 0)).astype(f32)
        if i == 0:
            m["xh"] = np.zeros((128, KD, 2), f32)
        else:
            m["xh"] = np.ascontiguousarray(x[b, i * TC - 2:i * TC, :].reshape(2, KD, 128).transpose(2, 1, 0)).astype(f32)
        ps_ = p[:nl, b, i * TC:(i + 1) * TC, :]
        m["pT"] = np.ascontiguousarray(ps_.reshape(nl, TC, 2, 128).transpose(0, 3, 2, 1)).astype(f32)
        m.update(wshards[i])
        m["hflag"] = np.full((128, 1), 0.0 if i == 0 else 1.0, f32)
        ab = np.zeros((128, 3, 256), f32)
        for g in range(3):
            h = g * 4 + j
            slope = 2.0 ** (-8.0 * (h + 1) / 12.0)
            d = DIL[g]
            st_cur = (kq - kk).astype(np.float64)
            cur = np.where(st_cur >= 0, -slope * st_cur * d, MASKV)
            st_prev = (128 + kq - kk).astype(np.float64)
            prev = np.where(st_prev <= 128, -slope * st_prev * d, MASKV)
            ab[:, g, 0:128] = cur
            ab[:, g, 128:256] = prev
        m["abias"] = ab
        in_maps.append(m)
    return in_maps


_CACHE = {}


def kernel(**inputs):
    inputs = {k: np.asarray(v) for k, v in inputs.items()}
    x = inputs["x"]
    B, S, _ = x.shape
    nl = inputs["w_in"].shape[0]
    key = (S, nl)
    if key not in _CACHE:
        import os as _os2
        _CACHE[key] = build(S, nl, stop_after=_os2.environ.get("K_STOP"))[0]
    nc = _CACHE[key]
    in_maps = make_inputs(S, nl, **inputs)
    res = run_bass_kernel_spmd(nc, in_maps, core_ids=list(range(8)))
    TC = S // 4
    out = np.empty((B, S, D), np.float32)
    for c in range(8):
        b, i = divmod(c, 4)
        o = np.asarray(res.results[c]["outT"])
        out[b, i * TC:(i + 1) * TC, :] = o.transpose(2, 1, 0).reshape(TC, D)
    return out
```
